# Optimizing a Trainium2 kernel written in Bass

```python
import math
import jax, jax.numpy as jnp
from jax import lax
import numpy as np

D_MODEL = 1024
BATCH = 8
SEQ = 2048
DEPTH = 4
DEC_BATCH = 128
DEC_SEQ = 4
PAST_LEN = 16384
PAGE_SIZE = 128

POOL_WINDOWS = (2, 4, 8, 16)
N_POOL_GROUPS = 4
POOL_GROUP_DIM = D_MODEL // 8
POOL_DIM = N_POOL_GROUPS * POOL_GROUP_DIM
POOL_BUF = 15
SSM_INNER = D_MODEL
SSM_HEAD_DIM = 64
SSM_HEADS = SSM_INNER // SSM_HEAD_DIM
SSM_GROUPS = 4
SSM_HEADS_PER_GROUP = SSM_HEADS // SSM_GROUPS
SSM_STATE = 128
CONV_WIDTH = 4
CONV_DIM = SSM_INNER + 2 * SSM_GROUPS * SSM_STATE
SSD_CHUNK = 128
N_MEM = 256
MEM_HEADS = 4
MEM_HEAD_DIM = D_MODEL // 8
MEM_DIM = MEM_HEADS * MEM_HEAD_DIM
N_BRANCH = 3
D_FF = 2816
EPS = 1e-6
IN_SPLITS = (POOL_DIM, SSM_INNER, CONV_DIM, SSM_HEADS, MEM_DIM, N_BRANCH * D_MODEL)
IN_COLS = 7184
F32 = jnp.float32

kernel_name = 'hybrid_pool_ssd_memxattn_macaron_step'


def rmsnorm(x, g):
    xf = x.astype(F32)
    y = xf * lax.rsqrt(jnp.mean(xf * xf, axis=-1, keepdims=True) + EPS)
    return (y * g.astype(F32)).astype(x.dtype)


def swiglu(h, w_gate, w_up, w_down):
    return (jax.nn.silu(h @ w_gate) * (h @ w_up)) @ w_down


def pool_mix(u, prev, start_pos, pool_w, pool_scale):
    b, L, _ = u.shape
    full = jnp.concatenate([prev.astype(u.dtype), u], axis=1)
    cs = jnp.cumsum(full.astype(F32), axis=1)
    cs = jnp.concatenate([jnp.zeros((b, 1, POOL_DIM), F32), cs], axis=1)
    upto = cs[:, POOL_BUF + 1:]
    pos = start_pos + jnp.arange(L)
    means = []
    for gi, w in enumerate(POOL_WINDOWS):
        c = slice(gi * POOL_GROUP_DIM, (gi + 1) * POOL_GROUP_DIM)
        before = cs[:, POOL_BUF + 1 - w: POOL_BUF + 1 - w + L, c]
        cnt = jnp.minimum(pos + 1, w).astype(F32)[None, :, None]
        means.append((upto[..., c] - before) / cnt)
    d = (jnp.concatenate(means, axis=-1) - u.astype(F32)).astype(u.dtype)
    d = d.reshape(b, L, N_POOL_GROUPS, POOL_GROUP_DIM)
    y = jnp.einsum('blgc,gcd->blgd', d, pool_w).reshape(b, L, POOL_DIM)
    return y * pool_scale, full[:, -POOL_BUF:]


def causal_conv(xbc, prev, conv_w, conv_b):
    full = jnp.concatenate([prev.astype(xbc.dtype), xbc], axis=1)
    y = lax.conv_general_dilated(full, conv_w[:, None, :].astype(xbc.dtype), window_strides=(1,),
                                 padding='VALID', dimension_numbers=('NWC', 'WIO', 'NWC'),
                                 feature_group_count=CONV_DIM)
    return jax.nn.silu(y + conv_b), full[:, -(CONV_WIDTH - 1):]


def ssd_scan(x, dt, a, bm, cm, h0):
    b, L = x.shape[0], x.shape[1]
    q = min(SSD_CHUNK, L)
    nc = -(-L // q)
    pad = nc * q - L

    def tpad(t):
        return jnp.pad(t, [(0, 0), (0, pad)] + [(0, 0)] * (t.ndim - 2))

    x, dt, bm, cm = tpad(x.astype(F32)), tpad(dt), tpad(bm.astype(F32)), tpad(cm.astype(F32))
    G, HG, P, N = SSM_GROUPS, SSM_HEADS_PER_GROUP, SSM_HEAD_DIM, SSM_STATE
    xdt = (x * dt[..., None]).reshape(b, nc, q, G, HG, P)
    la = (dt * a).reshape(b, nc, q, G, HG).transpose(0, 1, 3, 4, 2)
    acum = jnp.cumsum(la, axis=-1)
    bm = bm.reshape(b, nc, q, G, N)
    cm = cm.reshape(b, nc, q, G, N)
    causal = jnp.tril(jnp.ones((q, q), bool))
    seg = acum[..., :, None] - acum[..., None, :]
    decay = jnp.exp(jnp.where(causal, seg, -jnp.inf))
    cb = jnp.einsum('bctgn,bcsgn->bcgts', cm, bm)
    y_diag = jnp.einsum('bcghts,bcsghp->bctghp', cb[:, :, :, None] * decay, xdt)
    to_end = jnp.exp(acum[..., -1:] - acum).transpose(0, 1, 4, 2, 3)
    chunk_states = jnp.einsum('bcsgn,bcsghp->bcghpn', bm, xdt * to_end[..., None])
    chunk_decay = jnp.exp(acum[..., -1])

    def step(h, inp):
        st, dc = inp
        return h * dc[..., None, None] + st, h

    h_last, h_in = lax.scan(step, h0.reshape(b, G, HG, P, N),
                            (jnp.moveaxis(chunk_states, 1, 0), jnp.moveaxis(chunk_decay, 1, 0)))
    h_in = jnp.moveaxis(h_in, 0, 1)
    from_start = jnp.exp(acum).transpose(0, 1, 4, 2, 3)
    y_off = jnp.einsum('bctgn,bcghpn->bctghp', cm, h_in) * from_start[..., None]
    y = (y_diag + y_off).reshape(b, nc * q, SSM_HEADS, P)[:, :L]
    return y, h_last.reshape(b, SSM_HEADS, P, N)


def mem_attend(q, k, v):
    s = jnp.einsum('blhd,bmhd->bhlm', q, k.astype(q.dtype)).astype(F32) * (MEM_HEAD_DIM ** -0.5)
    w = jax.nn.softmax(s, axis=-1).astype(v.dtype)
    return jnp.einsum('bhlm,bmhd->blhd', w, v)


def mixer(h, start_pos, pool_prev, conv_prev, ssm_h0, mem_k, mem_v, p):
    b, L, _ = h.shape
    cuts = [int(c) for c in np.cumsum(IN_SPLITS)[:-1]]
    u_pool, z, xbc, dt_raw, q_mem, g_raw = jnp.split(h @ p['w_in'], cuts, axis=-1)
    y_pool, new_pool = pool_mix(u_pool, pool_prev, start_pos, p['pool_w'], p['pool_scale'])
    xbc, new_conv = causal_conv(xbc, conv_prev, p['conv_w'], p['conv_b'])
    xs, bm, cm = jnp.split(xbc, [SSM_INNER, SSM_INNER + SSM_GROUPS * SSM_STATE], axis=-1)
    xs = xs.reshape(b, L, SSM_HEADS, SSM_HEAD_DIM)
    dt = jax.nn.softplus(dt_raw.astype(F32) + p['dt_bias'].astype(F32))
    a = -jnp.exp(p['a_log'].astype(F32))
    y, new_ssm = ssd_scan(xs, dt, a, bm.reshape(b, L, SSM_GROUPS, SSM_STATE),
                          cm.reshape(b, L, SSM_GROUPS, SSM_STATE), ssm_h0.astype(F32))
    y = (y + p['d_skip'].astype(F32)[:, None] * xs.astype(F32)).reshape(b, L, SSM_INNER)
    yg = (y * jax.nn.silu(z.astype(F32))).reshape(b, L, SSM_GROUPS, SSM_INNER // SSM_GROUPS)
    yg = yg * lax.rsqrt(jnp.mean(yg * yg, axis=-1, keepdims=True) + EPS)
    y_ssm = (yg.reshape(b, L, SSM_INNER) * p['ssm_norm'].astype(F32)).astype(h.dtype)
    y_mem = mem_attend(q_mem.reshape(b, L, MEM_HEADS, MEM_HEAD_DIM), mem_k, mem_v).reshape(b, L, MEM_DIM)
    gates = jax.nn.sigmoid(g_raw.astype(F32) + p['gate_bias'].astype(F32)).reshape(b, L, N_BRANCH, D_MODEL)
    merged = (gates[:, :, 0] * (y_pool @ p['w_br_pool']).astype(F32)
              + gates[:, :, 1] * (y_ssm @ p['w_br_ssm']).astype(F32)
              + gates[:, :, 2] * (y_mem.astype(h.dtype) @ p['w_br_mem']).astype(F32))
    return merged.astype(h.dtype) @ p['w_o'], new_pool, new_conv, new_ssm.astype(ssm_h0.dtype)


def trunk_layer(x, start_pos, pool_prev, conv_prev, ssm_h0, mem_k, mem_v, p):
    x = x + 0.5 * swiglu(rmsnorm(x, p['ffn1_norm']), p['ffn1_w_gate'], p['ffn1_w_up'], p['ffn1_w_down'])
    mix, new_pool, new_conv, new_ssm = mixer(rmsnorm(x, p['mix_norm']), start_pos, pool_prev, conv_prev,
                                             ssm_h0, mem_k, mem_v, p)
    x = x + mix
    x = x + 0.5 * swiglu(rmsnorm(x, p['ffn2_norm']), p['ffn2_w_gate'], p['ffn2_w_up'], p['ffn2_w_down'])
    return x, new_pool, new_conv, new_ssm


def setup_inputs(seed: int = 0) -> dict:
    key = jax.random.key(seed)
    ks = jax.random.split(key, 40)

    def nrm(i, shape, scale):
        return scale * jax.random.normal(ks[i], shape, F32)

    def gain(i, shape):
        return 1.0 + nrm(i, shape, 0.02)

    Lr, D = DEPTH, D_MODEL
    dt0 = jnp.exp(jax.random.uniform(ks[20], (Lr, SSM_HEADS), F32, math.log(1e-3), math.log(1e-1)))
    dt_bias = dt0 + jnp.log(-jnp.expm1(-dt0))
    a_log = jnp.log(jax.random.uniform(ks[21], (Lr, SSM_HEADS), F32, 1.0, 16.0))
    return {
        'x_prompt': nrm(0, (BATCH, SEQ, D), 1.0),
        'x_sample': nrm(1, (DEC_BATCH, DEC_SEQ, D), 1.0),
        'mem_prompt': nrm(2, (BATCH, N_MEM, D), 1.0),
        'state_pool': nrm(3, (DEPTH, DEC_BATCH, POOL_BUF, POOL_DIM), 1.0),
        'state_conv': nrm(4, (DEPTH, DEC_BATCH, CONV_WIDTH - 1, CONV_DIM), 1.0),
        'state_ssm': nrm(5, (DEPTH, DEC_BATCH, SSM_HEADS, SSM_HEAD_DIM, SSM_STATE), 0.3),
        'cache_mem_k': nrm(6, (DEPTH, DEC_BATCH, N_MEM, MEM_HEADS, MEM_HEAD_DIM), 1.0),
        'cache_mem_v': nrm(7, (DEPTH, DEC_BATCH, N_MEM, MEM_HEADS, MEM_HEAD_DIM), 1.0),
        'ffn1_norm': gain(8, (Lr, D)),
        'ffn1_w_gate': nrm(9, (Lr, D, D_FF), D ** -0.5),
        'ffn1_w_up': nrm(10, (Lr, D, D_FF), D ** -0.5),
        'ffn1_w_down': nrm(11, (Lr, D_FF, D), D_FF ** -0.5),
        'mix_norm': gain(12, (Lr, D)),
        'w_in': nrm(13, (Lr, D, IN_COLS), D ** -0.5),
        'gate_bias': nrm(14, (Lr, N_BRANCH * D), 0.02),
        'pool_w': nrm(15, (Lr, N_POOL_GROUPS, POOL_GROUP_DIM, POOL_GROUP_DIM), POOL_GROUP_DIM ** -0.5),
        'pool_scale': 1.0 + nrm(16, (Lr, POOL_DIM), 0.1),
        'conv_w': nrm(17, (Lr, CONV_WIDTH, CONV_DIM), CONV_WIDTH ** -0.5),
        'conv_b': nrm(18, (Lr, CONV_DIM), 0.02),
        'dt_bias': dt_bias,
        'a_log': a_log,
        'd_skip': 1.0 + nrm(22, (Lr, SSM_HEADS), 0.1),
        'ssm_norm': gain(23, (Lr, SSM_INNER)),
        'mem_norm': gain(24, (Lr, D)),
        'w_mem_k': nrm(25, (Lr, D, MEM_DIM), D ** -0.5),
        'w_mem_v': nrm(26, (Lr, D, MEM_DIM), D ** -0.5),
        'w_br_pool': nrm(27, (Lr, POOL_DIM, D), POOL_DIM ** -0.5),
        'w_br_ssm': nrm(28, (Lr, SSM_INNER, D), SSM_INNER ** -0.5),
        'w_br_mem': nrm(29, (Lr, MEM_DIM, D), MEM_DIM ** -0.5),
        'w_o': nrm(30, (Lr, D, D), D ** -0.5),
        'ffn2_norm': gain(31, (Lr, D)),
        'ffn2_w_gate': nrm(32, (Lr, D, D_FF), D ** -0.5),
        'ffn2_w_up': nrm(33, (Lr, D, D_FF), D ** -0.5),
        'ffn2_w_down': nrm(34, (Lr, D_FF, D), D_FF ** -0.5),
        'final_norm': gain(35, (D,)),
    }


def reference(x_prompt, x_sample, mem_prompt, state_pool, state_conv, state_ssm, cache_mem_k, cache_mem_v,
              ffn1_norm, ffn1_w_gate, ffn1_w_up, ffn1_w_down, mix_norm, w_in, gate_bias,
              pool_w, pool_scale, conv_w, conv_b, dt_bias, a_log, d_skip, ssm_norm,
              mem_norm, w_mem_k, w_mem_v, w_br_pool, w_br_ssm, w_br_mem, w_o,
              ffn2_norm, ffn2_w_gate, ffn2_w_up, ffn2_w_down, final_norm):
    bp = x_prompt.shape[0]
    n_mem = mem_prompt.shape[1]
    xp, xs = x_prompt, x_sample
    pool_p, conv_p, ssm_p, mk_p, mv_p = [], [], [], [], []
    pool_s, conv_s, ssm_s = [], [], []
    for i in range(DEPTH):
        p = dict(ffn1_norm=ffn1_norm[i], ffn1_w_gate=ffn1_w_gate[i], ffn1_w_up=ffn1_w_up[i],
                 ffn1_w_down=ffn1_w_down[i], mix_norm=mix_norm[i], w_in=w_in[i], gate_bias=gate_bias[i],
                 pool_w=pool_w[i], pool_scale=pool_scale[i], conv_w=conv_w[i], conv_b=conv_b[i],
                 dt_bias=dt_bias[i], a_log=a_log[i], d_skip=d_skip[i], ssm_norm=ssm_norm[i],
                 w_br_pool=w_br_pool[i], w_br_ssm=w_br_ssm[i], w_br_mem=w_br_mem[i], w_o=w_o[i],
                 ffn2_norm=ffn2_norm[i], ffn2_w_gate=ffn2_w_gate[i], ffn2_w_up=ffn2_w_up[i],
                 ffn2_w_down=ffn2_w_down[i])
        mem_h = rmsnorm(mem_prompt, mem_norm[i])
        mk = (mem_h @ w_mem_k[i]).reshape(bp, n_mem, MEM_HEADS, MEM_HEAD_DIM)
        mv = (mem_h @ w_mem_v[i]).reshape(bp, n_mem, MEM_HEADS, MEM_HEAD_DIM)
        xp, npool, nconv, nssm = trunk_layer(
            xp, 0,
            jnp.zeros((bp, POOL_BUF, POOL_DIM), xp.dtype),
            jnp.zeros((bp, CONV_WIDTH - 1, CONV_DIM), xp.dtype),
            jnp.zeros((bp, SSM_HEADS, SSM_HEAD_DIM, SSM_STATE), F32),
            mk, mv, p)
        pool_p.append(npool)
        conv_p.append(nconv)
        ssm_p.append(nssm)
        mk_p.append(mk)
        mv_p.append(mv)
        xs, npool, nconv, nssm = trunk_layer(
            xs, PAST_LEN, state_pool[i], state_conv[i], state_ssm[i], cache_mem_k[i], cache_mem_v[i], p)
        pool_s.append(npool)
        conv_s.append(nconv)
        ssm_s.append(nssm)
    y_prompt = rmsnorm(xp, final_norm)
    y_sample = rmsnorm(xs, final_norm)
    return (y_prompt, y_sample,
            jnp.stack(pool_p), jnp.stack(conv_p), jnp.stack(ssm_p), jnp.stack(mk_p), jnp.stack(mv_p),
            jnp.stack(pool_s), jnp.stack(conv_s), jnp.stack(ssm_s))
```

```python
import math
from contextlib import ExitStack
import numpy as np
import concourse.bass as bass
import concourse.mybir as mybir
import concourse.bass_utils as bass_utils
from concourse.alu_op_type import AluOpType as ALU

F32 = mybir.dt.float32
BF16 = mybir.dt.bfloat16
AF = mybir.ActivationFunctionType
AX = mybir.AxisListType

D = 1024
KC = 8
DFF = 2816
NFC = 22
NS = 16
LS = 4
EPS = 1e-6
INC = 7184
C_POOL, C_Z, C_XBC, C_DT, C_Q, C_G = 0, 512, 1536, 3584, 3600, 4112
NVL = 160
SCALE = 128 ** -0.5
import os as _os
USE_PRECAST = bool(int(_os.environ.get('K_PRECAST', '1')))
SKIP_SELF = bool(int(_os.environ.get('K_SKIP_SELF', '0')))


class Prog:
    ENGS = ("pe", "act", "dve", "pool", "sp")

    def __init__(self, nc, es):
        self.nc = nc
        self.ops = {e: [] for e in self.ENGS}
        self.sem = {e: es.enter_context(nc.semaphore("sem_" + e)) for e in self.ENGS}
        self.cnt = {e: 0 for e in self.ENGS}
        self.RING = 20
        self.dsem = {q: [es.enter_context(nc.semaphore(f"dsem_{q}{i}")) for i in range(self.RING)]
                     for q in ("pool", "sp", "act")}
        self.dval = {q: [0] * self.RING for q in self.dsem}
        self.dn = {q: 0 for q in self.dsem}
        self.semobj = {}
        self.seen = {e: {} for e in self.ENGS}
        self.state = {}
        for e in self.ENGS:
            self.semobj[id(self.sem[e])] = self.sem[e]
        for q in self.dsem:
            for s in self.dsem[q]:
                self.semobj[id(s)] = s
        self.all_tokens = {}
        import os as _os
        self.limit = int(_os.environ.get('K_LIMIT', '0'))
        self.nops = 0
        self.force = False
        self.log = []
        self.marks = []

    @staticmethod
    def _k(key):
        if isinstance(key, tuple):
            return key[0], key[1:]
        return key, "*"

    def _deps(self, key, is_write, toks, eng=None):
        name, sub = self._k(key)
        st = self.state.setdefault(name, {})
        subs = list(st.keys()) if sub == "*" else [s for s in (sub, "*") if s in st]
        for s in subs:
            e = st[s]
            if e["w"] is not None:
                toks.append(e["w"])
            if is_write:
                toks.extend(e["r"].items())
            elif name == "ps":
                own = id(self.sem[eng]) if eng in self.sem else None
                toks.extend((sid, v) for sid, v in e["r"].items() if sid != own)

    def _rec(self, key, is_write, tok):
        name, sub = self._k(key)
        st = self.state.setdefault(name, {})
        if is_write:
            if sub == "*":
                st.clear()
            st[sub] = {"w": tok, "r": {}}
        else:
            e = st.setdefault(sub, {"w": None, "r": {}})
            if e["r"].get(tok[0], 0) < tok[1]:
                e["r"][tok[0]] = tok[1]

    def _waits(self, eng, toks, skip_self):
        own = id(self.sem[eng])
        need = {}
        for sid, val in toks:
            if skip_self and sid == own:
                continue
            if need.get(sid, 0) < val:
                need[sid] = val
        for sid, val in need.items():
            if self.seen[eng].get(sid, 0) < val:
                self.seen[eng][sid] = val
                s = self.semobj[sid]
                self.ops[eng].append(lambda e, s=s, val=val: e.wait_ge(s, val))

    def op(self, eng, method, reads=(), writes=(), **kw):
        self.nops += 1
        if self.limit and self.nops > self.limit and not self.force:
            return
        self.log.append((self.nops, eng, method, [str(k) for k in writes]))
        toks = []
        ENG = eng
        for k in reads:
            self._deps(k, False, toks, ENG)
        for k in writes:
            self._deps(k, True, toks, ENG)
        self._waits(eng, toks, skip_self=(eng == "pe" or SKIP_SELF))
        self.cnt[eng] += 1
        tok = (id(self.sem[eng]), self.cnt[eng])
        s = self.sem[eng]
        self.ops[eng].append(lambda e, method=method, kw=kw, s=s: getattr(e, method)(**kw).then_inc(s, 1))
        for k in reads:
            self._rec(k, False, tok)
        for k in writes:
            self._rec(k, True, tok)
        self.all_tokens[tok[0]] = tok[1]

    def mmgroup(self, kws, reads, writes):
        self.nops += 1
        if self.limit and self.nops > self.limit and not self.force:
            return
        self.log.append((self.nops, 'pe', kws[0][0], [str(k) for k in writes]))
        toks = []
        ENG = "pe"
        for k in reads:
            self._deps(k, False, toks, ENG)
        for k in writes:
            self._deps(k, True, toks, ENG)
        self._waits("pe", toks, skip_self=True)
        self.cnt["pe"] += 1
        tok = (id(self.sem["pe"]), self.cnt["pe"])
        s = self.sem["pe"]
        self.npe_real = getattr(self, "npe_real", 0) + len(kws)
        for i, (method, kw) in enumerate(kws):
            if i == len(kws) - 1:
                self.ops["pe"].append(lambda e, method=method, kw=kw, s=s: getattr(e, method)(**kw).then_inc(s, 1))
            else:
                self.ops["pe"].append(lambda e, method=method, kw=kw: getattr(e, method)(**kw))
        for k in reads:
            self._rec(k, False, tok)
        for k in writes:
            self._rec(k, True, tok)
        self.all_tokens[tok[0]] = tok[1]

    def dma(self, q, out, in_, reads=(), writes=()):
        self.nops += 1
        if self.limit and self.nops > self.limit and not self.force:
            return
        self.log.append((self.nops, q, 'dma', [str(k) for k in writes]))
        toks = []
        ENG = q
        for k in reads:
            self._deps(k, False, toks, ENG)
        for k in writes:
            self._deps(k, True, toks, ENG)
        i = self.dn[q] % self.RING
        self.dn[q] += 1
        s = self.dsem[q][i]
        if self.dval[q][i] > 0:
            toks.append((id(s), self.dval[q][i]))
        self._waits(q, toks, skip_self=False)
        self.dval[q][i] += 16
        tok = (id(s), self.dval[q][i])
        self.ops[q].append(lambda e, out=out, in_=in_, s=s: e.dma_start(out=out, in_=in_).then_inc(s, 16))
        for k in reads:
            self._rec(k, False, tok)
        for k in writes:
            self._rec(k, True, tok)
        self.all_tokens[tok[0]] = tok[1]

    def mark(self, label):
        self.marks.append((label, getattr(self, "npe_real", 0)))

    def barrier(self, engines=None):
        toks = list(self.all_tokens.items())
        for e in (engines or self.ENGS):
            self._waits(e, toks, skip_self=True)

    def emit(self, block):
        nc = self.nc
        m = {"pe": block.tensor, "act": block.scalar, "dve": block.vector, "pool": block.gpsimd, "sp": block.sync}
        for name in self.ENGS:
            lst = self.ops[name]

            def body(e, lst=lst):
                for f in lst:
                    f(e)
            m[name](body)


def build_program(LP, DEPTH, MB=256, PHASES=("ffn1", "mix", "ffn2")):
    T = LP + NS * LS
    nc = bass.Bass("TRN2", target_bir_lowering=False)
    dt_ = lambda name, shape, kind="ExternalInput", dt=F32: nc.dram_tensor(name, shape, dt, kind=kind).ap()
    xT_d = dt_("xT", [D, T])
    memT_d = dt_("memT", [D, 256])
    pool_in = dt_("pool_in", [DEPTH, 512, NS, 15])
    conv_in = dt_("conv_in", [DEPTH, 2048, NS, 3])
    ssmT_in = dt_("ssmT_in", [DEPTH, NS, 128, 1024])
    kT_in = dt_("kT_in", [DEPTH, NS, 4, 128, 256])
    v_in = dt_("v_in", [DEPTH, NS, 256, 512])
    vecs_d = dt_("vecs", [128, NVL * DEPTH + 8])
    rows_d = dt_("rows", [128, DEPTH * 32])
    consts_d = dt_("consts", [128, 720])
    W = {}
    for nm, shp in (("ffn1_w_gate", [DEPTH, D, DFF]), ("ffn1_w_up", [DEPTH, D, DFF]), ("ffn1_w_down", [DEPTH, DFF, D]),
                    ("ffn2_w_gate", [DEPTH, D, DFF]), ("ffn2_w_up", [DEPTH, D, DFF]), ("ffn2_w_down", [DEPTH, DFF, D]),
                    ("w_in", [DEPTH, D, INC]), ("pool_w", [DEPTH, 4, 128, 128]),
                    ("w_mem_k", [DEPTH, D, 512]), ("w_mem_v", [DEPTH, D, 512]),
                    ("w_br_pool", [DEPTH, 512, D]), ("w_br_ssm", [DEPTH, 1024, D]), ("w_br_mem", [DEPTH, 512, D]),
                    ("w_o", [DEPTH, D, D])):
        W[nm] = dt_(nm, shp)
    WB = {}
    for nm, shp in (("w_in", [DEPTH, D, INC]), ("pool_w", [DEPTH, 4, 128, 128]), ("w_br_pool", [DEPTH, 512, D]), ("w_br_ssm", [DEPTH, 1024, D]),
                    ("w_br_mem", [DEPTH, 512, D]), ("w_o", [DEPTH, D, D])):
        WB[nm] = nc.dram_tensor("wb_" + nm, shp, BF16, kind="Internal").ap()
    yT_o = dt_("yT", [D, T], "ExternalOutput")
    pool_o = dt_("pool_o", [DEPTH, 512, NS + 1, 15], "ExternalOutput")
    conv_o = dt_("conv_o", [DEPTH, 2048, NS + 1, 3], "ExternalOutput")
    ssmT_o = dt_("ssmT_o", [DEPTH, NS + 1, 128, 1024], "ExternalOutput")
    mkT_o = dt_("mkT_o", [DEPTH, 512, 256], "ExternalOutput")
    mv_o = dt_("mv_o", [DEPTH, 256, 512], "ExternalOutput")

    es = ExitStack()
    sb = lambda name, shape, dt=F32: es.enter_context(nc.sbuf_tensor("s_" + name, shape, dt))
    P = Prog(nc, es)
    x = sb("x", [128, KC, T])
    vecs = sb("vecs", [128, NVL * DEPTH + 8])
    rows = sb("rows", [128, DEPTH * 32])
    cst = sb("cst", [128, 720])
    identb = sb("identb", [128, 128], BF16)
    onesb = sb("onesb", [128, 128], BF16)
    EPSC = sb("epsc", [128, 1])
    arow = sb("arow", [128, 16])
    NPAGE = 19
    PAGE = 1024
    wsl = sb("wsl", [128, NPAGE * PAGE], BF16)
    wsmall = sb("wsmall", [128, 2, 640], BF16)
    rstd = sb("rstd", [128, 512])
    KTp = sb("KTp", [128, 4, 256], BF16)
    Vp = sb("Vp", [128, 2, 512], BF16)
    halp = sb("halp", [128, 16, 3])
    hbh = sb("hbh", [128, KC, 4], BF16)
    hT = sb("hT", [128, 1024])
    hTb = sb("hTb", [128, 1024], BF16)
    UBW = max(15 + MB, NS * (15 + LS))
    ub = sb("ub", [128, 4, UBW])
    ARENA_B = 76 * 1024
    arena = sb("arena", [128, ARENA_B // 4])
    ps = [es.enter_context(nc.psum_tensor(f"ps{i}", [128, 512], F32)) for i in range(8)]
    psb = [p_[:].bitcast(BF16) for p_ in ps]

    def carve(off, shape, dt=F32):
        n = 1
        for s_ in shape[1:]:
            n *= s_
        nb = n * (4 if dt == F32 else 2)
        assert off % 4 == 0 and off + nb <= ARENA_B, (off, nb)
        v = arena[:, off // 4: (off + nb + 3) // 4]
        if dt != F32:
            v = v.bitcast(dt)[:, 0:n]
        if len(shape) == 3:
            v = v.rearrange("p (a b) -> p a b", b=shape[2])
        elif len(shape) == 4:
            v = v.rearrange("p (a b c) -> p a b c", b=shape[2], c=shape[3])
        return v, off + ((nb + 3) // 4) * 4

    o = 0
    xn, o = carve(o, [128, KC, T], BF16)
    sqF, o = carve(o, [128, KC, 512], BF16)
    hTf, o = carve(o, [128, 2, 4, 512], BF16)
    sg, o = carve(o, [128, 2, 512])
    youtA, o = carve(o, [128, 2, 512])
    o = 0
    hb, o = carve(o, [128, KC, MB], BF16)
    ypool, o = carve(o, [128, 4, MB], BF16)
    ymem, o = carve(o, [128, 4, MB], BF16)
    yssm, o = carve(o, [128, KC, MB], BF16)
    o_stage = o
    sqM, o = carve(o, [128, KC, MB], BF16)
    pa, o = carve(o, [128, UBW])
    pb_, o = carve(o, [128, UBW])
    dpl, o = carve(o, [128, 4, MB], BF16)
    qT, o = carve(o, [128, 4, MB], BF16)
    KTs, o = carve(o, [128, 2, 4, 256], BF16)
    Vs, o = carve(o, [128, 2, 2, 512], BF16)
    att_mx, o = carve(o, [128, 2, 4])
    att_pe, o = carve(o, [128, 2, 256])
    att_pb, o = carve(o, [128, 2, 256], BF16)
    att_pT, o = carve(o, [128, 2, 2, 128], BF16)
    Qexp, o = carve(o, [128, 4, 1088], BF16)
    att_pT4, o = carve(o, [128, 4, 2, 64], BF16)
    gsb, o = carve(o, [128, 2, 3, MB])
    mt, o = carve(o, [128, 2, 3, MB])
    mrgb, o = carve(o, [128, KC, MB], BF16)
    o_mem = o
    o = o_stage
    memx, o = carve(o, [128, KC, 256])
    memn, o = carve(o, [128, KC, 256], BF16)
    sqX, o = carve(o, [128, KC, 256], BF16)
    kst, o = carve(o, [128, 4, 256])
    vst, o = carve(o, [128, 2, 512])
    o = o_stage
    siluz, o = carve(o, [128, KC, MB])
    cbuf, o = carve(o, [128, 2, NS * (3 + LS) if NS * (3 + LS) > 3 + MB else 3 + MB])
    cacc, o = carve(o, [128, 2, MB])
    xsT, o = carve(o, [128, KC, MB])
    BT, o = carve(o, [128, 4, MB], BF16)
    CT, o = carve(o, [128, 4, MB], BF16)
    dtb, o = carve(o, [128, 2, 16])
    lab, o = carve(o, [128, 2, 16])
    acum, o = carve(o, [128, 2, 16])
    nacum, o = carve(o, [128, 2, 16])
    te, o = carve(o, [128, 2, 16])
    cdr, o = carve(o, [128, 2, 16])
    xdt, o = carve(o, [128, 2, 1024], BF16)
    Btok, o = carve(o, [128, 2, 512], BF16)
    seg, o = carve(o, [128, 2, 4, 128])
    Dm, o = carve(o, [128, 2, 16, 128], BF16)
    cbm, o = carve(o, [128, 2, 4, 128], BF16)
    fsr, o = carve(o, [128, 2, 16, 128], BF16)
    y2, o = carve(o, [128, 2, 128])
    ysq, o = carve(o, [128, 2, 128], BF16)
    yrs, o = carve(o, [128, 128])
    htmp, o = carve(o, [128, 1024])
    cdrS, o = carve(o, [128, 16, 16])
    Bmk, o = carve(o, [128, 2, 512], BF16)

    TRI = cst[:, 0:128]
    IDF = cst[:, 128:256]
    FIX = cst[:, 256:320]
    TRIBD = cst[:, 320:384]
    SAMEBD = cst[:, 384:448]
    ONESF = cst[:, 448:576]
    SEQM = cst[:, 576:592]

    def vcol(l, off, n=1):
        return vecs[:, l * NVL + off: l * NVL + off + n]
    V_F1, V_MIX, V_F2, V_GB, V_PS, V_CW, V_CB, V_DS, V_SN, V_MN = 0, 8, 16, 24, 48, 52, 116, 132, 140, 148
    V_FIN = NVL * DEPTH

    wctr = [0]

    def wload(src_ap, a, b, rk=()):
        n = a * b
        npg = (n + PAGE - 1) // PAGE
        p0 = wctr[0]
        if p0 + npg > NPAGE:
            p0 = 0
        wctr[0] = (p0 + npg) % NPAGE
        keys = [("wsl", p) for p in range(p0, p0 + npg)]
        view = wsl[:, p0 * PAGE:p0 * PAGE + n].rearrange("p (a b) -> p a b", b=b)
        P.dma("pool", view, src_ap, reads=list(rk), writes=keys)
        return view, keys

    def kcp(src):
        return src.rearrange("(c p) n -> p c n", p=128)

    def MM(out, lhsT, rhs, start=True, stop=True):
        return ("matmul", dict(out=out, lhsT=lhsT, rhs=rhs, start=start, stop=stop))

    def TR(out, in_, identity):
        return ("transpose", dict(out=out, in_=in_, identity=identity))

    def rmsnorm(dst, dkeyf, src, skeyf, tb, gcol, sq, inv_n=1.0 / D, nchunk=KC):
        for c in range(nchunk):
            P.op("act", "activation", reads=[skeyf(c)], writes=[("sq", c)], out=sq[:, c, 0:tb], in_=src[c], func=AF.Square)
        P.mmgroup([MM(ps[7][:, 0:tb], onesb[:], sq[:, c, 0:tb], c == 0, c == nchunk - 1) for c in range(nchunk)],
                  reads=["sq", "onesb"], writes=[("ps", 7)])
        P.op("act", "activation", reads=[("ps", 7), "epsc"], writes=["rstd"], out=rstd[:, 0:tb], in_=ps[7][:, 0:tb], func=AF.Sqrt, bias=EPSC[:], scale=inv_n)
        P.op("dve", "reciprocal", reads=["rstd"], writes=["rstd"], out=rstd[:, 0:tb], in_=rstd[:, 0:tb])
        for c in range(nchunk):
            P.op("dve", "scalar_tensor_tensor", reads=[skeyf(c), "rstd", "vecs"], writes=[dkeyf(c)],
                 out=dst[c], in0=src[c], scalar=gcol[:, c:c + 1], in1=rstd[:, 0:tb], op0=ALU.mult, op1=ALU.mult)

    P.dma("sp", vecs[:], vecs_d[:, :], writes=["vecs"])
    P.dma("sp", rows[:], rows_d[:, :], writes=["rows"])
    P.dma("sp", cst[:], consts_d[:, :], writes=["cst"])
    P.dma("sp", x[:], xT_d.rearrange("(c p) t -> p c t", p=128), writes=["x"])
    P.op("dve", "tensor_copy", reads=["cst"], writes=["identb"], out=identb[:], in_=IDF)
    P.op("dve", "memset", writes=["onesb"], ap=onesb[:], constant=1.0)
    P.op("dve", "memset", writes=["epsc"], ap=EPSC[:], constant=EPS)

    fblocks = [(t, min(512, LP - t)) for t in range(0, LP, 512)] + [(LP, NS * LS)]
    groups = [(0, 4), (4, 4), (8, 4), (12, 4), (16, 4), (20, 2)]

    def precast_pieces(l):
        pcs = []
        for r in range(8):
            pcs.append((WB["w_in"][l, r * 128:(r + 1) * 128, :].rearrange("p (a b) -> p a b", a=4), W["w_in"][l, r * 128:(r + 1) * 128, :].rearrange("p (a b) -> p a b", a=4)))
        for nm, nrow in (("w_br_pool", 512), ("w_br_ssm", 1024), ("w_br_mem", 512), ("w_o", 1024)):
            for r in range(0, nrow, 512):
                pcs.append((WB[nm][l, r:r + 512, :], W[nm][l, r:r + 512, :]))
        pcs.append((WB["pool_w"][l].rearrange("g c d -> (g c) d"), W["pool_w"][l].rearrange("g c d -> (g c) d")))
        return pcs

    def ffn(l, wg_d, wu_d, wd_d, gcol, pcs=()):
        pcs = list(pcs)
        def norm_blk(bi):
            t0, tb = fblocks[bi]
            rmsnorm([xn[:, c, t0:t0 + tb] for c in range(KC)], lambda c, t0=t0: ("xn", t0),
                    [x[:, c, t0:t0 + tb] for c in range(KC)], lambda c, t0=t0: ("x", t0, c), tb, gcol, sqF)
        norm_blk(0)
        normed = 1
        pi = 0
        for (f0, G) in groups:
            wgs, wus, wds = [], [], []
            for ci in range(G):
                f = f0 + ci
                wgs.append(wload(kcp(wg_d[l, :, f * 128:(f + 1) * 128]), KC, 128))
                wus.append(wload(kcp(wu_d[l, :, f * 128:(f + 1) * 128]), KC, 128))
                wds.append(wload(wd_d[l, f * 128:(f + 1) * 128, :].rearrange("p (o n) -> p o n", o=1), 1, D))
            for _ in range(3):
                if pcs:
                    o_, i_ = pcs.pop(0)
                    P.dma("pool", o_, i_, writes=[("wb%d" % l, len(pcs))])
            for bi_, (t0, tb) in enumerate(fblocks):
                par = pi % 2
                pi += 1
                if normed < len(fblocks) and bi_ + 1 == normed:
                    norm_blk(normed)
                    normed += 1
                for ci in range(G):
                    bg, bu = (0, 1) if (ci % 2 == 0) else (2, 3)
                    wg, kg = wgs[ci]
                    wu, ku = wus[ci]
                    P.mmgroup([MM(ps[bg][:, 0:tb], wg[:, c, :], xn[:, c, t0:t0 + tb], c == 0, c == KC - 1) for c in range(KC)],
                              reads=kg + [("xn", t0)], writes=[("ps", bg)])
                    P.mmgroup([MM(ps[bu][:, 0:tb], wu[:, c, :], xn[:, c, t0:t0 + tb], c == 0, c == KC - 1) for c in range(KC)],
                              reads=ku + [("xn", t0)], writes=[("ps", bu)])
                    P.op("act", "activation", reads=[("ps", bg)], writes=[("sg", ci % 2)], out=sg[:, ci % 2, 0:tb], in_=ps[bg][:, 0:tb], func=AF.Silu)
                    P.op("dve", "tensor_tensor", reads=[("ps", bu), ("sg", ci % 2)], writes=[("hTf", par, ci)],
                         out=hTf[:, par, ci, 0:tb], in0=ps[bu][:, 0:tb], in1=sg[:, ci % 2, 0:tb], op=ALU.mult)
                for dc in range(KC):
                    by = 4 + dc % 3
                    P.mmgroup([MM(ps[by][:, 0:tb], wds[ci][0][:, 0, dc * 128:(dc + 1) * 128], hTf[:, par, ci, 0:tb], ci == 0, ci == G - 1) for ci in range(G)],
                              reads=sum([wds[ci][1] for ci in range(G)], []) + [("hTf", par, ci) for ci in range(G)], writes=[("ps", by)])
                    P.op("dve", "scalar_tensor_tensor", reads=[("ps", by), ("x", t0, dc)], writes=[("x", t0, dc)],
                         out=x[:, dc, t0:t0 + tb], in0=ps[by][:, 0:tb], scalar=0.5, in1=x[:, dc, t0:t0 + tb], op0=ALU.mult, op1=ALU.add)

    WSRC = WB if (USE_PRECAST and 'ffn1' in PHASES) else W
    WRK = (lambda l: ["wb%d" % l]) if (USE_PRECAST and 'ffn1' in PHASES) else (lambda l: [])

    def memory_stage(l):
        P.dma("sp", memx[:], memT_d.rearrange("(c p) t -> p c t", p=128), writes=["memx"])
        rmsnorm([memn[:, c, :] for c in range(KC)], lambda c: ("memn", c), [memx[:, c, :] for c in range(KC)], lambda c: "memx",
                256, vcol(l, V_MN, 8), sqX)
        wk, kk = wload(kcp(W["w_mem_k"][l]), KC, 512)
        wv, kv = wload(kcp(W["w_mem_v"][l]), KC, 512)
        for hd in range(4):
            P.mmgroup([MM(ps[hd][:, 0:256], wk[:, c, hd * 128:(hd + 1) * 128], memn[:, c, :], c == 0, c == KC - 1) for c in range(KC)],
                      reads=kk + ["memn"], writes=[("ps", hd)])
            P.op("act", "activation", reads=[("ps", hd)], writes=[("kst", hd)], out=kst[:, hd, :], in_=ps[hd][:, 0:256], func=AF.Copy)
            P.op("dve", "tensor_copy", reads=[("ps", hd)], writes=[("KTp", hd)], out=KTp[:, hd, :], in_=ps[hd][:, 0:256])
        P.dma("sp", mkT_o[l].rearrange("(h p) m -> p h m", p=128), kst[:], reads=["kst"], writes=[("mkT_o", l)])
        for mc in range(2):
            P.mmgroup([MM(ps[4 + mc][:, 0:512], memn[:, c, mc * 128:(mc + 1) * 128], wv[:, c, :], c == 0, c == KC - 1) for c in range(KC)],
                      reads=kv + ["memn"], writes=[("ps", 4 + mc)])
            P.op("act", "activation", reads=[("ps", 4 + mc)], writes=[("vst", mc)], out=vst[:, mc, :], in_=ps[4 + mc][:, 0:512], func=AF.Copy)
            P.op("dve", "tensor_copy", reads=[("ps", 4 + mc)], writes=[("Vp", mc)], out=Vp[:, mc, :], in_=ps[4 + mc][:, 0:512])
        P.dma("sp", mv_o[l].rearrange("(c p) n -> p c n", p=128), vst[:], reads=["vst"], writes=[("mv_o", l)])
        P.op("act", "activation", reads=["rows"], writes=["arow"], out=arow[:], in_=rows[:, l * 32 + 16:l * 32 + 32], func=AF.Exp)
        P.op("dve", "tensor_scalar", reads=["arow"], writes=["arow"], out=arow[:], in0=arow[:], scalar1=-1.0, scalar2=None, op0=ALU.mult)

    def v3(ap2, nseq, L):
        return ap2.rearrange("p (s j) -> p s j", j=L)

    def s1_pool(l, kind, first, last, nseq, L, TB):
        wp, kp = wload(kcp(WSRC["w_in"][l, :, C_POOL:C_POOL + 512]), KC, 512, WRK(l))
        P.dma("pool", wsmall[:, 0, 0:512].rearrange("p (g d) -> p g d", d=128), WSRC["pool_w"][l].rearrange("g c d -> c g d"), reads=WRK(l), writes=[("wsmall", 0)])
        pw = wsmall[:, 0, 0:512].rearrange("p (g d) -> p g d", d=128)
        HL = 15 + L

        def U(g):
            return v3(ub[:, g, 0:nseq * HL], nseq, HL)
        A = v3(pa[:, 0:nseq * HL], nseq, HL)
        Bv = v3(pb_[:, 0:nseq * HL], nseq, HL)
        if kind == "p":
            if first:
                P.op("dve", "memset", writes=["ub"], ap=ub[:, :, 0:15], constant=0.0)
            else:
                P.op("dve", "tensor_copy", reads=["ub"], writes=["ub"], out=ub[:, :, 0:15], in_=ub[:, :, MB:MB + 15])
        else:
            for g in range(4):
                P.dma("sp", U(g)[:, :, 0:15], pool_in[l, g * 128:(g + 1) * 128, :, :], reads=[], writes=["ub"])
        for g in range(4):
            P.mmgroup([MM(ps[g][:, 0:TB], wp[:, c, g * 128:(g + 1) * 128], hb[:, c, 0:TB], c == 0, c == KC - 1) for c in range(KC)],
                      reads=kp + ["hb"], writes=[("ps", g)])
            P.op("act", "activation", reads=[("ps", g), "ub"], writes=[("ub", g)], out=U(g)[:, :, 15:15 + L], in_=v3(ps[g][:, 0:TB], nseq, L), func=AF.Copy)
        for g in range(4):
            w = 2 ** (g + 1)
            cur, ckey = U(g), ("ub", g)
            bufs = [(A, "pa"), (Bv, "pb")]
            for k in range(1, g + 2):
                lo = 2 ** k - 1
                sh = 2 ** (k - 1)
                nb, nkey = bufs[k % 2]
                P.op("dve", "tensor_tensor", reads=[ckey], writes=[nkey], out=nb[:, :, lo:HL], in0=cur[:, :, lo:HL], in1=cur[:, :, lo - sh:HL - sh], op=ALU.add)
                cur, ckey = nb, nkey
            if kind == "p" and first:
                P.op("dve", "tensor_tensor", reads=[ckey, "cst"], writes=[ckey], out=cur[:, 0, 15:30], in0=cur[:, 0, 15:30], in1=FIX[:, g * 16:g * 16 + 15], op=ALU.mult)
            P.op("dve", "scalar_tensor_tensor", reads=[ckey, ("ub", g)], writes=[("dpl", g)], out=v3(dpl[:, g, 0:TB], nseq, L),
                 in0=cur[:, :, 15:HL], scalar=1.0 / w, in1=U(g)[:, :, 15:HL], op0=ALU.mult, op1=ALU.subtract)
        for g in range(4):
            b_ = 4 + g % 2
            P.mmgroup([MM(ps[b_][:, 0:TB], pw[:, g, :], dpl[:, g, 0:TB])], reads=[("wsmall", 0), ("dpl", g)], writes=[("ps", b_)])
            P.op("act", "activation", reads=[("ps", b_), "vecs"], writes=[("ypool", g)], out=ypool[:, g, 0:TB], in_=ps[b_][:, 0:TB], func=AF.Identity,
                 scale=vcol(l, V_PS + g))
        if kind == "p" and last:
            for g in range(4):
                P.dma("sp", pool_o[l, g * 128:(g + 1) * 128, 0, :], ub[:, g, MB:MB + 15], reads=[("ub", g)], writes=[("pool_o", l, g, 0)])
        if kind == "s":
            for g in range(4):
                P.dma("sp", pool_o[l, g * 128:(g + 1) * 128, 1:NS + 1, :], U(g)[:, :, LS:LS + 15], reads=[("ub", g)], writes=[("pool_o", l, g, 1)])

    def s2_sample_batched(l):
        Lt = NS * LS
        for hd in range(4):
            P.op("dve", "memset", writes=[("Qexp", hd)], ap=Qexp[:, hd, :], constant=0.0)
            P.op("dve", "tensor_copy", reads=[("qT", hd)], writes=[("Qexp", hd)], out=Qexp[:, hd, :].rearrange("p (b s) -> p b s", s=68)[:, :, 0:LS],
                 in_=qT[:, hd, 0:Lt].rearrange("p (b j) -> p b j", j=LS))
        for b in range(NS):
            sl = b % 2
            P.dma("pool", KTs[:, sl], kT_in[l, b].rearrange("h d m -> d h m"), writes=[("KTs", sl)])
            for hd in range(4):
                P.mmgroup([("matmul", dict(out=ps[hd][0:Lt, 0:256], lhsT=Qexp[:, hd, b * 64:(b + 1) * 64], rhs=KTs[:, sl, hd, :], start=(b == 0), stop=(b == NS - 1)))],
                          reads=[("Qexp", hd), ("KTs", sl)], writes=[("ps", hd)])
        for hd in range(4):
            par = hd % 2
            sc = ps[hd][0:Lt, 0:256]
            mxk = ("att_mx", par)
            P.op("dve", "reduce_max", reads=[("ps", hd)], writes=[mxk], out=att_mx[0:Lt, par, 1:2], in_=sc, axis=AX.X, negate=True)
            P.op("act", "activation", reads=[("ps", hd), mxk], writes=[("att_pe", par), mxk], out=att_pe[0:Lt, par, :], in_=sc, func=AF.Exp,
                 bias=att_mx[0:Lt, par, 1:2], scale=1.0, accum_out=att_mx[0:Lt, par, 2:3])
            P.op("dve", "reciprocal", reads=[mxk], writes=[mxk], out=att_mx[0:Lt, par, 3:4], in_=att_mx[0:Lt, par, 2:3])
            P.op("dve", "tensor_scalar", reads=[("att_pe", par), mxk], writes=[("att_pb", par)], out=att_pb[0:Lt, par, :], in0=att_pe[0:Lt, par, :],
                 scalar1=att_mx[0:Lt, par, 3:4], scalar2=None, op0=ALU.mult)
            P.mmgroup([TR(psb[6][:, mc * 128:mc * 128 + Lt], att_pb[0:Lt, par, mc * 128:(mc + 1) * 128], identb[0:Lt, 0:Lt]) for mc in range(2)],
                      reads=[("att_pb", par), "identb"], writes=[("ps", 6)])
            P.op("act", "activation", reads=[("ps", 6)], writes=[("att_pT4", hd)], out=att_pT4[:, hd, :, 0:Lt],
                 in_=psb[6][:, 0:256].rearrange("p (m t) -> p m t", t=128)[:, :, 0:Lt], func=AF.Copy)
        for b in range(NS):
            sl = b % 2
            P.dma("pool", Vs[:, sl], kcp(v_in[l, b]), writes=[("Vs", sl)])
            kws = []
            for hd in range(4):
                for mc in range(2):
                    kws.append(("matmul", dict(out=ps[7][:, hd * 64 + b * LS:hd * 64 + (b + 1) * LS], lhsT=Vs[:, sl, mc, hd * 128:(hd + 1) * 128],
                                               rhs=att_pT4[:, hd, mc, b * LS:(b + 1) * LS], start=(mc == 0), stop=(mc == 1), skip_group_check=True)))
            P.mmgroup(kws, reads=[("Vs", sl), "att_pT4"], writes=[("ps", 7)])
        P.op("dve", "tensor_copy", reads=[("ps", 7)], writes=["ymem"], out=ymem[:, :, 0:Lt], in_=ps[7][:, 0:4 * 64].rearrange("p (h t) -> p h t", t=64))

    def s2_attn(l, kind, nseq, L, TB):
        wq, kq = wload(kcp(WSRC["w_in"][l, :, C_Q:C_Q + 512]), KC, 512, WRK(l))
        for hd in range(4):
            P.mmgroup([MM(ps[hd][:, 0:TB], wq[:, c, hd * 128:(hd + 1) * 128], hb[:, c, 0:TB], c == 0, c == KC - 1) for c in range(KC)],
                      reads=kq + ["hb"], writes=[("ps", hd)])
            P.op("dve", "tensor_scalar", reads=[("ps", hd)], writes=[("qT", hd)], out=qT[:, hd, 0:TB], in0=ps[hd][:, 0:TB], scalar1=SCALE, scalar2=None, op0=ALU.mult)
        it = 0
        if kind == "s":
            s2_sample_batched(l)
            return
        tiles = [(None, c0, 128) for c0 in range(0, TB, 128)]
        for (b, c0, Lt) in tiles:
            if b is None:
                KT, Vv, kkey, vkey = KTp, Vp, "KTp", "Vp"
            else:
                sl = b % 2
                P.dma("pool", KTs[:, sl], kT_in[l, b].rearrange("h d m -> d h m"), writes=[("KTs", sl)])
                P.dma("pool", Vs[:, sl], kcp(v_in[l, b]), writes=[("Vs", sl)])
                KT, Vv, kkey, vkey = KTs[:, sl], Vs[:, sl], ("KTs", sl), ("Vs", sl)
            for hd in range(4):
                par = it % 2
                it += 1
                sc = ps[4 + par][0:Lt, 0:256]
                P.mmgroup([MM(sc, qT[:, hd, c0:c0 + Lt], KT[:, hd, :])], reads=[("qT", hd), kkey], writes=[("ps", 4 + par)])
                mxk = ("att_mx", par)
                P.op("dve", "reduce_max", reads=[("ps", 4 + par)], writes=[mxk], out=att_mx[0:Lt, par, 1:2], in_=sc, axis=AX.X, negate=True)
                P.op("act", "activation", reads=[("ps", 4 + par), mxk], writes=[("att_pe", par), mxk], out=att_pe[0:Lt, par, :], in_=sc, func=AF.Exp,
                     bias=att_mx[0:Lt, par, 1:2], scale=1.0, accum_out=att_mx[0:Lt, par, 2:3])
                P.op("dve", "reciprocal", reads=[mxk], writes=[mxk], out=att_mx[0:Lt, par, 3:4], in_=att_mx[0:Lt, par, 2:3])
                P.op("dve", "tensor_scalar", reads=[("att_pe", par), mxk], writes=[("att_pb", par)], out=att_pb[0:Lt, par, :], in0=att_pe[0:Lt, par, :],
                     scalar1=att_mx[0:Lt, par, 3:4], scalar2=None, op0=ALU.mult)
                P.mmgroup([TR(psb[6][:, mc * 128:mc * 128 + Lt], att_pb[0:Lt, par, mc * 128:(mc + 1) * 128], identb[0:Lt, 0:Lt]) for mc in range(2)],
                          reads=[("att_pb", par), "identb"], writes=[("ps", 6)])
                P.op("act", "activation", reads=[("ps", 6)], writes=[("att_pT", par)], out=att_pT[:, par, :, 0:Lt],
                     in_=psb[6][:, 0:256].rearrange("p (m t) -> p m t", t=128)[:, :, 0:Lt], func=AF.Copy)
                P.mmgroup([MM(ps[7][:, 0:Lt], Vv[:, mc, hd * 128:(hd + 1) * 128], att_pT[:, par, mc, 0:Lt], mc == 0, mc == 1) for mc in range(2)],
                          reads=[vkey, ("att_pT", par)], writes=[("ps", 7)])
                P.op("dve", "tensor_copy", reads=[("ps", 7)], writes=[("ymem", hd)], out=ymem[:, hd, c0:c0 + Lt], in_=ps[7][:, 0:Lt])

    def ycol(j, Lc):
        return (0, j * 64) if Lc <= 64 else (j // 4, (j % 4) * 128)

    def chunk_A(l, c0, Lc, par, TRIm, SAMEm, sample):
        pk = lambda n: (n, par)
        wdt = wsmall[:, 1, 0:128].rearrange("p (c n) -> p c n", n=16)
        P.mmgroup([MM(ps[3][0:Lc, 16:32], hb[:, c, c0:c0 + Lc], wdt[:, c, :], c == 0, c == KC - 1) for c in range(KC)],
                  reads=["hb", ("wsmall", 1)], writes=[("ps", 3)])
        P.op("dve", "tensor_tensor", reads=[("ps", 3), "rows"], writes=[pk("dtb")], out=dtb[0:Lc, par, :], in0=ps[3][0:Lc, 16:32], in1=rows[0:Lc, l * 32:l * 32 + 16], op=ALU.add)
        P.op("act", "activation", reads=[pk("dtb")], writes=[pk("dtb")], out=dtb[0:Lc, par, :], in_=dtb[0:Lc, par, :], func=AF.Exp)
        P.op("act", "activation", reads=[pk("dtb")], writes=[pk("dtb")], out=dtb[0:Lc, par, :], in_=dtb[0:Lc, par, :], func=AF.Ln, bias=1.0)
        P.op("dve", "tensor_tensor", reads=[pk("dtb"), "arow"], writes=[pk("lab")], out=lab[0:Lc, par, :], in0=dtb[0:Lc, par, :], in1=arow[0:Lc, :], op=ALU.mult)
        P.mmgroup([MM(ps[3][0:Lc, 0:16], TRIm[0:Lc, 0:Lc], lab[0:Lc, par, :]), MM(ps[3][0:Lc, 32:48], SAMEm[0:Lc, 0:Lc], lab[0:Lc, par, :])],
                  reads=["cst", pk("lab")], writes=[("ps", 3)])
        P.op("dve", "tensor_scalar", reads=[("ps", 3)], writes=[pk("nacum")], out=nacum[0:Lc, par, :], in0=ps[3][0:Lc, 0:16], scalar1=-1.0, scalar2=None, op0=ALU.mult)
        P.op("dve", "tensor_tensor", reads=[("ps", 3), pk("nacum")], writes=[pk("te")], out=te[0:Lc, par, :], in0=ps[3][0:Lc, 32:48], in1=nacum[0:Lc, par, :], op=ALU.add)
        P.op("act", "activation", reads=[pk("te")], writes=[pk("te")], out=te[0:Lc, par, :], in_=te[0:Lc, par, :], func=AF.Exp)
        for hq in range(4):
            P.mmgroup([MM(ps[4 + hq][:, hh * 128:hh * 128 + Lc], lab[0:Lc, par, hq * 4 + hh:hq * 4 + hh + 1].broadcast_to([Lc, 128]), TRIm[0:Lc, 0:Lc]) for hh in range(4)],
                      reads=[pk("lab"), "cst"], writes=[("ps", 4 + hq)])
        nb = 1 if Lc <= 64 else 2
        for bk in range(2):
            P.mmgroup([TR(ps[bk][0:Lc, jj * 128:(jj + 1) * 128], xsT[:, bk * 4 + jj, c0:c0 + Lc], IDF) for jj in range(4)],
                      reads=["xsT", "cst"], writes=[("ps", bk)])
            P.op("dve", "tensor_tensor", reads=[("ps", bk), pk("dtb")], writes=[pk("xdt")], out=xdt[0:Lc, par, bk * 512:(bk + 1) * 512].rearrange("p (h q) -> p h q", q=64),
                 in0=ps[bk][0:Lc, :].rearrange("p (h q) -> p h q", q=64), in1=dtb[0:Lc, par, bk * 8:(bk + 1) * 8].unsqueeze(2).broadcast_to([Lc, 8, 64]), op=ALU.mult)
        P.mmgroup([TR(psb[2][0:Lc, g * 128:(g + 1) * 128], BT[:, g, c0:c0 + Lc], identb[:]) for g in range(4)], reads=["BT", "identb"], writes=[("ps", 2)])
        P.op("act", "activation", reads=[("ps", 2)], writes=[pk("Btok")], out=Btok[0:Lc, par, :], in_=psb[2][0:Lc, 0:512], func=AF.Copy)
        P.mmgroup([MM(ps[2][0:Lc, g * 128:g * 128 + Lc], BT[:, g, c0:c0 + Lc], CT[:, g, c0:c0 + Lc]) for g in range(4)], reads=["BT", "CT"], writes=[("ps", 2)])
        for g in range(4):
            P.op("dve", "tensor_tensor", reads=[("ps", 2), "cst"], writes=[("cbm", par, g)], out=cbm[0:Lc, par, g, 0:Lc], in0=ps[2][0:Lc, g * 128:g * 128 + Lc], in1=TRIm[0:Lc, 0:Lc], op=ALU.mult)
        for hq in range(4):
            sp_ = hq % 2
            for hh in range(4):
                h = hq * 4 + hh
                P.op("dve", "tensor_scalar", reads=[("ps", 4 + hq), pk("nacum")], writes=[("seg", sp_)], out=seg[0:Lc, sp_, hh, 0:Lc], in0=ps[4 + hq][0:Lc, hh * 128:hh * 128 + Lc],
                     scalar1=nacum[0:Lc, par, h:h + 1], scalar2=0.0, op0=ALU.add, op1=ALU.min)
            P.op("act", "activation", reads=[("seg", sp_)], writes=[("Dm", par, hq)], out=Dm[0:Lc, par, hq * 4:(hq + 1) * 4, 0:Lc], in_=seg[0:Lc, sp_, :, 0:Lc], func=AF.Exp)
            pv = ps[4 + hq][:, :].rearrange("p (h t) -> p h t", t=128)
            P.op("act", "activation", reads=[("ps", 4 + hq)], writes=[("fsr", par, hq)], out=fsr[:, par, hq * 4:(hq + 1) * 4, 0:Lc], in_=pv[:, :, 0:Lc], func=AF.Exp)
            if not sample:
                P.op("act", "activation", reads=[("ps", 4 + hq)], writes=[pk("cdr")], out=cdr[:, par, hq * 4:(hq + 1) * 4], in_=pv[:, :, Lc - 1], func=AF.Exp)
            else:
                P.op("act", "activation", reads=[("ps", 4 + hq)], writes=["cdrS"], out=cdrS[:, :, hq * 4:(hq + 1) * 4].rearrange("p b h -> p h b"),
                     in_=pv[:, :, LS - 1:Lc:LS], func=AF.Exp)
        for g in range(4):
            P.op("dve", "tensor_tensor", reads=[("Dm", par, g), ("cbm", par, g)], writes=[("Dm", par, g)], out=Dm[0:Lc, par, g * 4:(g + 1) * 4, 0:Lc], in0=Dm[0:Lc, par, g * 4:(g + 1) * 4, 0:Lc],
                 in1=cbm[0:Lc, par, g, 0:Lc].unsqueeze(1).broadcast_to([Lc, 4, Lc]), op=ALU.mult)
            P.op("dve", "tensor_tensor", reads=[("fsr", par, g), "CT"], writes=[("fsr", par, g)], out=fsr[:, par, g * 4:(g + 1) * 4, 0:Lc], in0=fsr[:, par, g * 4:(g + 1) * 4, 0:Lc],
                 in1=CT[:, g, c0:c0 + Lc].unsqueeze(1).broadcast_to([128, 4, Lc]), op=ALU.mult)

    def gating(l, c0, Lc):
        for g in range(4):
            for jj in range(2):
                j = 2 * g + jj
                bk, cl = ycol(j, Lc)
                yps = ps[bk][:, cl:cl + Lc]
                P.op("dve", "scalar_tensor_tensor", reads=["xsT", ("ps", bk), "vecs"], writes=[("y2", jj)], out=y2[:, jj, 0:Lc], in0=xsT[:, j, c0:c0 + Lc],
                     scalar=vcol(l, V_DS + j), in1=yps, op0=ALU.mult, op1=ALU.add)
                P.op("dve", "tensor_tensor", reads=[("y2", jj), "siluz"], writes=[("y2", jj)], out=y2[:, jj, 0:Lc], in0=y2[:, jj, 0:Lc], in1=siluz[:, j, c0:c0 + Lc], op=ALU.mult)
                P.op("act", "activation", reads=[("y2", jj)], writes=[("ysq", jj)], out=ysq[:, jj, 0:Lc], in_=y2[:, jj, 0:Lc], func=AF.Square)
            P.mmgroup([MM(ps[3][:, 128:128 + Lc], onesb[:], ysq[:, jj, 0:Lc], jj == 0, jj == 1) for jj in range(2)], reads=["ysq", "onesb"], writes=[("ps", 3)])
            P.op("act", "activation", reads=[("ps", 3), "epsc"], writes=["yrs"], out=yrs[:, 0:Lc], in_=ps[3][:, 128:128 + Lc], func=AF.Sqrt, bias=EPSC[:], scale=1.0 / 256)
            P.op("dve", "reciprocal", reads=["yrs"], writes=["yrs"], out=yrs[:, 0:Lc], in_=yrs[:, 0:Lc])
            for jj in range(2):
                j = 2 * g + jj
                P.op("dve", "scalar_tensor_tensor", reads=[("y2", jj), "yrs", "vecs"], writes=[("yssm", j)], out=yssm[:, j, c0:c0 + Lc], in0=y2[:, jj, 0:Lc],
                     scalar=vcol(l, V_SN + j), in1=yrs[:, 0:Lc], op0=ALU.mult, op1=ALU.mult)

    def chunk_B_prompt(l, c0, Lc, par, init, seq_out):
        if init == "zero":
            P.op("dve", "memset", writes=["hT"], ap=hT[:], constant=0.0)
            P.op("dve", "memset", writes=["hTb"], ap=hTb[:], constant=0.0)
        for j in range(8):
            kws = []
            bk, cl = ycol(j, Lc)
            for hh in range(2):
                h = 2 * j + hh
                outp = ps[bk][64 * hh:64 * hh + 64, cl:cl + Lc]
                kws.append(MM(outp, hTb[:, h * 64:(h + 1) * 64], fsr[:, par, h, 0:Lc], True, False))
                kws.append(MM(outp, xdt[0:Lc, par, h * 64:(h + 1) * 64], Dm[0:Lc, par, h, 0:Lc], False, True))
            P.mmgroup(kws, reads=["hTb", ("fsr", par, j // 2), ("xdt", par), ("Dm", par, j // 2)], writes=[("ps", bk)])
        gating(l, c0, Lc)
        P.op("dve", "tensor_tensor", reads=[("xdt", par), ("te", par)], writes=[("xdt", par)], out=xdt[0:Lc, par, :].rearrange("p (h q) -> p h q", q=64),
             in0=xdt[0:Lc, par, :].rearrange("p (h q) -> p h q", q=64), in1=te[0:Lc, par, :].unsqueeze(2).broadcast_to([Lc, 16, 64]), op=ALU.mult)
        for g in range(4):
            P.mmgroup([MM(ps[4 + g // 2][:, (g % 2) * 256:(g % 2 + 1) * 256], Btok[0:Lc, par, g * 128:(g + 1) * 128], xdt[0:Lc, par, g * 256:(g + 1) * 256])],
                      reads=[("Btok", par), ("xdt", par)], writes=[("ps", 4 + g // 2)])
        P.op("dve", "tensor_tensor", reads=["hT", ("cdr", par)], writes=["htmp"], out=htmp[:, :].rearrange("p (h q) -> p h q", q=64), in0=hT[:, :].rearrange("p (h q) -> p h q", q=64),
             in1=cdr[:, par, :].unsqueeze(2).broadcast_to([128, 16, 64]), op=ALU.mult)
        for bk in range(2):
            P.op("dve", "tensor_tensor", reads=["htmp", ("ps", 4 + bk)], writes=["hT"], out=hT[:, bk * 512:(bk + 1) * 512], in0=ps[4 + bk][:, :], in1=htmp[:, bk * 512:(bk + 1) * 512], op=ALU.add)
        P.op("act", "activation", reads=["hT"], writes=["hTb"], out=hTb[:], in_=hT[:], func=AF.Copy)
        if seq_out is not None:
            P.dma("sp", ssmT_o[l, seq_out], hT[:], reads=["hT"], writes=[("ssmT_o", l, seq_out)])

    def chunk_B_sample(l):
        Lc = NS * LS
        par = 0
        P.op("dve", "tensor_tensor", reads=[("xdt", 0), ("te", 0)], writes=[("xdt", 1)], out=xdt[0:Lc, 1, :].rearrange("p (h q) -> p h q", q=64),
             in0=xdt[0:Lc, 0, :].rearrange("p (h q) -> p h q", q=64), in1=te[0:Lc, 0, :].unsqueeze(2).broadcast_to([Lc, 16, 64]), op=ALU.mult)
        for b in range(NS):
            P.dma("act", hT[:], ssmT_in[l, b], writes=["hT"])
            P.op("act", "activation", reads=["hT"], writes=["hTb"], out=hTb[:], in_=hT[:], func=AF.Copy)
            kws = []
            for j in range(8):
                for hh in range(2):
                    h = 2 * j + hh
                    outp = ps[0][64 * hh:64 * hh + 64, j * 64 + b * LS:j * 64 + (b + 1) * LS]
                    kws.append(("matmul", dict(out=outp, lhsT=hTb[:, h * 64:(h + 1) * 64], rhs=fsr[:, 0, h, b * LS:(b + 1) * LS], start=(b == 0 and j == 0), stop=False, skip_group_check=True)))
            P.mmgroup(kws, reads=["hTb", "fsr"], writes=[("ps", 0)])
            mb = b % 2
            P.op("dve", "tensor_scalar", reads=[("Btok", 0), "cst"], writes=[("Bmk", mb)], out=Bmk[0:Lc, mb, :], in0=Btok[0:Lc, 0, :], scalar1=SEQM[0:Lc, b:b + 1], scalar2=None, op0=ALU.mult)
            pb0 = 4 + 2 * mb
            for g in range(4):
                P.mmgroup([MM(ps[pb0 + g // 2][:, (g % 2) * 256:(g % 2 + 1) * 256], Bmk[0:Lc, mb, g * 128:(g + 1) * 128], xdt[0:Lc, 1, g * 256:(g + 1) * 256])],
                          reads=[("Bmk", mb), ("xdt", 1)], writes=[("ps", pb0 + g // 2)])
            P.op("dve", "tensor_tensor", reads=["hT", "cdrS"], writes=["htmp"], out=htmp[:, :].rearrange("p (h q) -> p h q", q=64), in0=hT[:, :].rearrange("p (h q) -> p h q", q=64),
                 in1=cdrS[:, b, :].unsqueeze(2).broadcast_to([128, 16, 64]), op=ALU.mult)
            for bk in range(2):
                P.op("dve", "tensor_tensor", reads=["htmp", ("ps", pb0 + bk)], writes=["htmp"], out=htmp[:, bk * 512:(bk + 1) * 512], in0=ps[pb0 + bk][:, :], in1=htmp[:, bk * 512:(bk + 1) * 512], op=ALU.add)
            P.dma("sp", ssmT_o[l, 1 + b], htmp[:], reads=["htmp"], writes=[("ssmT_o", l, 1 + b)])
        kws = []
        for j in range(8):
            for hh in range(2):
                h = 2 * j + hh
                outp = ps[0][64 * hh:64 * hh + 64, j * 64:j * 64 + Lc]
                kws.append(("matmul", dict(out=outp, lhsT=xdt[0:Lc, 0, h * 64:(h + 1) * 64], rhs=Dm[0:Lc, 0, h, 0:Lc], start=False, stop=True, skip_group_check=True)))
        P.mmgroup(kws, reads=[("xdt", 0), "Dm"], writes=[("ps", 0)])
        gating(l, 0, Lc)

    def s3_ssd(l, kind, first, last, nseq, L, TB):
        P.dma("pool", wsmall[:, 1, 0:128].rearrange("p (c n) -> p c n", n=16), kcp(WSRC["w_in"][l, :, C_DT:C_DT + 16]), reads=WRK(l), writes=[("wsmall", 1)])
        rot = 0
        for half in range(2):
            for q in range(4):
                zc = half * 4 + q
                if zc % 2 == 0:
                    wz, kz = wload(kcp(WSRC["w_in"][l, :, C_Z + zc * 128:C_Z + (zc + 2) * 128]), KC, 256, WRK(l))
                b_ = rot % 4
                rot += 1
                P.mmgroup([MM(ps[b_][:, 0:TB], wz[:, c, (zc % 2) * 128:(zc % 2 + 1) * 128], hb[:, c, 0:TB], c == 0, c == KC - 1) for c in range(KC)], reads=kz + ["hb"], writes=[("ps", b_)])
                P.op("act", "activation", reads=[("ps", b_)], writes=["siluz"], out=siluz[:, zc, 0:TB], in_=ps[b_][:, 0:TB], func=AF.Silu)
        HL = 3 + L
        if kind == "p":
            if first:
                P.op("dve", "memset", writes=["hbh"], ap=hbh[:], constant=0.0)
            for c2 in range(0, 16, 2):
                wx, kx = wload(kcp(WSRC["w_in"][l, :, C_XBC + c2 * 128:C_XBC + (c2 + 2) * 128]), KC, 256, WRK(l))
                pvs, accs, aks, bks = [], [], [], []
                for q in range(2):
                    cc = c2 + q
                    b_ = rot % 4
                    rot += 1
                    P.mmgroup([MM(ps[b_][:, 0:3], wx[:, c, q * 128:(q + 1) * 128], hbh[:, c, 0:3], c == 0, c == KC - 1) for c in range(KC)]
                              + [MM(ps[b_][:, 3:3 + L], wx[:, c, q * 128:(q + 1) * 128], hb[:, c, 0:L], c == 0, c == KC - 1) for c in range(KC)],
                              reads=kx + ["hb", "hbh"], writes=[("ps", b_)])
                    pvs.append(ps[b_])
                    bks.append(b_)
                    accs.append(cacc[:, q, 0:L])
                    aks.append(("cacc", q))
                for q in range(2):
                    cc = c2 + q
                    P.op("dve", "tensor_scalar", reads=[("ps", bks[q]), "vecs"], writes=[aks[q]], out=accs[q], in0=pvs[q][:, 3:3 + L], scalar1=vcol(l, V_CW + 3 * 16 + cc),
                         scalar2=vcol(l, V_CB + cc), op0=ALU.mult, op1=ALU.add)
                for j in range(3):
                    for q in range(2):
                        cc = c2 + q
                        P.op("dve", "scalar_tensor_tensor", reads=[("ps", bks[q]), aks[q], "vecs"], writes=[aks[q]], out=accs[q], in0=pvs[q][:, j:j + L], scalar=vcol(l, V_CW + j * 16 + cc),
                             in1=accs[q], op0=ALU.mult, op1=ALU.add)
                for q in range(2):
                    cc = c2 + q
                    if last:
                        nhal = cbuf[:, q, 0:3]
                        P.op("act", "activation", reads=[("ps", bks[q])], writes=[("nhalo", q)], out=nhal, in_=pvs[q][:, L:L + 3], func=AF.Copy)
                        P.dma("sp", conv_o[l, cc * 128:(cc + 1) * 128, 0, :], nhal, reads=[("nhalo", q)], writes=[("conv_o", l, cc, 0)])
                    if cc < 8:
                        dst, dk = xsT[:, cc, 0:TB], "xsT"
                    elif cc < 12:
                        dst, dk = BT[:, cc - 8, 0:TB], "BT"
                    else:
                        dst, dk = CT[:, cc - 12, 0:TB], "CT"
                    P.op("act", "activation", reads=[aks[q]], writes=[dk], out=dst, in_=accs[q], func=AF.Silu)
            P.op("dve", "tensor_copy", reads=["hb"], writes=["hbh"], out=hbh[:, :, 0:3], in_=hb[:, :, L - 3:L])
        else:
            for cc in range(16):
                if cc % 2 == 0:
                    wx, kx = wload(kcp(WSRC["w_in"][l, :, C_XBC + cc * 128:C_XBC + (cc + 2) * 128]), KC, 256, WRK(l))
                b_ = rot % 4
                rot += 1
                par = cc % 2
                hk = ("halo", par)
                P.mmgroup([MM(ps[b_][:, 0:TB], wx[:, c, (cc % 2) * 128:(cc % 2 + 1) * 128], hb[:, c, 0:TB], c == 0, c == KC - 1) for c in range(KC)], reads=kx + ["hb"], writes=[("ps", b_)])
                pv3 = v3(ps[b_][:, 0:TB], nseq, L)
                hal = v3(cbuf[:, par, 0:nseq * 3], nseq, 3)
                if kind == "p":
                    if first:
                        P.op("dve", "memset", writes=[hk], ap=hal, constant=0.0)
                    else:
                        P.op("dve", "tensor_copy", reads=[("halp", cc)], writes=[hk], out=hal[:, 0, :], in_=halp[:, cc, :])
                else:
                    P.dma("sp", hal, conv_in[l, cc * 128:(cc + 1) * 128, :, :], writes=[hk])
                acc = v3(cacc[:, par, 0:TB], nseq, L)
                ak = ("cacc", par)
                P.op("dve", "tensor_scalar", reads=[("ps", b_), "vecs"], writes=[ak], out=acc, in0=pv3, scalar1=vcol(l, V_CW + 3 * 16 + cc), scalar2=vcol(l, V_CB + cc), op0=ALU.mult, op1=ALU.add)
                for j in range(3):
                    sh = 3 - j
                    if L > sh:
                        P.op("dve", "scalar_tensor_tensor", reads=[("ps", b_), ak, "vecs"], writes=[ak], out=acc[:, :, sh:L], in0=pv3[:, :, 0:L - sh], scalar=vcol(l, V_CW + j * 16 + cc),
                             in1=acc[:, :, sh:L], op0=ALU.mult, op1=ALU.add)
                    n_h = min(sh, L)
                    P.op("dve", "scalar_tensor_tensor", reads=[hk, ak, "vecs"], writes=[ak], out=acc[:, :, 0:n_h], in0=hal[:, :, j:j + n_h], scalar=vcol(l, V_CW + j * 16 + cc),
                         in1=acc[:, :, 0:n_h], op0=ALU.mult, op1=ALU.add)
                nhal = v3(cbuf[:, par, nseq * 3:nseq * 6], nseq, 3)
                nk_ = ("nhalo", par)
                if L >= 3:
                    P.op("act", "activation", reads=[("ps", b_)], writes=[nk_], out=nhal, in_=pv3[:, :, L - 3:L], func=AF.Copy)
                else:
                    P.op("act", "activation", reads=[hk], writes=[nk_], out=nhal[:, :, 0:3 - L], in_=hal[:, :, L:3], func=AF.Copy)
                    P.op("act", "activation", reads=[("ps", b_), nk_], writes=[nk_], out=nhal[:, :, 3 - L:3], in_=pv3, func=AF.Copy)
                if kind == "p":
                    P.op("dve", "tensor_copy", reads=[nk_], writes=[("halp", cc)], out=halp[:, cc, :], in_=nhal[:, 0, :])
                    if last:
                        P.dma("sp", conv_o[l, cc * 128:(cc + 1) * 128, 0, :], nhal[:, 0, :], reads=[nk_], writes=[("conv_o", l, cc, 0)])
                else:
                    P.dma("sp", conv_o[l, cc * 128:(cc + 1) * 128, 1:NS + 1, :], nhal, reads=[nk_], writes=[("conv_o", l, cc, 1)])
                if cc < 8:
                    dst, dk = xsT[:, cc, 0:TB], "xsT"
                elif cc < 12:
                    dst, dk = BT[:, cc - 8, 0:TB], "BT"
                else:
                    dst, dk = CT[:, cc - 12, 0:TB], "CT"
                P.op("act", "activation", reads=[ak], writes=[dk], out=dst, in_=cacc[:, par, 0:TB], func=AF.Silu)

        if kind == "p":
            nch = TB // 128
            for ci in range(nch):
                chunk_A(l, ci * 128, 128, ci % 2, TRI, ONESF, False)
            for ci in range(nch):
                chunk_B_prompt(l, ci * 128, 128, ci % 2, "zero" if (first and ci == 0) else None, 0 if (last and ci == nch - 1) else None)
        else:
            chunk_A(l, 0, NS * LS, 0, TRIBD, SAMEBD, True)
            chunk_B_sample(l)

    def s4_merge(l, t0, TB):
        brs = [(ypool, "ypool", 4, "w_br_pool"), (yssm, "yssm", 8, "w_br_ssm"), (ymem, "ymem", 4, "w_br_mem")]
        wcache = {}
        for dc in range(KC):
            par = dc % 2
            for br, (yb, ykey, nk, wname) in enumerate(brs):
                if dc % 2 == 0:
                    wcache[("g", br)] = wload(kcp(WSRC["w_in"][l, :, C_G + br * 1024 + dc * 128:C_G + br * 1024 + (dc + 2) * 128]), KC, 256, WRK(l))
                    wcache[("b", br)] = wload(kcp(WSRC[wname][l, :, dc * 128:(dc + 2) * 128]), nk, 256, WRK(l))
                wg, kg = wcache[("g", br)]
                wb, kb = wcache[("b", br)]
                cs = slice((dc % 2) * 128, (dc % 2 + 1) * 128)
                P.mmgroup([MM(ps[br][:, 0:TB], wg[:, c, cs], hb[:, c, 0:TB], c == 0, c == KC - 1) for c in range(KC)], reads=kg + ["hb"], writes=[("ps", br)])
                P.op("act", "activation", reads=[("ps", br), "vecs"], writes=[("gsb", par, br)], out=gsb[:, par, br, 0:TB], in_=ps[br][:, 0:TB], func=AF.Sigmoid,
                     bias=vcol(l, V_GB + br * 8 + dc))
                P.mmgroup([MM(ps[3 + br][:, 0:TB], wb[:, k, cs], yb[:, k, 0:TB], k == 0, k == nk - 1) for k in range(nk)], reads=kb + [ykey], writes=[("ps", 3 + br)])
            for br in range(3):
                P.op("dve", "tensor_tensor", reads=[("ps", 3 + br), ("gsb", par, br)], writes=[("mt", par, br)], out=mt[:, par, br, 0:TB], in0=ps[3 + br][:, 0:TB],
                     in1=gsb[:, par, br, 0:TB], op=ALU.mult)
            P.op("dve", "tensor_tensor", reads=[("mt", par, 0), ("mt", par, 1)], writes=[("mt", par, 0)], out=mt[:, par, 0, 0:TB], in0=mt[:, par, 0, 0:TB], in1=mt[:, par, 1, 0:TB], op=ALU.add)
            P.op("dve", "tensor_tensor", reads=[("mt", par, 0), ("mt", par, 2)], writes=[("mrgb", dc)], out=mrgb[:, dc, 0:TB], in0=mt[:, par, 0, 0:TB], in1=mt[:, par, 2, 0:TB], op=ALU.add)
        for dc in range(KC):
            if dc % 2 == 0:
                wo, ko = wload(kcp(WSRC["w_o"][l, :, dc * 128:(dc + 2) * 128]), KC, 256, WRK(l))
            po = 6 + dc % 2
            cs = slice((dc % 2) * 128, (dc % 2 + 1) * 128)
            P.mmgroup([MM(ps[po][:, 0:TB], wo[:, k, cs], mrgb[:, k, 0:TB], k == 0, k == KC - 1) for k in range(KC)], reads=ko + ["mrgb"], writes=[("ps", po)])
            P.op("dve", "tensor_tensor", reads=[("ps", po), ("x", "m", dc)], writes=[("x", "m", dc)], out=x[:, dc, t0:t0 + TB], in0=ps[po][:, 0:TB], in1=x[:, dc, t0:t0 + TB], op=ALU.add)

    def mixer(l):
        ON = lambda n: ("mix" in PHASES) or (n in PHASES)
        P.mark(f"L{l} mem")
        if ON("mem"):
            memory_stage(l)
        blocks = [("p", t, MB) for t in range(0, LP, MB)] + [("s", LP, NS * LS)]
        for bi, (kind, t0, TB) in enumerate(blocks):
            nseq, L = (1, TB) if kind == "p" else (NS, LS)
            first = (bi == 0)
            last = (kind == "p" and t0 + TB == LP)
            if kind == "s" or bi == 0:
                P.barrier(["pe", "act", "dve", "sp"] + (["pool"] if kind == "s" else []))
            rmsnorm([hb[:, c, 0:TB] for c in range(KC)], lambda c: "hb", [x[:, c, t0:t0 + TB] for c in range(KC)], lambda c: ("x", "m", c), TB, vcol(l, V_MIX, 8), sqM)
            P.mark(f"L{l} b{bi} s1")
            if ON("s1"):
                s1_pool(l, kind, first, last, nseq, L, TB)
            P.mark(f"L{l} b{bi} s2")
            if ON("s2"):
                s2_attn(l, kind, nseq, L, TB)
            P.barrier(["pe", "act", "dve", "sp", "pool"] if kind == "s" else ["pe", "act", "dve", "sp"])
            P.mark(f"L{l} b{bi} s3")
            if ON("s3"):
                s3_ssd(l, kind, first, last, nseq, L, TB)
            P.barrier(["pe", "act", "dve", "sp"])
            P.mark(f"L{l} b{bi} s4")
            if ON("s4"):
                s4_merge(l, t0, TB)

    for l in range(DEPTH):
        P.mark(f"L{l} ffn1")
        if "ffn1" in PHASES:
            ffn(l, W["ffn1_w_gate"], W["ffn1_w_up"], W["ffn1_w_down"], vcol(l, V_F1, 8), precast_pieces(l) if USE_PRECAST else ())
        P.barrier(["pe", "act", "dve", "sp"])
        if any(p_ in PHASES for p_ in ("mix", "mem", "s1", "s2", "s3", "s4")):
            mixer(l)
        P.barrier(["pe", "act", "dve", "sp"])
        P.mark(f"L{l} ffn2")
        if "ffn2" in PHASES:
            ffn(l, W["ffn2_w_gate"], W["ffn2_w_up"], W["ffn2_w_down"], vcol(l, V_F2, 8))
        P.barrier(["pe", "act", "dve", "sp"])

    P.force = True
    P.mark('final')
    for (t0, tb) in fblocks:
        for c in range(KC):
            P.op("act", "activation", reads=[("x", t0, c)], writes=[("sq", c)], out=sqF[:, c, 0:tb], in_=x[:, c, t0:t0 + tb], func=AF.Square)
        P.mmgroup([MM(ps[7][:, 0:tb], onesb[:], sqF[:, c, 0:tb], c == 0, c == KC - 1) for c in range(KC)], reads=["sq", "onesb"], writes=[("ps", 7)])
        P.op("act", "activation", reads=[("ps", 7), "epsc"], writes=["rstd"], out=rstd[:, 0:tb], in_=ps[7][:, 0:tb], func=AF.Sqrt, bias=EPSC[:], scale=1.0 / D)
        P.op("dve", "reciprocal", reads=["rstd"], writes=["rstd"], out=rstd[:, 0:tb], in_=rstd[:, 0:tb])
        for c in range(KC):
            k = c % 2
            P.op("dve", "scalar_tensor_tensor", reads=[("x", t0, c), "rstd", "vecs"], writes=[("yout", k)], out=youtA[:, k, 0:tb], in0=x[:, c, t0:t0 + tb],
                 scalar=vecs[:, V_FIN + c:V_FIN + c + 1], in1=rstd[:, 0:tb], op0=ALU.mult, op1=ALU.mult)
            P.dma("sp", yT_o[c * 128:(c + 1) * 128, t0:t0 + tb], youtA[:, k, 0:tb], reads=[("yout", k)], writes=[("yT_o", t0, c)])
    P.barrier(["sp"])
    with nc.Block() as block:
        P.emit(block)
    es.close()
    return nc, P


def _consts():
    c = np.zeros((128, 720), np.float32)
    c[:, 0:128] = np.triu(np.ones((128, 128), np.float32))
    c[:, 128:256] = np.eye(128, dtype=np.float32)
    for g in range(4):
        w = 2 ** (g + 1)
        for t in range(16):
            c[:, 256 + g * 16 + t] = w / min(t + 1, w)
    same = np.kron(np.eye(16, dtype=np.float32), np.ones((4, 4), np.float32))
    c[0:64, 320:384] = same * np.triu(np.ones((64, 64), np.float32))
    c[0:64, 384:448] = same
    c[:, 448:576] = 1.0
    c[0:64, 576:592] = np.kron(np.eye(16, dtype=np.float32), np.ones((4, 1), np.float32))
    return c


def _cols(v):
    return np.ascontiguousarray(np.asarray(v, np.float32).reshape(-1, 128).T)


def _pack_vecs(inp, depth):
    vecs = np.zeros((128, NVL * depth + 8), np.float32)
    rows = np.zeros((128, depth * 32), np.float32)
    for l in range(depth):
        o = l * NVL
        vecs[:, o + 0:o + 8] = _cols(inp["ffn1_norm"][l])
        vecs[:, o + 8:o + 16] = _cols(inp["mix_norm"][l])
        vecs[:, o + 16:o + 24] = _cols(inp["ffn2_norm"][l])
        vecs[:, o + 24:o + 48] = _cols(inp["gate_bias"][l])
        vecs[:, o + 48:o + 52] = _cols(inp["pool_scale"][l])
        for j in range(4):
            vecs[:, o + 52 + j * 16:o + 52 + (j + 1) * 16] = _cols(inp["conv_w"][l, j])
        vecs[:, o + 116:o + 132] = _cols(inp["conv_b"][l])
        vecs[:, o + 132:o + 140] = _cols(np.repeat(np.asarray(inp["d_skip"][l]), 64))
        vecs[:, o + 140:o + 148] = _cols(inp["ssm_norm"][l])
        vecs[:, o + 148:o + 156] = _cols(inp["mem_norm"][l])
        rows[:, l * 32:l * 32 + 16] = np.asarray(inp["dt_bias"][l])[None, :]
        rows[:, l * 32 + 16:l * 32 + 32] = np.asarray(inp["a_log"][l])[None, :]
    vecs[:, NVL * depth:NVL * depth + 8] = _cols(inp["final_norm"])
    return vecs, rows


_WNAMES = ("ffn1_w_gate", "ffn1_w_up", "ffn1_w_down", "ffn2_w_gate", "ffn2_w_up", "ffn2_w_down", "w_in", "pool_w",
           "w_mem_k", "w_mem_v", "w_br_pool", "w_br_ssm", "w_br_mem", "w_o")


def run_cores(inp, n_cores, LP, depth, phases=("ffn1", "mix", "ffn2"), runner=None):
    f32 = lambda a: np.ascontiguousarray(np.asarray(a, np.float32))
    nc, _ = build_program(LP, depth, PHASES=phases)
    vecs, rows = _pack_vecs(inp, depth)
    cst = _consts()
    shared = {nm: f32(inp[nm]) for nm in _WNAMES}
    in_maps = []
    for i in range(n_cores):
        sl = slice(i * NS, (i + 1) * NS)
        xs = f32(inp["x_sample"][sl]).reshape(NS * LS, D)
        m = dict(shared)
        m["xT"] = f32(np.concatenate([f32(inp["x_prompt"][i]), xs], axis=0).T)
        m["memT"] = f32(f32(inp["mem_prompt"][i]).T)
        m["pool_in"] = f32(np.transpose(f32(inp["state_pool"][:, sl]), (0, 3, 1, 2)))
        m["conv_in"] = f32(np.transpose(f32(inp["state_conv"][:, sl]), (0, 3, 1, 2)))
        m["ssmT_in"] = f32(np.transpose(f32(inp["state_ssm"][:, sl]).reshape(depth, NS, 1024, 128), (0, 1, 3, 2)))
        m["kT_in"] = f32(np.transpose(f32(inp["cache_mem_k"][:, sl]), (0, 1, 3, 4, 2)))
        m["v_in"] = f32(f32(inp["cache_mem_v"][:, sl]).reshape(depth, NS, 256, 512))
        m["vecs"] = vecs
        m["rows"] = rows
        m["consts"] = cst
        in_maps.append(m)
    if runner is None:
        res = bass_utils.run_bass_kernel_spmd(nc, in_maps, core_ids=list(range(n_cores))).results
    else:
        res = runner(nc, in_maps)
    B = n_cores
    y_p = np.stack([res[i]["yT"][:, :LP].T for i in range(B)])
    y_s = np.concatenate([res[i]["yT"][:, LP:].T.reshape(NS, LS, D) for i in range(B)])
    po = np.stack([res[i]["pool_o"] for i in range(B)])
    co = np.stack([res[i]["conv_o"] for i in range(B)])
    so = np.stack([res[i]["ssmT_o"] for i in range(B)])
    pool_p = np.transpose(po[:, :, :, 0, :], (1, 0, 3, 2))
    conv_p = np.transpose(co[:, :, :, 0, :], (1, 0, 3, 2))
    ssm_p = np.transpose(so[:, :, 0], (1, 0, 3, 2)).reshape(depth, B, 16, 64, 128)
    mk = np.stack([np.transpose(res[i]["mkT_o"], (0, 2, 1)).reshape(depth, 256, 4, 128) for i in range(B)], axis=1)
    mv = np.stack([res[i]["mv_o"].reshape(depth, 256, 4, 128) for i in range(B)], axis=1)
    pool_s = np.transpose(po[:, :, :, 1:, :], (1, 0, 3, 4, 2)).reshape(depth, B * NS, 15, 512)
    conv_s = np.transpose(co[:, :, :, 1:, :], (1, 0, 3, 4, 2)).reshape(depth, B * NS, 3, 2048)
    ssm_s = np.transpose(so[:, :, 1:], (1, 0, 2, 4, 3)).reshape(depth, B * NS, 16, 64, 128)
    outs = (y_p, y_s, pool_p, conv_p, ssm_p, mk, mv, pool_s, conv_s, ssm_s)
    return tuple(np.ascontiguousarray(o, dtype=np.float32) for o in outs)


def kernel(**inputs):
    return run_cores(inputs, 8, 2048, 4)
```

```python
import math
from contextlib import ExitStack
import numpy as np
import concourse.bass as bass
import concourse.mybir as mybir
import concourse.bass_utils as bass_utils
from concourse.alu_op_type import AluOpType as ALU

F32 = mybir.dt.float32
BF16 = mybir.dt.bfloat16
AF = mybir.ActivationFunctionType
AX = mybir.AxisListType

D = 1024
KC = 8
DFF = 2816
NFC = 22
NS = 16
LS = 4
EPS = 1e-6
INC = 7184
C_POOL, C_Z, C_XBC, C_DT, C_Q, C_G = 0, 512, 1536, 3584, 3600, 4112
NVL = 160
SCALE = 128 ** -0.5
import os as _os
USE_PRECAST = bool(int(_os.environ.get('K_PRECAST', '1')))
SKIP_SELF = bool(int(_os.environ.get('K_SKIP_SELF', '0')))


class Prog:
    ENGS = ("pe", "act", "dve", "pool", "sp")

    def __init__(self, nc, es):
        self.nc = nc
        self.ops = {e: [] for e in self.ENGS}
        self.sem = {e: es.enter_context(nc.semaphore("sem_" + e)) for e in self.ENGS}
        self.cnt = {e: 0 for e in self.ENGS}
        self.RING = 20
        self.dsem = {q: [es.enter_context(nc.semaphore(f"dsem_{q}{i}")) for i in range(self.RING)]
                     for q in ("pool", "sp", "act")}
        self.dval = {q: [0] * self.RING for q in self.dsem}
        self.dn = {q: 0 for q in self.dsem}
        self.semobj = {}
        self.seen = {e: {} for e in self.ENGS}
        self.state = {}
        for e in self.ENGS:
            self.semobj[id(self.sem[e])] = self.sem[e]
        for q in self.dsem:
            for s in self.dsem[q]:
                self.semobj[id(s)] = s
        self.all_tokens = {}
        import os as _os
        self.limit = int(_os.environ.get('K_LIMIT', '0'))
        self.nops = 0
        self.force = False
        self.log = []
        self.marks = []

    @staticmethod
    def _k(key):
        if isinstance(key, tuple):
            return key[0], key[1:]
        return key, "*"

    def _deps(self, key, is_write, toks, eng=None):
        name, sub = self._k(key)
        st = self.state.setdefault(name, {})
        subs = list(st.keys()) if sub == "*" else [s for s in (sub, "*") if s in st]
        for s in subs:
            e = st[s]
            if e["w"] is not None:
                toks.append(e["w"])
            if is_write:
                toks.extend(e["r"].items())
            elif name == "ps":
                own = id(self.sem[eng]) if eng in self.sem else None
                toks.extend((sid, v) for sid, v in e["r"].items() if sid != own)

    def _rec(self, key, is_write, tok):
        name, sub = self._k(key)
        st = self.state.setdefault(name, {})
        if is_write:
            if sub == "*":
                st.clear()
            st[sub] = {"w": tok, "r": {}}
        else:
            e = st.setdefault(sub, {"w": None, "r": {}})
            if e["r"].get(tok[0], 0) < tok[1]:
                e["r"][tok[0]] = tok[1]

    def _waits(self, eng, toks, skip_self):
        own = id(self.sem[eng])
        need = {}
        for sid, val in toks:
            if skip_self and sid == own:
                continue
            if need.get(sid, 0) < val:
                need[sid] = val
        for sid, val in need.items():
            if self.seen[eng].get(sid, 0) < val:
                self.seen[eng][sid] = val
                s = self.semobj[sid]
                self.ops[eng].append(lambda e, s=s, val=val: e.wait_ge(s, val))

    def op(self, eng, method, reads=(), writes=(), **kw):
        self.nops += 1
        if self.limit and self.nops > self.limit and not self.force:
            return
        self.log.append((self.nops, eng, method, [str(k) for k in writes]))
        toks = []
        ENG = eng
        for k in reads:
            self._deps(k, False, toks, ENG)
        for k in writes:
            self._deps(k, True, toks, ENG)
        self._waits(eng, toks, skip_self=(eng == "pe" or SKIP_SELF))
        self.cnt[eng] += 1
        tok = (id(self.sem[eng]), self.cnt[eng])
        s = self.sem[eng]
        self.ops[eng].append(lambda e, method=method, kw=kw, s=s: getattr(e, method)(**kw).then_inc(s, 1))
        for k in reads:
            self._rec(k, False, tok)
        for k in writes:
            self._rec(k, True, tok)
        self.all_tokens[tok[0]] = tok[1]

    def mmgroup(self, kws, reads, writes):
        self.nops += 1
        if self.limit and self.nops > self.limit and not self.force:
            return
        self.log.append((self.nops, 'pe', kws[0][0], [str(k) for k in writes]))
        toks = []
        ENG = "pe"
        for k in reads:
            self._deps(k, False, toks, ENG)
        for k in writes:
            self._deps(k, True, toks, ENG)
        self._waits("pe", toks, skip_self=True)
        self.cnt["pe"] += 1
        tok = (id(self.sem["pe"]), self.cnt["pe"])
        s = self.sem["pe"]
        self.npe_real = getattr(self, "npe_real", 0) + len(kws)
        for i, (method, kw) in enumerate(kws):
            if i == len(kws) - 1:
                self.ops["pe"].append(lambda e, method=method, kw=kw, s=s: getattr(e, method)(**kw).then_inc(s, 1))
            else:
                self.ops["pe"].append(lambda e, method=method, kw=kw: getattr(e, method)(**kw))
        for k in reads:
            self._rec(k, False, tok)
        for k in writes:
            self._rec(k, True, tok)
        self.all_tokens[tok[0]] = tok[1]

    def dma(self, q, out, in_, reads=(), writes=()):
        self.nops += 1
        if self.limit and self.nops > self.limit and not self.force:
            return
        self.log.append((self.nops, q, 'dma', [str(k) for k in writes]))
        toks = []
        ENG = q
        for k in reads:
            self._deps(k, False, toks, ENG)
        for k in writes:
            self._deps(k, True, toks, ENG)
        i = self.dn[q] % self.RING
        self.dn[q] += 1
        s = self.dsem[q][i]
        if self.dval[q][i] > 0:
            toks.append((id(s), self.dval[q][i]))
        self._waits(q, toks, skip_self=False)
        self.dval[q][i] += 16
        tok = (id(s), self.dval[q][i])
        self.ops[q].append(lambda e, out=out, in_=in_, s=s: e.dma_start(out=out, in_=in_).then_inc(s, 16))
        for k in reads:
            self._rec(k, False, tok)
        for k in writes:
            self._rec(k, True, tok)
        self.all_tokens[tok[0]] = tok[1]

    def mark(self, label):
        self.marks.append((label, getattr(self, "npe_real", 0)))

    def barrier(self, engines=None):
        toks = list(self.all_tokens.items())
        for e in (engines or self.ENGS):
            self._waits(e, toks, skip_self=True)

    def emit(self, block):
        nc = self.nc
        m = {"pe": block.tensor, "act": block.scalar, "dve": block.vector, "pool": block.gpsimd, "sp": block.sync}
        for name in self.ENGS:
            lst = self.ops[name]

            def body(e, lst=lst):
                for f in lst:
                    f(e)
            m[name](body)


def build_program(LP, DEPTH, MB=256, PHASES=("ffn1", "mix", "ffn2")):
    T = LP + NS * LS
    nc = bass.Bass("TRN2", target_bir_lowering=False)
    dt_ = lambda name, shape, kind="ExternalInput", dt=F32: nc.dram_tensor(name, shape, dt, kind=kind).ap()
    xT_d = dt_("xT", [D, T])
    memT_d = dt_("memT", [D, 256])
    pool_in = dt_("pool_in", [DEPTH, 512, NS, 15])
    conv_in = dt_("conv_in", [DEPTH, 2048, NS, 3])
    ssmT_in = dt_("ssmT_in", [DEPTH, NS, 128, 1024])
    kT_in = dt_("kT_in", [DEPTH, NS, 4, 128, 256])
    v_in = dt_("v_in", [DEPTH, NS, 256, 512])
    vecs_d = dt_("vecs", [128, NVL * DEPTH + 8])
    rows_d = dt_("rows", [128, DEPTH * 32])
    consts_d = dt_("consts", [128, 720])
    W = {}
    for nm, shp in (("ffn1_w_gate", [DEPTH, D, DFF]), ("ffn1_w_up", [DEPTH, D, DFF]), ("ffn1_w_down", [DEPTH, DFF, D]),
                    ("ffn2_w_gate", [DEPTH, D, DFF]), ("ffn2_w_up", [DEPTH, D, DFF]), ("ffn2_w_down", [DEPTH, DFF, D]),
                    ("w_in", [DEPTH, D, INC]), ("pool_w", [DEPTH, 4, 128, 128]),
                    ("w_mem_k", [DEPTH, D, 512]), ("w_mem_v", [DEPTH, D, 512]),
                    ("w_br_pool", [DEPTH, 512, D]), ("w_br_ssm", [DEPTH, 1024, D]), ("w_br_mem", [DEPTH, 512, D]),
                    ("w_o", [DEPTH, D, D])):
        W[nm] = dt_(nm, shp)
    WB = {}
    for nm, shp in (("w_in", [DEPTH, D, INC]), ("pool_w", [DEPTH, 4, 128, 128]), ("w_br_pool", [DEPTH, 512, D]), ("w_br_ssm", [DEPTH, 1024, D]),
                    ("w_br_mem", [DEPTH, 512, D]), ("w_o", [DEPTH, D, D])):
        WB[nm] = nc.dram_tensor("wb_" + nm, shp, BF16, kind="Internal").ap()
    WD = nc.dram_tensor("wd_conv", [DEPTH, 8, 128, 8, 128], BF16, kind="Internal").ap()
    yT_o = dt_("yT", [D, T], "ExternalOutput")
    pool_o = dt_("pool_o", [DEPTH, 512, NS + 1, 15], "ExternalOutput")
    conv_o = dt_("conv_o", [DEPTH, 2048, NS + 1, 3], "ExternalOutput")
    ssmT_o = dt_("ssmT_o", [DEPTH, NS + 1, 128, 1024], "ExternalOutput")
    mkT_o = dt_("mkT_o", [DEPTH, 512, 256], "ExternalOutput")
    mv_o = dt_("mv_o", [DEPTH, 256, 512], "ExternalOutput")

    es = ExitStack()
    sb = lambda name, shape, dt=F32: es.enter_context(nc.sbuf_tensor("s_" + name, shape, dt))
    P = Prog(nc, es)
    x = sb("x", [128, KC, T])
    vecs = sb("vecs", [128, NVL * DEPTH + 8])
    rows = sb("rows", [128, DEPTH * 32])
    cst = sb("cst", [128, 720])
    identb = sb("identb", [128, 128], BF16)
    onesb = sb("onesb", [128, 128], BF16)
    EPSC = sb("epsc", [128, 1])
    arow = sb("arow", [128, 16])
    NPAGE = 19
    PAGE = 1024
    wsl = sb("wsl", [128, NPAGE * PAGE], BF16)
    wsmall = sb("wsmall", [128, 2, 640], BF16)
    rstd = sb("rstd", [128, 512])
    KTp = sb("KTp", [128, 4, 256], BF16)
    Vp = sb("Vp", [128, 2, 512], BF16)
    halp = sb("halp", [128, 16, 3])
    hbh = sb("hbh", [128, KC, 4], BF16)
    hT = sb("hT", [128, 1024])
    hTb = sb("hTb", [128, 1024], BF16)
    UBW = max(15 + MB, NS * (15 + LS))
    ub = sb("ub", [128, 4, UBW])
    ARENA_B = 76 * 1024
    arena = sb("arena", [128, ARENA_B // 4])
    ps = [es.enter_context(nc.psum_tensor(f"ps{i}", [128, 512], F32)) for i in range(8)]
    psb = [p_[:].bitcast(BF16) for p_ in ps]

    def carve(off, shape, dt=F32):
        n = 1
        for s_ in shape[1:]:
            n *= s_
        nb = n * (4 if dt == F32 else 2)
        assert off % 4 == 0 and off + nb <= ARENA_B, (off, nb)
        v = arena[:, off // 4: (off + nb + 3) // 4]
        if dt != F32:
            v = v.bitcast(dt)[:, 0:n]
        if len(shape) == 3:
            v = v.rearrange("p (a b) -> p a b", b=shape[2])
        elif len(shape) == 4:
            v = v.rearrange("p (a b c) -> p a b c", b=shape[2], c=shape[3])
        return v, off + ((nb + 3) // 4) * 4

    o = 0
    xn, o = carve(o, [128, KC, T], BF16)
    sqF, o = carve(o, [128, KC, 512], BF16)
    hTf, o = carve(o, [128, 2, 4, 512], BF16)
    sg, o = carve(o, [128, 2, 512])
    youtA, o = carve(o, [128, 2, 512])
    o = 0
    hb, o = carve(o, [128, KC, MB], BF16)
    ypool, o = carve(o, [128, 4, MB], BF16)
    ymem, o = carve(o, [128, 4, MB], BF16)
    yssm, o = carve(o, [128, KC, MB], BF16)
    o_stage = o
    sqM, o = carve(o, [128, KC, MB], BF16)
    pa, o = carve(o, [128, UBW])
    pb_, o = carve(o, [128, UBW])
    dpl, o = carve(o, [128, 4, MB], BF16)
    qT, o = carve(o, [128, 4, MB], BF16)
    KTs, o = carve(o, [128, 2, 4, 256], BF16)
    Vs, o = carve(o, [128, 2, 2, 512], BF16)
    att_mx, o = carve(o, [128, 2, 4])
    att_pe, o = carve(o, [128, 2, 256])
    att_pb, o = carve(o, [128, 2, 256], BF16)
    att_pT, o = carve(o, [128, 2, 2, 128], BF16)
    Qexp, o = carve(o, [128, 4, 1088], BF16)
    att_pT4, o = carve(o, [128, 4, 2, 64], BF16)
    gsb, o = carve(o, [128, 2, 3, MB])
    mt, o = carve(o, [128, 2, 3, MB])
    mrgb, o = carve(o, [128, KC, MB], BF16)
    o_mem = o
    o = o_stage
    memx, o = carve(o, [128, KC, 256])
    memn, o = carve(o, [128, KC, 256], BF16)
    sqX, o = carve(o, [128, KC, 256], BF16)
    kst, o = carve(o, [128, 4, 256])
    vst, o = carve(o, [128, 2, 512])
    dgt, o = carve(o, [128, 2, 8, 128], BF16)
    o = o_stage
    siluz, o = carve(o, [128, KC, MB])
    cbuf, o = carve(o, [128, 2, NS * (3 + LS) if NS * (3 + LS) > 3 + MB else 3 + MB])
    cacc, o = carve(o, [128, 2, MB])
    xpre, o = carve(o, [128, 2, 4 + MB], BF16)
    xsT, o = carve(o, [128, KC, MB])
    BT, o = carve(o, [128, 4, MB], BF16)
    CT, o = carve(o, [128, 4, MB], BF16)
    dtb, o = carve(o, [128, 2, 16])
    lab, o = carve(o, [128, 2, 16])
    acum, o = carve(o, [128, 2, 16])
    nacum, o = carve(o, [128, 2, 16])
    te, o = carve(o, [128, 2, 16])
    cdr, o = carve(o, [128, 2, 16])
    xdt, o = carve(o, [128, 2, 1024], BF16)
    Btok, o = carve(o, [128, 2, 512], BF16)
    seg, o = carve(o, [128, 2, 4, 128])
    Dm, o = carve(o, [128, 2, 16, 128], BF16)
    cbm, o = carve(o, [128, 2, 4, 128], BF16)
    fsr, o = carve(o, [128, 2, 16, 128], BF16)
    y2, o = carve(o, [128, 2, 128])
    ysq, o = carve(o, [128, 2, 128], BF16)
    yrs, o = carve(o, [128, 128])
    htmp, o = carve(o, [128, 1024])
    cdrS, o = carve(o, [128, 16, 16])
    Bmk, o = carve(o, [128, 2, 512], BF16)

    TRI = cst[:, 0:128]
    IDF = cst[:, 128:256]
    FIX = cst[:, 256:320]
    TRIBD = cst[:, 320:384]
    SAMEBD = cst[:, 384:448]
    ONESF = cst[:, 448:576]
    SEQM = cst[:, 576:592]

    def vcol(l, off, n=1):
        return vecs[:, l * NVL + off: l * NVL + off + n]
    V_F1, V_MIX, V_F2, V_GB, V_PS, V_CW, V_CB, V_DS, V_SN, V_MN = 0, 8, 16, 24, 48, 52, 116, 132, 140, 148
    V_FIN = NVL * DEPTH

    wctr = [0]

    def wload(src_ap, a, b, rk=()):
        n = a * b
        npg = (n + PAGE - 1) // PAGE
        p0 = wctr[0]
        if p0 + npg > NPAGE:
            p0 = 0
        wctr[0] = (p0 + npg) % NPAGE
        keys = [("wsl", p) for p in range(p0, p0 + npg)]
        view = wsl[:, p0 * PAGE:p0 * PAGE + n].rearrange("p (a b) -> p a b", b=b)
        P.dma("pool", view, src_ap, reads=list(rk), writes=keys)
        return view, keys

    def kcp(src):
        return src.rearrange("(c p) n -> p c n", p=128)

    def MM(out, lhsT, rhs, start=True, stop=True):
        return ("matmul", dict(out=out, lhsT=lhsT, rhs=rhs, start=start, stop=stop))

    def TR(out, in_, identity):
        return ("transpose", dict(out=out, in_=in_, identity=identity))

    def rmsnorm(dst, dkeyf, src, skeyf, tb, gcol, sq, inv_n=1.0 / D, nchunk=KC):
        for c in range(nchunk):
            P.op("act", "activation", reads=[skeyf(c)], writes=[("sq", c)], out=sq[:, c, 0:tb], in_=src[c], func=AF.Square)
        P.mmgroup([MM(ps[7][:, 0:tb], onesb[:], sq[:, c, 0:tb], c == 0, c == nchunk - 1) for c in range(nchunk)],
                  reads=["sq", "onesb"], writes=[("ps", 7)])
        P.op("act", "activation", reads=[("ps", 7), "epsc"], writes=["rstd"], out=rstd[:, 0:tb], in_=ps[7][:, 0:tb], func=AF.Sqrt, bias=EPSC[:], scale=inv_n)
        P.op("dve", "reciprocal", reads=["rstd"], writes=["rstd"], out=rstd[:, 0:tb], in_=rstd[:, 0:tb])
        for c in range(nchunk):
            P.op("dve", "scalar_tensor_tensor", reads=[skeyf(c), "rstd", "vecs"], writes=[dkeyf(c)],
                 out=dst[c], in0=src[c], scalar=gcol[:, c:c + 1], in1=rstd[:, 0:tb], op0=ALU.mult, op1=ALU.mult)

    P.dma("sp", vecs[:], vecs_d[:, :], writes=["vecs"])
    P.dma("sp", rows[:], rows_d[:, :], writes=["rows"])
    P.dma("sp", cst[:], consts_d[:, :], writes=["cst"])
    P.dma("sp", x[:], xT_d.rearrange("(c p) t -> p c t", p=128), writes=["x"])
    P.op("dve", "tensor_copy", reads=["cst"], writes=["identb"], out=identb[:], in_=IDF)
    P.op("dve", "memset", writes=["onesb"], ap=onesb[:], constant=1.0)
    P.op("dve", "memset", writes=["epsc"], ap=EPSC[:], constant=EPS)

    fblocks = [(t, min(512, LP - t)) for t in range(0, LP, 512)] + [(LP, NS * LS)]
    groups = [(0, 4), (4, 4), (8, 4), (12, 4), (16, 4), (20, 2)]

    def precast_pieces(l):
        pcs = []
        for r in range(8):
            pcs.append((WB["w_in"][l, r * 128:(r + 1) * 128, :].rearrange("p (a b) -> p a b", a=4), W["w_in"][l, r * 128:(r + 1) * 128, :].rearrange("p (a b) -> p a b", a=4)))
        for nm, nrow in (("w_br_pool", 512), ("w_br_ssm", 1024), ("w_br_mem", 512), ("w_o", 1024)):
            for r in range(0, nrow, 512):
                pcs.append((WB[nm][l, r:r + 512, :], W[nm][l, r:r + 512, :]))
        pcs.append((WB["pool_w"][l].rearrange("g c d -> (g c) d"), W["pool_w"][l].rearrange("g c d -> (g c) d")))
        return pcs

    def ffn(l, wg_d, wu_d, wd_d, gcol, pcs=()):
        pcs = list(pcs)
        def norm_blk(bi):
            t0, tb = fblocks[bi]
            rmsnorm([xn[:, c, t0:t0 + tb] for c in range(KC)], lambda c, t0=t0: ("xn", t0),
                    [x[:, c, t0:t0 + tb] for c in range(KC)], lambda c, t0=t0: ("x", t0, c), tb, gcol, sqF)
        norm_blk(0)
        normed = 1
        pi = 0
        for (f0, G) in groups:
            wgs, wus, wds = [], [], []
            for ci in range(G):
                f = f0 + ci
                wgs.append(wload(kcp(wg_d[l, :, f * 128:(f + 1) * 128]), KC, 128))
                wus.append(wload(kcp(wu_d[l, :, f * 128:(f + 1) * 128]), KC, 128))
                wds.append(wload(wd_d[l, f * 128:(f + 1) * 128, :].rearrange("p (o n) -> p o n", o=1), 1, D))
            for _ in range(3):
                if pcs:
                    o_, i_ = pcs.pop(0)
                    P.dma("pool", o_, i_, writes=[("wb%d" % l, len(pcs))])
            for bi_, (t0, tb) in enumerate(fblocks):
                par = pi % 2
                pi += 1
                if normed < len(fblocks) and bi_ + 1 == normed:
                    norm_blk(normed)
                    normed += 1
                for ci in range(G):
                    bg, bu = (0, 1) if (ci % 2 == 0) else (2, 3)
                    wg, kg = wgs[ci]
                    wu, ku = wus[ci]
                    P.mmgroup([MM(ps[bg][:, 0:tb], wg[:, c, :], xn[:, c, t0:t0 + tb], c == 0, c == KC - 1) for c in range(KC)],
                              reads=kg + [("xn", t0)], writes=[("ps", bg)])
                    P.mmgroup([MM(ps[bu][:, 0:tb], wu[:, c, :], xn[:, c, t0:t0 + tb], c == 0, c == KC - 1) for c in range(KC)],
                              reads=ku + [("xn", t0)], writes=[("ps", bu)])
                    P.op("act", "activation", reads=[("ps", bg)], writes=[("sg", ci % 2)], out=sg[:, ci % 2, 0:tb], in_=ps[bg][:, 0:tb], func=AF.Silu)
                    P.op("dve", "tensor_tensor", reads=[("ps", bu), ("sg", ci % 2)], writes=[("hTf", par, ci)],
                         out=hTf[:, par, ci, 0:tb], in0=ps[bu][:, 0:tb], in1=sg[:, ci % 2, 0:tb], op=ALU.mult)
                for dc in range(KC):
                    by = 4 + dc % 3
                    P.mmgroup([MM(ps[by][:, 0:tb], wds[ci][0][:, 0, dc * 128:(dc + 1) * 128], hTf[:, par, ci, 0:tb], ci == 0, ci == G - 1) for ci in range(G)],
                              reads=sum([wds[ci][1] for ci in range(G)], []) + [("hTf", par, ci) for ci in range(G)], writes=[("ps", by)])
                    P.op("dve", "scalar_tensor_tensor", reads=[("ps", by), ("x", t0, dc)], writes=[("x", t0, dc)],
                         out=x[:, dc, t0:t0 + tb], in0=ps[by][:, 0:tb], scalar=0.5, in1=x[:, dc, t0:t0 + tb], op0=ALU.mult, op1=ALU.add)

    WSRC = WB if (USE_PRECAST and 'ffn1' in PHASES) else W
    WRK = (lambda l: ["wb%d" % l]) if (USE_PRECAST and 'ffn1' in PHASES) else (lambda l: [])

    def memory_stage(l):
        P.dma("sp", memx[:], memT_d.rearrange("(c p) t -> p c t", p=128), writes=["memx"])
        rmsnorm([memn[:, c, :] for c in range(KC)], lambda c: ("memn", c), [memx[:, c, :] for c in range(KC)], lambda c: "memx",
                256, vcol(l, V_MN, 8), sqX)
        wk, kk = wload(kcp(W["w_mem_k"][l]), KC, 512)
        wv, kv = wload(kcp(W["w_mem_v"][l]), KC, 512)
        for hd in range(4):
            P.mmgroup([MM(ps[hd][:, 0:256], wk[:, c, hd * 128:(hd + 1) * 128], memn[:, c, :], c == 0, c == KC - 1) for c in range(KC)],
                      reads=kk + ["memn"], writes=[("ps", hd)])
            P.op("act", "activation", reads=[("ps", hd)], writes=[("kst", hd)], out=kst[:, hd, :], in_=ps[hd][:, 0:256], func=AF.Copy)
            P.op("dve", "tensor_copy", reads=[("ps", hd)], writes=[("KTp", hd)], out=KTp[:, hd, :], in_=ps[hd][:, 0:256])
        P.dma("sp", mkT_o[l].rearrange("(h p) m -> p h m", p=128), kst[:], reads=["kst"], writes=[("mkT_o", l)])
        for mc in range(2):
            P.mmgroup([MM(ps[4 + mc][:, 0:512], memn[:, c, mc * 128:(mc + 1) * 128], wv[:, c, :], c == 0, c == KC - 1) for c in range(KC)],
                      reads=kv + ["memn"], writes=[("ps", 4 + mc)])
            P.op("act", "activation", reads=[("ps", 4 + mc)], writes=[("vst", mc)], out=vst[:, mc, :], in_=ps[4 + mc][:, 0:512], func=AF.Copy)
            P.op("dve", "tensor_copy", reads=[("ps", 4 + mc)], writes=[("Vp", mc)], out=Vp[:, mc, :], in_=ps[4 + mc][:, 0:512])
        P.dma("sp", mv_o[l].rearrange("(c p) n -> p c n", p=128), vst[:], reads=["vst"], writes=[("mv_o", l)])
        for pr in range(8):
            par = pr % 2
            for q in range(2):
                for j in range(4):
                    P.op("dve", "tensor_scalar", reads=["identb", "vecs"], writes=[("dgt", par)], out=dgt[:, par, q * 4 + j, :], in0=identb[:], scalar1=vcol(l, V_CW + j * 16 + 2 * pr + q),
                         scalar2=None, op0=ALU.mult)
            P.dma("sp", WD[l, pr], dgt[:, par], reads=[("dgt", par)], writes=[("wdc%d" % l, pr)])
        P.op("act", "activation", reads=["rows"], writes=["arow"], out=arow[:], in_=rows[:, l * 32 + 16:l * 32 + 32], func=AF.Exp)
        P.op("dve", "tensor_scalar", reads=["arow"], writes=["arow"], out=arow[:], in0=arow[:], scalar1=-1.0, scalar2=None, op0=ALU.mult)

    def v3(ap2, nseq, L):
        return ap2.rearrange("p (s j) -> p s j", j=L)

    def s1_pool(l, kind, first, last, nseq, L, TB):
        wp, kp = wload(kcp(WSRC["w_in"][l, :, C_POOL:C_POOL + 512]), KC, 512, WRK(l))
        P.dma("pool", wsmall[:, 0, 0:512].rearrange("p (g d) -> p g d", d=128), WSRC["pool_w"][l].rearrange("g c d -> c g d"), reads=WRK(l), writes=[("wsmall", 0)])
        pw = wsmall[:, 0, 0:512].rearrange("p (g d) -> p g d", d=128)
        HL = 15 + L

        def U(g):
            return v3(ub[:, g, 0:nseq * HL], nseq, HL)
        A = v3(pa[:, 0:nseq * HL], nseq, HL)
        Bv = v3(pb_[:, 0:nseq * HL], nseq, HL)
        if kind == "p":
            if first:
                P.op("dve", "memset", writes=["ub"], ap=ub[:, :, 0:15], constant=0.0)
            else:
                P.op("dve", "tensor_copy", reads=["ub"], writes=["ub"], out=ub[:, :, 0:15], in_=ub[:, :, MB:MB + 15])
        else:
            for g in range(4):
                P.dma("sp", U(g)[:, :, 0:15], pool_in[l, g * 128:(g + 1) * 128, :, :], reads=[], writes=["ub"])
        for g in range(4):
            P.mmgroup([MM(ps[g][:, 0:TB], wp[:, c, g * 128:(g + 1) * 128], hb[:, c, 0:TB], c == 0, c == KC - 1) for c in range(KC)],
                      reads=kp + ["hb"], writes=[("ps", g)])
            P.op("act", "activation", reads=[("ps", g), "ub"], writes=[("ub", g)], out=U(g)[:, :, 15:15 + L], in_=v3(ps[g][:, 0:TB], nseq, L), func=AF.Copy)
        for g in range(4):
            w = 2 ** (g + 1)
            cur, ckey = U(g), ("ub", g)
            bufs = [(A, "pa"), (Bv, "pb")]
            for k in range(1, g + 2):
                lo = 2 ** k - 1
                sh = 2 ** (k - 1)
                nb, nkey = bufs[k % 2]
                P.op("dve", "tensor_tensor", reads=[ckey], writes=[nkey], out=nb[:, :, lo:HL], in0=cur[:, :, lo:HL], in1=cur[:, :, lo - sh:HL - sh], op=ALU.add)
                cur, ckey = nb, nkey
            if kind == "p" and first:
                P.op("dve", "tensor_tensor", reads=[ckey, "cst"], writes=[ckey], out=cur[:, 0, 15:30], in0=cur[:, 0, 15:30], in1=FIX[:, g * 16:g * 16 + 15], op=ALU.mult)
            P.op("dve", "scalar_tensor_tensor", reads=[ckey, ("ub", g)], writes=[("dpl", g)], out=v3(dpl[:, g, 0:TB], nseq, L),
                 in0=cur[:, :, 15:HL], scalar=1.0 / w, in1=U(g)[:, :, 15:HL], op0=ALU.mult, op1=ALU.subtract)
        for g in range(4):
            b_ = 4 + g % 2
            P.mmgroup([MM(ps[b_][:, 0:TB], pw[:, g, :], dpl[:, g, 0:TB])], reads=[("wsmall", 0), ("dpl", g)], writes=[("ps", b_)])
            P.op("act", "activation", reads=[("ps", b_), "vecs"], writes=[("ypool", g)], out=ypool[:, g, 0:TB], in_=ps[b_][:, 0:TB], func=AF.Identity,
                 scale=vcol(l, V_PS + g))
        if kind == "p" and last:
            for g in range(4):
                P.dma("sp", pool_o[l, g * 128:(g + 1) * 128, 0, :], ub[:, g, MB:MB + 15], reads=[("ub", g)], writes=[("pool_o", l, g, 0)])
        if kind == "s":
            for g in range(4):
                P.dma("sp", pool_o[l, g * 128:(g + 1) * 128, 1:NS + 1, :], U(g)[:, :, LS:LS + 15], reads=[("ub", g)], writes=[("pool_o", l, g, 1)])

    def s2_sample_batched(l):
        Lt = NS * LS
        for hd in range(4):
            P.op("dve", "memset", writes=[("Qexp", hd)], ap=Qexp[:, hd, :], constant=0.0)
            P.op("dve", "tensor_copy", reads=[("qT", hd)], writes=[("Qexp", hd)], out=Qexp[:, hd, :].rearrange("p (b s) -> p b s", s=68)[:, :, 0:LS],
                 in_=qT[:, hd, 0:Lt].rearrange("p (b j) -> p b j", j=LS))
        for b in range(NS):
            sl = b % 2
            P.dma("pool", KTs[:, sl], kT_in[l, b].rearrange("h d m -> d h m"), writes=[("KTs", sl)])
            for hd in range(4):
                P.mmgroup([("matmul", dict(out=ps[hd][0:Lt, 0:256], lhsT=Qexp[:, hd, b * 64:(b + 1) * 64], rhs=KTs[:, sl, hd, :], start=(b == 0), stop=(b == NS - 1)))],
                          reads=[("Qexp", hd), ("KTs", sl)], writes=[("ps", hd)])
        for hd in range(4):
            par = hd % 2
            sc = ps[hd][0:Lt, 0:256]
            mxk = ("att_mx", par)
            P.op("dve", "reduce_max", reads=[("ps", hd)], writes=[mxk], out=att_mx[0:Lt, par, 1:2], in_=sc, axis=AX.X, negate=True)
            P.op("act", "activation", reads=[("ps", hd), mxk], writes=[("att_pe", par), mxk], out=att_pe[0:Lt, par, :], in_=sc, func=AF.Exp,
                 bias=att_mx[0:Lt, par, 1:2], scale=1.0, accum_out=att_mx[0:Lt, par, 2:3])
            P.op("dve", "reciprocal", reads=[mxk], writes=[mxk], out=att_mx[0:Lt, par, 3:4], in_=att_mx[0:Lt, par, 2:3])
            P.op("dve", "tensor_scalar", reads=[("att_pe", par), mxk], writes=[("att_pb", par)], out=att_pb[0:Lt, par, :], in0=att_pe[0:Lt, par, :],
                 scalar1=att_mx[0:Lt, par, 3:4], scalar2=None, op0=ALU.mult)
            P.mmgroup([TR(psb[6][:, mc * 128:mc * 128 + Lt], att_pb[0:Lt, par, mc * 128:(mc + 1) * 128], identb[0:Lt, 0:Lt]) for mc in range(2)],
                      reads=[("att_pb", par), "identb"], writes=[("ps", 6)])
            P.op("act", "activation", reads=[("ps", 6)], writes=[("att_pT4", hd)], out=att_pT4[:, hd, :, 0:Lt],
                 in_=psb[6][:, 0:256].rearrange("p (m t) -> p m t", t=128)[:, :, 0:Lt], func=AF.Copy)
        for b in range(NS):
            sl = b % 2
            P.dma("pool", Vs[:, sl], kcp(v_in[l, b]), writes=[("Vs", sl)])
            kws = []
            for hd in range(4):
                for mc in range(2):
                    kws.append(("matmul", dict(out=ps[7][:, hd * 64 + b * LS:hd * 64 + (b + 1) * LS], lhsT=Vs[:, sl, mc, hd * 128:(hd + 1) * 128],
                                               rhs=att_pT4[:, hd, mc, b * LS:(b + 1) * LS], start=(mc == 0), stop=(mc == 1), skip_group_check=True)))
            P.mmgroup(kws, reads=[("Vs", sl), "att_pT4"], writes=[("ps", 7)])
        P.op("dve", "tensor_copy", reads=[("ps", 7)], writes=["ymem"], out=ymem[:, :, 0:Lt], in_=ps[7][:, 0:4 * 64].rearrange("p (h t) -> p h t", t=64))

    def s2_attn(l, kind, nseq, L, TB):
        wq, kq = wload(kcp(WSRC["w_in"][l, :, C_Q:C_Q + 512]), KC, 512, WRK(l))
        for hd in range(4):
            P.mmgroup([MM(ps[hd][:, 0:TB], wq[:, c, hd * 128:(hd + 1) * 128], hb[:, c, 0:TB], c == 0, c == KC - 1) for c in range(KC)],
                      reads=kq + ["hb"], writes=[("ps", hd)])
            P.op("dve", "tensor_scalar", reads=[("ps", hd)], writes=[("qT", hd)], out=qT[:, hd, 0:TB], in0=ps[hd][:, 0:TB], scalar1=SCALE, scalar2=None, op0=ALU.mult)
        it = 0
        if kind == "s":
            s2_sample_batched(l)
            return
        tiles = [(None, c0, 128) for c0 in range(0, TB, 128)]
        for (b, c0, Lt) in tiles:
            if b is None:
                KT, Vv, kkey, vkey = KTp, Vp, "KTp", "Vp"
            else:
                sl = b % 2
                P.dma("pool", KTs[:, sl], kT_in[l, b].rearrange("h d m -> d h m"), writes=[("KTs", sl)])
                P.dma("pool", Vs[:, sl], kcp(v_in[l, b]), writes=[("Vs", sl)])
                KT, Vv, kkey, vkey = KTs[:, sl], Vs[:, sl], ("KTs", sl), ("Vs", sl)
            for hd in range(4):
                par = it % 2
                it += 1
                sc = ps[4 + par][0:Lt, 0:256]
                P.mmgroup([MM(sc, qT[:, hd, c0:c0 + Lt], KT[:, hd, :])], reads=[("qT", hd), kkey], writes=[("ps", 4 + par)])
                mxk = ("att_mx", par)
                P.op("dve", "reduce_max", reads=[("ps", 4 + par)], writes=[mxk], out=att_mx[0:Lt, par, 1:2], in_=sc, axis=AX.X, negate=True)
                P.op("act", "activation", reads=[("ps", 4 + par), mxk], writes=[("att_pe", par), mxk], out=att_pe[0:Lt, par, :], in_=sc, func=AF.Exp,
                     bias=att_mx[0:Lt, par, 1:2], scale=1.0, accum_out=att_mx[0:Lt, par, 2:3])
                P.op("dve", "reciprocal", reads=[mxk], writes=[mxk], out=att_mx[0:Lt, par, 3:4], in_=att_mx[0:Lt, par, 2:3])
                P.op("dve", "tensor_scalar", reads=[("att_pe", par), mxk], writes=[("att_pb", par)], out=att_pb[0:Lt, par, :], in0=att_pe[0:Lt, par, :],
                     scalar1=att_mx[0:Lt, par, 3:4], scalar2=None, op0=ALU.mult)
                P.mmgroup([TR(psb[6][:, mc * 128:mc * 128 + Lt], att_pb[0:Lt, par, mc * 128:(mc + 1) * 128], identb[0:Lt, 0:Lt]) for mc in range(2)],
                          reads=[("att_pb", par), "identb"], writes=[("ps", 6)])
                P.op("act", "activation", reads=[("ps", 6)], writes=[("att_pT", par)], out=att_pT[:, par, :, 0:Lt],
                     in_=psb[6][:, 0:256].rearrange("p (m t) -> p m t", t=128)[:, :, 0:Lt], func=AF.Copy)
                P.mmgroup([MM(ps[7][:, 0:Lt], Vv[:, mc, hd * 128:(hd + 1) * 128], att_pT[:, par, mc, 0:Lt], mc == 0, mc == 1) for mc in range(2)],
                          reads=[vkey, ("att_pT", par)], writes=[("ps", 7)])
                P.op("dve", "tensor_copy", reads=[("ps", 7)], writes=[("ymem", hd)], out=ymem[:, hd, c0:c0 + Lt], in_=ps[7][:, 0:Lt])

    def ycol(j, Lc):
        return (0, j * 64) if Lc <= 64 else (j // 4, (j % 4) * 128)

    def chunk_A(l, c0, Lc, par, TRIm, SAMEm, sample):
        pk = lambda n: (n, par)
        wdt = wsmall[:, 1, 0:128].rearrange("p (c n) -> p c n", n=16)
        P.mmgroup([MM(ps[3][0:Lc, 16:32], hb[:, c, c0:c0 + Lc], wdt[:, c, :], c == 0, c == KC - 1) for c in range(KC)],
                  reads=["hb", ("wsmall", 1)], writes=[("ps", 3)])
        P.op("dve", "tensor_tensor", reads=[("ps", 3), "rows"], writes=[pk("dtb")], out=dtb[0:Lc, par, :], in0=ps[3][0:Lc, 16:32], in1=rows[0:Lc, l * 32:l * 32 + 16], op=ALU.add)
        P.op("act", "activation", reads=[pk("dtb")], writes=[pk("dtb")], out=dtb[0:Lc, par, :], in_=dtb[0:Lc, par, :], func=AF.Exp)
        P.op("act", "activation", reads=[pk("dtb")], writes=[pk("dtb")], out=dtb[0:Lc, par, :], in_=dtb[0:Lc, par, :], func=AF.Ln, bias=1.0)
        P.op("dve", "tensor_tensor", reads=[pk("dtb"), "arow"], writes=[pk("lab")], out=lab[0:Lc, par, :], in0=dtb[0:Lc, par, :], in1=arow[0:Lc, :], op=ALU.mult)
        P.mmgroup([MM(ps[3][0:Lc, 0:16], TRIm[0:Lc, 0:Lc], lab[0:Lc, par, :]), MM(ps[3][0:Lc, 32:48], SAMEm[0:Lc, 0:Lc], lab[0:Lc, par, :])],
                  reads=["cst", pk("lab")], writes=[("ps", 3)])
        P.op("dve", "tensor_scalar", reads=[("ps", 3)], writes=[pk("nacum")], out=nacum[0:Lc, par, :], in0=ps[3][0:Lc, 0:16], scalar1=-1.0, scalar2=None, op0=ALU.mult)
        P.op("dve", "tensor_tensor", reads=[("ps", 3), pk("nacum")], writes=[pk("te")], out=te[0:Lc, par, :], in0=ps[3][0:Lc, 32:48], in1=nacum[0:Lc, par, :], op=ALU.add)
        P.op("act", "activation", reads=[pk("te")], writes=[pk("te")], out=te[0:Lc, par, :], in_=te[0:Lc, par, :], func=AF.Exp)
        for hq in range(4):
            P.mmgroup([MM(ps[4 + hq][:, hh * 128:hh * 128 + Lc], lab[0:Lc, par, hq * 4 + hh:hq * 4 + hh + 1].broadcast_to([Lc, 128]), TRIm[0:Lc, 0:Lc]) for hh in range(4)],
                      reads=[pk("lab"), "cst"], writes=[("ps", 4 + hq)])
        nb = 1 if Lc <= 64 else 2
        for bk in range(2):
            P.mmgroup([TR(ps[bk][0:Lc, jj * 128:(jj + 1) * 128], xsT[:, bk * 4 + jj, c0:c0 + Lc], IDF) for jj in range(4)],
                      reads=["xsT", "cst"], writes=[("ps", bk)])
            P.op("dve", "tensor_tensor", reads=[("ps", bk), pk("dtb")], writes=[pk("xdt")], out=xdt[0:Lc, par, bk * 512:(bk + 1) * 512].rearrange("p (h q) -> p h q", q=64),
                 in0=ps[bk][0:Lc, :].rearrange("p (h q) -> p h q", q=64), in1=dtb[0:Lc, par, bk * 8:(bk + 1) * 8].unsqueeze(2).broadcast_to([Lc, 8, 64]), op=ALU.mult)
        P.mmgroup([TR(psb[2][0:Lc, g * 128:(g + 1) * 128], BT[:, g, c0:c0 + Lc], identb[:]) for g in range(4)], reads=["BT", "identb"], writes=[("ps", 2)])
        P.op("act", "activation", reads=[("ps", 2)], writes=[pk("Btok")], out=Btok[0:Lc, par, :], in_=psb[2][0:Lc, 0:512], func=AF.Copy)
        P.mmgroup([MM(ps[2][0:Lc, g * 128:g * 128 + Lc], BT[:, g, c0:c0 + Lc], CT[:, g, c0:c0 + Lc]) for g in range(4)], reads=["BT", "CT"], writes=[("ps", 2)])
        for g in range(4):
            P.op("dve", "tensor_tensor", reads=[("ps", 2), "cst"], writes=[("cbm", par, g)], out=cbm[0:Lc, par, g, 0:Lc], in0=ps[2][0:Lc, g * 128:g * 128 + Lc], in1=TRIm[0:Lc, 0:Lc], op=ALU.mult)
        for hq in range(4):
            sp_ = hq % 2
            for hh in range(4):
                h = hq * 4 + hh
                P.op("dve", "tensor_scalar", reads=[("ps", 4 + hq), pk("nacum")], writes=[("seg", sp_)], out=seg[0:Lc, sp_, hh, 0:Lc], in0=ps[4 + hq][0:Lc, hh * 128:hh * 128 + Lc],
                     scalar1=nacum[0:Lc, par, h:h + 1], scalar2=0.0, op0=ALU.add, op1=ALU.min)
            P.op("act", "activation", reads=[("seg", sp_)], writes=[("Dm", par, hq)], out=Dm[0:Lc, par, hq * 4:(hq + 1) * 4, 0:Lc], in_=seg[0:Lc, sp_, :, 0:Lc], func=AF.Exp)
            pv = ps[4 + hq][:, :].rearrange("p (h t) -> p h t", t=128)
            P.op("act", "activation", reads=[("ps", 4 + hq)], writes=[("fsr", par, hq)], out=fsr[:, par, hq * 4:(hq + 1) * 4, 0:Lc], in_=pv[:, :, 0:Lc], func=AF.Exp)
            if not sample:
                P.op("act", "activation", reads=[("ps", 4 + hq)], writes=[pk("cdr")], out=cdr[:, par, hq * 4:(hq + 1) * 4], in_=pv[:, :, Lc - 1], func=AF.Exp)
            else:
                P.op("act", "activation", reads=[("ps", 4 + hq)], writes=["cdrS"], out=cdrS[:, :, hq * 4:(hq + 1) * 4].rearrange("p b h -> p h b"),
                     in_=pv[:, :, LS - 1:Lc:LS], func=AF.Exp)
        for g in range(4):
            P.op("dve", "tensor_tensor", reads=[("Dm", par, g), ("cbm", par, g)], writes=[("Dm", par, g)], out=Dm[0:Lc, par, g * 4:(g + 1) * 4, 0:Lc], in0=Dm[0:Lc, par, g * 4:(g + 1) * 4, 0:Lc],
                 in1=cbm[0:Lc, par, g, 0:Lc].unsqueeze(1).broadcast_to([Lc, 4, Lc]), op=ALU.mult)
            P.op("dve", "tensor_tensor", reads=[("fsr", par, g), "CT"], writes=[("fsr", par, g)], out=fsr[:, par, g * 4:(g + 1) * 4, 0:Lc], in0=fsr[:, par, g * 4:(g + 1) * 4, 0:Lc],
                 in1=CT[:, g, c0:c0 + Lc].unsqueeze(1).broadcast_to([128, 4, Lc]), op=ALU.mult)

    def gating(l, c0, Lc):
        for g in range(4):
            for jj in range(2):
                j = 2 * g + jj
                bk, cl = ycol(j, Lc)
                yps = ps[bk][:, cl:cl + Lc]
                P.op("dve", "scalar_tensor_tensor", reads=["xsT", ("ps", bk), "vecs"], writes=[("y2", jj)], out=y2[:, jj, 0:Lc], in0=xsT[:, j, c0:c0 + Lc],
                     scalar=vcol(l, V_DS + j), in1=yps, op0=ALU.mult, op1=ALU.add)
                P.op("dve", "tensor_tensor", reads=[("y2", jj), "siluz"], writes=[("y2", jj)], out=y2[:, jj, 0:Lc], in0=y2[:, jj, 0:Lc], in1=siluz[:, j, c0:c0 + Lc], op=ALU.mult)
                P.op("act", "activation", reads=[("y2", jj)], writes=[("ysq", jj)], out=ysq[:, jj, 0:Lc], in_=y2[:, jj, 0:Lc], func=AF.Square)
            P.mmgroup([MM(ps[3][:, 128:128 + Lc], onesb[:], ysq[:, jj, 0:Lc], jj == 0, jj == 1) for jj in range(2)], reads=["ysq", "onesb"], writes=[("ps", 3)])
            P.op("act", "activation", reads=[("ps", 3), "epsc"], writes=["yrs"], out=yrs[:, 0:Lc], in_=ps[3][:, 128:128 + Lc], func=AF.Sqrt, bias=EPSC[:], scale=1.0 / 256)
            P.op("dve", "reciprocal", reads=["yrs"], writes=["yrs"], out=yrs[:, 0:Lc], in_=yrs[:, 0:Lc])
            for jj in range(2):
                j = 2 * g + jj
                P.op("dve", "scalar_tensor_tensor", reads=[("y2", jj), "yrs", "vecs"], writes=[("yssm", j)], out=yssm[:, j, c0:c0 + Lc], in0=y2[:, jj, 0:Lc],
                     scalar=vcol(l, V_SN + j), in1=yrs[:, 0:Lc], op0=ALU.mult, op1=ALU.mult)

    def chunk_B_prompt(l, c0, Lc, par, init, seq_out):
        if init == "zero":
            P.op("dve", "memset", writes=["hT"], ap=hT[:], constant=0.0)
            P.op("dve", "memset", writes=["hTb"], ap=hTb[:], constant=0.0)
        for j in range(8):
            kws = []
            bk, cl = ycol(j, Lc)
            for hh in range(2):
                h = 2 * j + hh
                outp = ps[bk][64 * hh:64 * hh + 64, cl:cl + Lc]
                kws.append(MM(outp, hTb[:, h * 64:(h + 1) * 64], fsr[:, par, h, 0:Lc], True, False))
                kws.append(MM(outp, xdt[0:Lc, par, h * 64:(h + 1) * 64], Dm[0:Lc, par, h, 0:Lc], False, True))
            P.mmgroup(kws, reads=["hTb", ("fsr", par, j // 2), ("xdt", par), ("Dm", par, j // 2)], writes=[("ps", bk)])
        gating(l, c0, Lc)
        P.op("dve", "tensor_tensor", reads=[("xdt", par), ("te", par)], writes=[("xdt", par)], out=xdt[0:Lc, par, :].rearrange("p (h q) -> p h q", q=64),
             in0=xdt[0:Lc, par, :].rearrange("p (h q) -> p h q", q=64), in1=te[0:Lc, par, :].unsqueeze(2).broadcast_to([Lc, 16, 64]), op=ALU.mult)
        for g in range(4):
            P.mmgroup([MM(ps[4 + g // 2][:, (g % 2) * 256:(g % 2 + 1) * 256], Btok[0:Lc, par, g * 128:(g + 1) * 128], xdt[0:Lc, par, g * 256:(g + 1) * 256])],
                      reads=[("Btok", par), ("xdt", par)], writes=[("ps", 4 + g // 2)])
        P.op("dve", "tensor_tensor", reads=["hT", ("cdr", par)], writes=["htmp"], out=htmp[:, :].rearrange("p (h q) -> p h q", q=64), in0=hT[:, :].rearrange("p (h q) -> p h q", q=64),
             in1=cdr[:, par, :].unsqueeze(2).broadcast_to([128, 16, 64]), op=ALU.mult)
        for bk in range(2):
            P.op("dve", "tensor_tensor", reads=["htmp", ("ps", 4 + bk)], writes=["hT"], out=hT[:, bk * 512:(bk + 1) * 512], in0=ps[4 + bk][:, :], in1=htmp[:, bk * 512:(bk + 1) * 512], op=ALU.add)
        P.op("act", "activation", reads=["hT"], writes=["hTb"], out=hTb[:], in_=hT[:], func=AF.Copy)
        if seq_out is not None:
            P.dma("sp", ssmT_o[l, seq_out], hT[:], reads=["hT"], writes=[("ssmT_o", l, seq_out)])

    def chunk_B_sample(l):
        Lc = NS * LS
        par = 0
        P.op("dve", "tensor_tensor", reads=[("xdt", 0), ("te", 0)], writes=[("xdt", 1)], out=xdt[0:Lc, 1, :].rearrange("p (h q) -> p h q", q=64),
             in0=xdt[0:Lc, 0, :].rearrange("p (h q) -> p h q", q=64), in1=te[0:Lc, 0, :].unsqueeze(2).broadcast_to([Lc, 16, 64]), op=ALU.mult)
        for b in range(NS):
            P.dma("act", hT[:], ssmT_in[l, b], writes=["hT"])
            P.op("act", "activation", reads=["hT"], writes=["hTb"], out=hTb[:], in_=hT[:], func=AF.Copy)
            kws = []
            for j in range(8):
                for hh in range(2):
                    h = 2 * j + hh
                    outp = ps[0][64 * hh:64 * hh + 64, j * 64 + b * LS:j * 64 + (b + 1) * LS]
                    kws.append(("matmul", dict(out=outp, lhsT=hTb[:, h * 64:(h + 1) * 64], rhs=fsr[:, 0, h, b * LS:(b + 1) * LS], start=(b == 0 and j == 0), stop=False, skip_group_check=True)))
            P.mmgroup(kws, reads=["hTb", "fsr"], writes=[("ps", 0)])
            mb = b % 2
            P.op("dve", "tensor_scalar", reads=[("Btok", 0), "cst"], writes=[("Bmk", mb)], out=Bmk[0:Lc, mb, :], in0=Btok[0:Lc, 0, :], scalar1=SEQM[0:Lc, b:b + 1], scalar2=None, op0=ALU.mult)
            pb0 = 4 + 2 * mb
            for g in range(4):
                P.mmgroup([MM(ps[pb0 + g // 2][:, (g % 2) * 256:(g % 2 + 1) * 256], Bmk[0:Lc, mb, g * 128:(g + 1) * 128], xdt[0:Lc, 1, g * 256:(g + 1) * 256])],
                          reads=[("Bmk", mb), ("xdt", 1)], writes=[("ps", pb0 + g // 2)])
            P.op("dve", "tensor_tensor", reads=["hT", "cdrS"], writes=["htmp"], out=htmp[:, :].rearrange("p (h q) -> p h q", q=64), in0=hT[:, :].rearrange("p (h q) -> p h q", q=64),
                 in1=cdrS[:, b, :].unsqueeze(2).broadcast_to([128, 16, 64]), op=ALU.mult)
            for bk in range(2):
                P.op("dve", "tensor_tensor", reads=["htmp", ("ps", pb0 + bk)], writes=["htmp"], out=htmp[:, bk * 512:(bk + 1) * 512], in0=ps[pb0 + bk][:, :], in1=htmp[:, bk * 512:(bk + 1) * 512], op=ALU.add)
            P.dma("sp", ssmT_o[l, 1 + b], htmp[:], reads=["htmp"], writes=[("ssmT_o", l, 1 + b)])
        kws = []
        for j in range(8):
            for hh in range(2):
                h = 2 * j + hh
                outp = ps[0][64 * hh:64 * hh + 64, j * 64:j * 64 + Lc]
                kws.append(("matmul", dict(out=outp, lhsT=xdt[0:Lc, 0, h * 64:(h + 1) * 64], rhs=Dm[0:Lc, 0, h, 0:Lc], start=False, stop=True, skip_group_check=True)))
        P.mmgroup(kws, reads=[("xdt", 0), "Dm"], writes=[("ps", 0)])
        gating(l, 0, Lc)

    def s3_ssd(l, kind, first, last, nseq, L, TB):
        P.dma("pool", wsmall[:, 1, 0:128].rearrange("p (c n) -> p c n", n=16), kcp(WSRC["w_in"][l, :, C_DT:C_DT + 16]), reads=WRK(l), writes=[("wsmall", 1)])
        rot = 0
        for half in range(2):
            for q in range(4):
                zc = half * 4 + q
                if zc % 2 == 0:
                    wz, kz = wload(kcp(WSRC["w_in"][l, :, C_Z + zc * 128:C_Z + (zc + 2) * 128]), KC, 256, WRK(l))
                b_ = rot % 4
                rot += 1
                P.mmgroup([MM(ps[b_][:, 0:TB], wz[:, c, (zc % 2) * 128:(zc % 2 + 1) * 128], hb[:, c, 0:TB], c == 0, c == KC - 1) for c in range(KC)], reads=kz + ["hb"], writes=[("ps", b_)])
                P.op("act", "activation", reads=[("ps", b_)], writes=["siluz"], out=siluz[:, zc, 0:TB], in_=ps[b_][:, 0:TB], func=AF.Silu)
        HL = 3 + L
        if kind == "p":
            if first:
                P.op("dve", "memset", writes=["hbh"], ap=hbh[:], constant=0.0)
            wst = {}

            def conv_proj(cc):
                q = cc % 2
                if q == 0:
                    wst["wx"] = wload(kcp(WSRC["w_in"][l, :, C_XBC + cc * 128:C_XBC + (cc + 2) * 128]), KC, 256, WRK(l))
                    wst[("dg", cc // 2)] = wload(WD[l, cc // 2], 8, 128, ["wdc%d" % l])
                wx, kx = wst["wx"]
                b_ = cc % 4
                P.mmgroup([MM(ps[b_][:, 0:3], wx[:, c, q * 128:(q + 1) * 128], hbh[:, c, 0:3], c == 0, c == KC - 1) for c in range(KC)]
                          + [MM(ps[b_][:, 3:3 + L], wx[:, c, q * 128:(q + 1) * 128], hb[:, c, 0:L], c == 0, c == KC - 1) for c in range(KC)],
                          reads=kx + ["hb", "hbh"], writes=[("ps", b_)])
                P.op("act", "activation", reads=[("ps", b_)], writes=[("xpre", q)], out=xpre[:, q, 0:3 + L], in_=ps[b_][:, 0:3 + L], func=AF.Copy)
                if last:
                    nhal = cbuf[:, q, 0:3]
                    P.op("dve", "tensor_copy", reads=[("ps", b_)], writes=[("nhalo", q)], out=nhal, in_=ps[b_][:, L:L + 3])
                    P.dma("sp", conv_o[l, cc * 128:(cc + 1) * 128, 0, :], nhal, reads=[("nhalo", q)], writes=[("conv_o", l, cc, 0)])

            def conv_apply(cc):
                q = cc % 2
                dg, kdg = wst[("dg", cc // 2)]
                b2 = 4 + cc % 4
                P.mmgroup([MM(ps[b2][:, 0:L], dg[:, q * 4 + j, :], xpre[:, q, j:j + L], j == 0, j == 3) for j in range(4)],
                          reads=kdg + [("xpre", q)], writes=[("ps", b2)])
                if cc < 8:
                    dst, dk = xsT[:, cc, 0:TB], "xsT"
                elif cc < 12:
                    dst, dk = BT[:, cc - 8, 0:TB], "BT"
                else:
                    dst, dk = CT[:, cc - 12, 0:TB], "CT"
                P.op("act", "activation", reads=[("ps", b2), "vecs"], writes=[dk], out=dst, in_=ps[b2][:, 0:L], func=AF.Silu, bias=vcol(l, V_CB + cc))

            conv_proj(0)
            for cc in range(16):
                if cc + 1 < 16:
                    conv_proj(cc + 1)
                conv_apply(cc)
            P.op("dve", "tensor_copy", reads=["hb"], writes=["hbh"], out=hbh[:, :, 0:3], in_=hb[:, :, L - 3:L])
        else:
            for cc in range(16):
                if cc % 2 == 0:
                    wx, kx = wload(kcp(WSRC["w_in"][l, :, C_XBC + cc * 128:C_XBC + (cc + 2) * 128]), KC, 256, WRK(l))
                b_ = rot % 4
                rot += 1
                par = cc % 2
                hk = ("halo", par)
                P.mmgroup([MM(ps[b_][:, 0:TB], wx[:, c, (cc % 2) * 128:(cc % 2 + 1) * 128], hb[:, c, 0:TB], c == 0, c == KC - 1) for c in range(KC)], reads=kx + ["hb"], writes=[("ps", b_)])
                pv3 = v3(ps[b_][:, 0:TB], nseq, L)
                hal = v3(cbuf[:, par, 0:nseq * 3], nseq, 3)
                if kind == "p":
                    if first:
                        P.op("dve", "memset", writes=[hk], ap=hal, constant=0.0)
                    else:
                        P.op("dve", "tensor_copy", reads=[("halp", cc)], writes=[hk], out=hal[:, 0, :], in_=halp[:, cc, :])
                else:
                    P.dma("sp", hal, conv_in[l, cc * 128:(cc + 1) * 128, :, :], writes=[hk])
                acc = v3(cacc[:, par, 0:TB], nseq, L)
                ak = ("cacc", par)
                P.op("dve", "tensor_scalar", reads=[("ps", b_), "vecs"], writes=[ak], out=acc, in0=pv3, scalar1=vcol(l, V_CW + 3 * 16 + cc), scalar2=vcol(l, V_CB + cc), op0=ALU.mult, op1=ALU.add)
                for j in range(3):
                    sh = 3 - j
                    if L > sh:
                        P.op("dve", "scalar_tensor_tensor", reads=[("ps", b_), ak, "vecs"], writes=[ak], out=acc[:, :, sh:L], in0=pv3[:, :, 0:L - sh], scalar=vcol(l, V_CW + j * 16 + cc),
                             in1=acc[:, :, sh:L], op0=ALU.mult, op1=ALU.add)
                    n_h = min(sh, L)
                    P.op("dve", "scalar_tensor_tensor", reads=[hk, ak, "vecs"], writes=[ak], out=acc[:, :, 0:n_h], in0=hal[:, :, j:j + n_h], scalar=vcol(l, V_CW + j * 16 + cc),
                         in1=acc[:, :, 0:n_h], op0=ALU.mult, op1=ALU.add)
                nhal = v3(cbuf[:, par, nseq * 3:nseq * 6], nseq, 3)
                nk_ = ("nhalo", par)
                if L >= 3:
                    P.op("act", "activation", reads=[("ps", b_)], writes=[nk_], out=nhal, in_=pv3[:, :, L - 3:L], func=AF.Copy)
                else:
                    P.op("act", "activation", reads=[hk], writes=[nk_], out=nhal[:, :, 0:3 - L], in_=hal[:, :, L:3], func=AF.Copy)
                    P.op("act", "activation", reads=[("ps", b_), nk_], writes=[nk_], out=nhal[:, :, 3 - L:3], in_=pv3, func=AF.Copy)
                if kind == "p":
                    P.op("dve", "tensor_copy", reads=[nk_], writes=[("halp", cc)], out=halp[:, cc, :], in_=nhal[:, 0, :])
                    if last:
                        P.dma("sp", conv_o[l, cc * 128:(cc + 1) * 128, 0, :], nhal[:, 0, :], reads=[nk_], writes=[("conv_o", l, cc, 0)])
                else:
                    P.dma("sp", conv_o[l, cc * 128:(cc + 1) * 128, 1:NS + 1, :], nhal, reads=[nk_], writes=[("conv_o", l, cc, 1)])
                if cc < 8:
                    dst, dk = xsT[:, cc, 0:TB], "xsT"
                elif cc < 12:
                    dst, dk = BT[:, cc - 8, 0:TB], "BT"
                else:
                    dst, dk = CT[:, cc - 12, 0:TB], "CT"
                P.op("act", "activation", reads=[ak], writes=[dk], out=dst, in_=cacc[:, par, 0:TB], func=AF.Silu)

        if kind == "p":
            nch = TB // 128
            for ci in range(nch):
                chunk_A(l, ci * 128, 128, ci % 2, TRI, ONESF, False)
            for ci in range(nch):
                chunk_B_prompt(l, ci * 128, 128, ci % 2, "zero" if (first and ci == 0) else None, 0 if (last and ci == nch - 1) else None)
        else:
            chunk_A(l, 0, NS * LS, 0, TRIBD, SAMEBD, True)
            chunk_B_sample(l)

    def s4_merge(l, t0, TB):
        brs = [(ypool, "ypool", 4, "w_br_pool"), (yssm, "yssm", 8, "w_br_ssm"), (ymem, "ymem", 4, "w_br_mem")]
        wcache = {}
        for dc in range(KC):
            par = dc % 2
            for br, (yb, ykey, nk, wname) in enumerate(brs):
                if dc % 2 == 0:
                    wcache[("g", br)] = wload(kcp(WSRC["w_in"][l, :, C_G + br * 1024 + dc * 128:C_G + br * 1024 + (dc + 2) * 128]), KC, 256, WRK(l))
                    wcache[("b", br)] = wload(kcp(WSRC[wname][l, :, dc * 128:(dc + 2) * 128]), nk, 256, WRK(l))
                wg, kg = wcache[("g", br)]
                wb, kb = wcache[("b", br)]
                cs = slice((dc % 2) * 128, (dc % 2 + 1) * 128)
                P.mmgroup([MM(ps[br][:, 0:TB], wg[:, c, cs], hb[:, c, 0:TB], c == 0, c == KC - 1) for c in range(KC)], reads=kg + ["hb"], writes=[("ps", br)])
                P.op("act", "activation", reads=[("ps", br), "vecs"], writes=[("gsb", par, br)], out=gsb[:, par, br, 0:TB], in_=ps[br][:, 0:TB], func=AF.Sigmoid,
                     bias=vcol(l, V_GB + br * 8 + dc))
                P.mmgroup([MM(ps[3 + br][:, 0:TB], wb[:, k, cs], yb[:, k, 0:TB], k == 0, k == nk - 1) for k in range(nk)], reads=kb + [ykey], writes=[("ps", 3 + br)])
            for br in range(3):
                P.op("dve", "tensor_tensor", reads=[("ps", 3 + br), ("gsb", par, br)], writes=[("mt", par, br)], out=mt[:, par, br, 0:TB], in0=ps[3 + br][:, 0:TB],
                     in1=gsb[:, par, br, 0:TB], op=ALU.mult)
            P.op("dve", "tensor_tensor", reads=[("mt", par, 0), ("mt", par, 1)], writes=[("mt", par, 0)], out=mt[:, par, 0, 0:TB], in0=mt[:, par, 0, 0:TB], in1=mt[:, par, 1, 0:TB], op=ALU.add)
            P.op("dve", "tensor_tensor", reads=[("mt", par, 0), ("mt", par, 2)], writes=[("mrgb", dc)], out=mrgb[:, dc, 0:TB], in0=mt[:, par, 0, 0:TB], in1=mt[:, par, 2, 0:TB], op=ALU.add)
        for dc in range(KC):
            if dc % 2 == 0:
                wo, ko = wload(kcp(WSRC["w_o"][l, :, dc * 128:(dc + 2) * 128]), KC, 256, WRK(l))
            po = 6 + dc % 2
            cs = slice((dc % 2) * 128, (dc % 2 + 1) * 128)
            P.mmgroup([MM(ps[po][:, 0:TB], wo[:, k, cs], mrgb[:, k, 0:TB], k == 0, k == KC - 1) for k in range(KC)], reads=ko + ["mrgb"], writes=[("ps", po)])
            P.op("dve", "tensor_tensor", reads=[("ps", po), ("x", "m", dc)], writes=[("x", "m", dc)], out=x[:, dc, t0:t0 + TB], in0=ps[po][:, 0:TB], in1=x[:, dc, t0:t0 + TB], op=ALU.add)

    def mixer(l):
        ON = lambda n: ("mix" in PHASES) or (n in PHASES)
        P.mark(f"L{l} mem")
        if ON("mem"):
            memory_stage(l)
        blocks = [("p", t, MB) for t in range(0, LP, MB)] + [("s", LP, NS * LS)]
        for bi, (kind, t0, TB) in enumerate(blocks):
            nseq, L = (1, TB) if kind == "p" else (NS, LS)
            first = (bi == 0)
            last = (kind == "p" and t0 + TB == LP)
            if kind == "s" or bi == 0:
                P.barrier(["pe", "act", "dve", "sp"] + (["pool"] if kind == "s" else []))
            rmsnorm([hb[:, c, 0:TB] for c in range(KC)], lambda c: "hb", [x[:, c, t0:t0 + TB] for c in range(KC)], lambda c: ("x", "m", c), TB, vcol(l, V_MIX, 8), sqM)
            P.mark(f"L{l} b{bi} s1")
            if ON("s1"):
                s1_pool(l, kind, first, last, nseq, L, TB)
            P.mark(f"L{l} b{bi} s2")
            if ON("s2"):
                s2_attn(l, kind, nseq, L, TB)
            P.barrier(["pe", "act", "dve", "sp", "pool"] if kind == "s" else ["pe", "act", "dve", "sp"])
            P.mark(f"L{l} b{bi} s3")
            if ON("s3"):
                s3_ssd(l, kind, first, last, nseq, L, TB)
            P.barrier(["pe", "act", "dve", "sp"])
            P.mark(f"L{l} b{bi} s4")
            if ON("s4"):
                s4_merge(l, t0, TB)

    for l in range(DEPTH):
        P.mark(f"L{l} ffn1")
        if "ffn1" in PHASES:
            ffn(l, W["ffn1_w_gate"], W["ffn1_w_up"], W["ffn1_w_down"], vcol(l, V_F1, 8), precast_pieces(l) if USE_PRECAST else ())
        P.barrier(["pe", "act", "dve", "sp"])
        if any(p_ in PHASES for p_ in ("mix", "mem", "s1", "s2", "s3", "s4")):
            mixer(l)
        P.barrier(["pe", "act", "dve", "sp"])
        P.mark(f"L{l} ffn2")
        if "ffn2" in PHASES:
            ffn(l, W["ffn2_w_gate"], W["ffn2_w_up"], W["ffn2_w_down"], vcol(l, V_F2, 8))
        P.barrier(["pe", "act", "dve", "sp"])

    P.force = True
    P.mark('final')
    for (t0, tb) in fblocks:
        for c in range(KC):
            P.op("act", "activation", reads=[("x", t0, c)], writes=[("sq", c)], out=sqF[:, c, 0:tb], in_=x[:, c, t0:t0 + tb], func=AF.Square)
        P.mmgroup([MM(ps[7][:, 0:tb], onesb[:], sqF[:, c, 0:tb], c == 0, c == KC - 1) for c in range(KC)], reads=["sq", "onesb"], writes=[("ps", 7)])
        P.op("act", "activation", reads=[("ps", 7), "epsc"], writes=["rstd"], out=rstd[:, 0:tb], in_=ps[7][:, 0:tb], func=AF.Sqrt, bias=EPSC[:], scale=1.0 / D)
        P.op("dve", "reciprocal", reads=["rstd"], writes=["rstd"], out=rstd[:, 0:tb], in_=rstd[:, 0:tb])
        for c in range(KC):
            k = c % 2
            P.op("dve", "scalar_tensor_tensor", reads=[("x", t0, c), "rstd", "vecs"], writes=[("yout", k)], out=youtA[:, k, 0:tb], in0=x[:, c, t0:t0 + tb],
                 scalar=vecs[:, V_FIN + c:V_FIN + c + 1], in1=rstd[:, 0:tb], op0=ALU.mult, op1=ALU.mult)
            P.dma("sp", yT_o[c * 128:(c + 1) * 128, t0:t0 + tb], youtA[:, k, 0:tb], reads=[("yout", k)], writes=[("yT_o", t0, c)])
    P.barrier(["sp"])
    with nc.Block() as block:
        P.emit(block)
    es.close()
    return nc, P


def _consts():
    c = np.zeros((128, 720), np.float32)
    c[:, 0:128] = np.triu(np.ones((128, 128), np.float32))
    c[:, 128:256] = np.eye(128, dtype=np.float32)
    for g in range(4):
        w = 2 ** (g + 1)
        for t in range(16):
            c[:, 256 + g * 16 + t] = w / min(t + 1, w)
    same = np.kron(np.eye(16, dtype=np.float32), np.ones((4, 4), np.float32))
    c[0:64, 320:384] = same * np.triu(np.ones((64, 64), np.float32))
    c[0:64, 384:448] = same
    c[:, 448:576] = 1.0
    c[0:64, 576:592] = np.kron(np.eye(16, dtype=np.float32), np.ones((4, 1), np.float32))
    return c


def _cols(v):
    return np.ascontiguousarray(np.asarray(v, np.float32).reshape(-1, 128).T)


def _pack_vecs(inp, depth):
    vecs = np.zeros((128, NVL * depth + 8), np.float32)
    rows = np.zeros((128, depth * 32), np.float32)
    for l in range(depth):
        o = l * NVL
        vecs[:, o + 0:o + 8] = _cols(inp["ffn1_norm"][l])
        vecs[:, o + 8:o + 16] = _cols(inp["mix_norm"][l])
        vecs[:, o + 16:o + 24] = _cols(inp["ffn2_norm"][l])
        vecs[:, o + 24:o + 48] = _cols(inp["gate_bias"][l])
        vecs[:, o + 48:o + 52] = _cols(inp["pool_scale"][l])
        for j in range(4):
            vecs[:, o + 52 + j * 16:o + 52 + (j + 1) * 16] = _cols(inp["conv_w"][l, j])
        vecs[:, o + 116:o + 132] = _cols(inp["conv_b"][l])
        vecs[:, o + 132:o + 140] = _cols(np.repeat(np.asarray(inp["d_skip"][l]), 64))
        vecs[:, o + 140:o + 148] = _cols(inp["ssm_norm"][l])
        vecs[:, o + 148:o + 156] = _cols(inp["mem_norm"][l])
        rows[:, l * 32:l * 32 + 16] = np.asarray(inp["dt_bias"][l])[None, :]
        rows[:, l * 32 + 16:l * 32 + 32] = np.asarray(inp["a_log"][l])[None, :]
    vecs[:, NVL * depth:NVL * depth + 8] = _cols(inp["final_norm"])
    return vecs, rows


_WNAMES = ("ffn1_w_gate", "ffn1_w_up", "ffn1_w_down", "ffn2_w_gate", "ffn2_w_up", "ffn2_w_down", "w_in", "pool_w",
           "w_mem_k", "w_mem_v", "w_br_pool", "w_br_ssm", "w_br_mem", "w_o")


def run_cores(inp, n_cores, LP, depth, phases=("ffn1", "mix", "ffn2"), runner=None):
    f32 = lambda a: np.ascontiguousarray(np.asarray(a, np.float32))
    nc, _ = build_program(LP, depth, PHASES=phases)
    vecs, rows = _pack_vecs(inp, depth)
    cst = _consts()
    shared = {nm: f32(inp[nm]) for nm in _WNAMES}
    in_maps = []
    for i in range(n_cores):
        sl = slice(i * NS, (i + 1) * NS)
        xs = f32(inp["x_sample"][sl]).reshape(NS * LS, D)
        m = dict(shared)
        m["xT"] = f32(np.concatenate([f32(inp["x_prompt"][i]), xs], axis=0).T)
        m["memT"] = f32(f32(inp["mem_prompt"][i]).T)
        m["pool_in"] = f32(np.transpose(f32(inp["state_pool"][:, sl]), (0, 3, 1, 2)))
        m["conv_in"] = f32(np.transpose(f32(inp["state_conv"][:, sl]), (0, 3, 1, 2)))
        m["ssmT_in"] = f32(np.transpose(f32(inp["state_ssm"][:, sl]).reshape(depth, NS, 1024, 128), (0, 1, 3, 2)))
        m["kT_in"] = f32(np.transpose(f32(inp["cache_mem_k"][:, sl]), (0, 1, 3, 4, 2)))
        m["v_in"] = f32(f32(inp["cache_mem_v"][:, sl]).reshape(depth, NS, 256, 512))
        m["vecs"] = vecs
        m["rows"] = rows
        m["consts"] = cst
        in_maps.append(m)
    if runner is None:
        res = bass_utils.run_bass_kernel_spmd(nc, in_maps, core_ids=list(range(n_cores))).results
    else:
        res = runner(nc, in_maps)
    B = n_cores
    y_p = np.stack([res[i]["yT"][:, :LP].T for i in range(B)])
    y_s = np.concatenate([res[i]["yT"][:, LP:].T.reshape(NS, LS, D) for i in range(B)])
    po = np.stack([res[i]["pool_o"] for i in range(B)])
    co = np.stack([res[i]["conv_o"] for i in range(B)])
    so = np.stack([res[i]["ssmT_o"] for i in range(B)])
    pool_p = np.transpose(po[:, :, :, 0, :], (1, 0, 3, 2))
    conv_p = np.transpose(co[:, :, :, 0, :], (1, 0, 3, 2))
    ssm_p = np.transpose(so[:, :, 0], (1, 0, 3, 2)).reshape(depth, B, 16, 64, 128)
    mk = np.stack([np.transpose(res[i]["mkT_o"], (0, 2, 1)).reshape(depth, 256, 4, 128) for i in range(B)], axis=1)
    mv = np.stack([res[i]["mv_o"].reshape(depth, 256, 4, 128) for i in range(B)], axis=1)
    pool_s = np.transpose(po[:, :, :, 1:, :], (1, 0, 3, 4, 2)).reshape(depth, B * NS, 15, 512)
    conv_s = np.transpose(co[:, :, :, 1:, :], (1, 0, 3, 4, 2)).reshape(depth, B * NS, 3, 2048)
    ssm_s = np.transpose(so[:, :, 1:], (1, 0, 2, 4, 3)).reshape(depth, B * NS, 16, 64, 128)
    outs = (y_p, y_s, pool_p, conv_p, ssm_p, mk, mv, pool_s, conv_s, ssm_s)
    return tuple(np.ascontiguousarray(o, dtype=np.float32) for o in outs)


def kernel(**inputs):
    return run_cores(inputs, 8, 2048, 4)
```

```python
import math
from contextlib import ExitStack
import numpy as np
import concourse.bass as bass
import concourse.mybir as mybir
import concourse.bass_utils as bass_utils
from concourse.alu_op_type import AluOpType as ALU

F32 = mybir.dt.float32
BF16 = mybir.dt.bfloat16
AF = mybir.ActivationFunctionType
AX = mybir.AxisListType

D = 1024
KC = 8
DFF = 2816
NFC = 22
NS = 16
LS = 4
EPS = 1e-6
INC = 7184
C_POOL, C_Z, C_XBC, C_DT, C_Q, C_G = 0, 512, 1536, 3584, 3600, 4112
NVL = 160
SCALE = 128 ** -0.5
import os as _os
USE_PRECAST = bool(int(_os.environ.get('K_PRECAST', '1')))
SKIP_SELF = bool(int(_os.environ.get('K_SKIP_SELF', '0')))


class Prog:
    ENGS = ("pe", "act", "dve", "pool", "sp")

    def __init__(self, nc, es):
        self.nc = nc
        self.ops = {e: [] for e in self.ENGS}
        self.sem = {e: es.enter_context(nc.semaphore("sem_" + e)) for e in self.ENGS}
        self.cnt = {e: 0 for e in self.ENGS}
        self.RING = 20
        self.dsem = {q: [es.enter_context(nc.semaphore(f"dsem_{q}{i}")) for i in range(self.RING)]
                     for q in ("pool", "sp", "act")}
        self.dval = {q: [0] * self.RING for q in self.dsem}
        self.dn = {q: 0 for q in self.dsem}
        self.semobj = {}
        self.seen = {e: {} for e in self.ENGS}
        self.state = {}
        for e in self.ENGS:
            self.semobj[id(self.sem[e])] = self.sem[e]
        for q in self.dsem:
            for s in self.dsem[q]:
                self.semobj[id(s)] = s
        self.all_tokens = {}
        import os as _os
        self.limit = int(_os.environ.get('K_LIMIT', '0'))
        self.nops = 0
        self.force = False
        self.log = []
        self.marks = []

    @staticmethod
    def _k(key):
        if isinstance(key, tuple):
            return key[0], key[1:]
        return key, "*"

    def _deps(self, key, is_write, toks, eng=None):
        name, sub = self._k(key)
        st = self.state.setdefault(name, {})
        subs = list(st.keys()) if sub == "*" else [s for s in (sub, "*") if s in st]
        for s in subs:
            e = st[s]
            if e["w"] is not None:
                toks.append(e["w"])
            if is_write:
                toks.extend(e["r"].items())
            elif name == "ps":
                own = id(self.sem[eng]) if eng in self.sem else None
                toks.extend((sid, v) for sid, v in e["r"].items() if sid != own)

    def _rec(self, key, is_write, tok):
        name, sub = self._k(key)
        st = self.state.setdefault(name, {})
        if is_write:
            if sub == "*":
                st.clear()
            st[sub] = {"w": tok, "r": {}}
        else:
            e = st.setdefault(sub, {"w": None, "r": {}})
            if e["r"].get(tok[0], 0) < tok[1]:
                e["r"][tok[0]] = tok[1]

    def _waits(self, eng, toks, skip_self):
        own = id(self.sem[eng])
        need = {}
        for sid, val in toks:
            if skip_self and sid == own:
                continue
            if need.get(sid, 0) < val:
                need[sid] = val
        for sid, val in need.items():
            if self.seen[eng].get(sid, 0) < val:
                self.seen[eng][sid] = val
                s = self.semobj[sid]
                self.ops[eng].append(lambda e, s=s, val=val: e.wait_ge(s, val))

    def op(self, eng, method, reads=(), writes=(), **kw):
        self.nops += 1
        if self.limit and self.nops > self.limit and not self.force:
            return
        self.log.append((self.nops, eng, method, [str(k) for k in writes]))
        toks = []
        ENG = eng
        for k in reads:
            self._deps(k, False, toks, ENG)
        for k in writes:
            self._deps(k, True, toks, ENG)
        self._waits(eng, toks, skip_self=(eng == "pe" or SKIP_SELF))
        self.cnt[eng] += 1
        tok = (id(self.sem[eng]), self.cnt[eng])
        s = self.sem[eng]
        self.ops[eng].append(lambda e, method=method, kw=kw, s=s: getattr(e, method)(**kw).then_inc(s, 1))
        for k in reads:
            self._rec(k, False, tok)
        for k in writes:
            self._rec(k, True, tok)
        self.all_tokens[tok[0]] = tok[1]

    def mmgroup(self, kws, reads, writes):
        self.nops += 1
        if self.limit and self.nops > self.limit and not self.force:
            return
        self.log.append((self.nops, 'pe', kws[0][0], [str(k) for k in writes]))
        toks = []
        ENG = "pe"
        for k in reads:
            self._deps(k, False, toks, ENG)
        for k in writes:
            self._deps(k, True, toks, ENG)
        self._waits("pe", toks, skip_self=True)
        self.cnt["pe"] += 1
        tok = (id(self.sem["pe"]), self.cnt["pe"])
        s = self.sem["pe"]
        self.npe_real = getattr(self, "npe_real", 0) + len(kws)
        for i, (method, kw) in enumerate(kws):
            if i == len(kws) - 1:
                self.ops["pe"].append(lambda e, method=method, kw=kw, s=s: getattr(e, method)(**kw).then_inc(s, 1))
            else:
                self.ops["pe"].append(lambda e, method=method, kw=kw: getattr(e, method)(**kw))
        for k in reads:
            self._rec(k, False, tok)
        for k in writes:
            self._rec(k, True, tok)
        self.all_tokens[tok[0]] = tok[1]

    def dma(self, q, out, in_, reads=(), writes=()):
        self.nops += 1
        if self.limit and self.nops > self.limit and not self.force:
            return
        self.log.append((self.nops, q, 'dma', [str(k) for k in writes]))
        toks = []
        ENG = q
        for k in reads:
            self._deps(k, False, toks, ENG)
        for k in writes:
            self._deps(k, True, toks, ENG)
        i = self.dn[q] % self.RING
        self.dn[q] += 1
        s = self.dsem[q][i]
        if self.dval[q][i] > 0:
            toks.append((id(s), self.dval[q][i]))
        self._waits(q, toks, skip_self=False)
        self.dval[q][i] += 16
        tok = (id(s), self.dval[q][i])
        self.ops[q].append(lambda e, out=out, in_=in_, s=s: e.dma_start(out=out, in_=in_).then_inc(s, 16))
        for k in reads:
            self._rec(k, False, tok)
        for k in writes:
            self._rec(k, True, tok)
        self.all_tokens[tok[0]] = tok[1]

    def mark(self, label):
        self.marks.append((label, getattr(self, "npe_real", 0)))

    def barrier(self, engines=None):
        toks = list(self.all_tokens.items())
        for e in (engines or self.ENGS):
            self._waits(e, toks, skip_self=True)

    def emit(self, block):
        nc = self.nc
        m = {"pe": block.tensor, "act": block.scalar, "dve": block.vector, "pool": block.gpsimd, "sp": block.sync}
        for name in self.ENGS:
            lst = self.ops[name]

            def body(e, lst=lst):
                for f in lst:
                    f(e)
            m[name](body)


def build_program(LP, DEPTH, MB=256, PHASES=("ffn1", "mix", "ffn2")):
    T = LP + NS * LS
    nc = bass.Bass("TRN2", target_bir_lowering=False)
    dt_ = lambda name, shape, kind="ExternalInput", dt=F32: nc.dram_tensor(name, shape, dt, kind=kind).ap()
    xT_d = dt_("xT", [D, T])
    memT_d = dt_("memT", [D, 256])
    pool_in = dt_("pool_in", [DEPTH, 512, NS, 15])
    conv_in = dt_("conv_in", [DEPTH, 2048, NS, 3])
    ssmT_in = dt_("ssmT_in", [DEPTH, NS, 128, 1024])
    kT_in = dt_("kT_in", [DEPTH, NS, 4, 128, 256])
    v_in = dt_("v_in", [DEPTH, NS, 256, 512])
    vecs_d = dt_("vecs", [128, NVL * DEPTH + 8])
    rows_d = dt_("rows", [128, DEPTH * 32])
    consts_d = dt_("consts", [128, 720])
    W = {}
    for nm, shp in (("ffn1_w_gate", [DEPTH, D, DFF]), ("ffn1_w_up", [DEPTH, D, DFF]), ("ffn1_w_down", [DEPTH, DFF, D]),
                    ("ffn2_w_gate", [DEPTH, D, DFF]), ("ffn2_w_up", [DEPTH, D, DFF]), ("ffn2_w_down", [DEPTH, DFF, D]),
                    ("w_in", [DEPTH, D, INC]), ("pool_w", [DEPTH, 4, 128, 128]),
                    ("w_mem_k", [DEPTH, D, 512]), ("w_mem_v", [DEPTH, D, 512]),
                    ("w_br_pool", [DEPTH, 512, D]), ("w_br_ssm", [DEPTH, 1024, D]), ("w_br_mem", [DEPTH, 512, D]),
                    ("w_o", [DEPTH, D, D])):
        W[nm] = dt_(nm, shp)
    WB = {}
    for nm, shp in (("w_in", [DEPTH, D, INC]), ("pool_w", [DEPTH, 4, 128, 128]), ("w_br_pool", [DEPTH, 512, D]), ("w_br_ssm", [DEPTH, 1024, D]),
                    ("w_br_mem", [DEPTH, 512, D]), ("w_o", [DEPTH, D, D])):
        WB[nm] = nc.dram_tensor("wb_" + nm, shp, BF16, kind="Internal").ap()
    yT_o = dt_("yT", [D, T], "ExternalOutput")
    pool_o = dt_("pool_o", [DEPTH, 512, NS + 1, 15], "ExternalOutput")
    conv_o = dt_("conv_o", [DEPTH, 2048, NS + 1, 3], "ExternalOutput")
    ssmT_o = dt_("ssmT_o", [DEPTH, NS + 1, 128, 1024], "ExternalOutput")
    mkT_o = dt_("mkT_o", [DEPTH, 512, 256], "ExternalOutput")
    mv_o = dt_("mv_o", [DEPTH, 256, 512], "ExternalOutput")

    es = ExitStack()
    sb = lambda name, shape, dt=F32: es.enter_context(nc.sbuf_tensor("s_" + name, shape, dt))
    P = Prog(nc, es)
    x = sb("x", [128, KC, T])
    vecs = sb("vecs", [128, NVL * DEPTH + 8])
    rows = sb("rows", [128, DEPTH * 32])
    cst = sb("cst", [128, 720])
    identb = sb("identb", [128, 128], BF16)
    onesb = sb("onesb", [128, 128], BF16)
    EPSC = sb("epsc", [128, 1])
    arow = sb("arow", [128, 16])
    NPAGE = 19
    PAGE = 1024
    wsl = sb("wsl", [128, NPAGE * PAGE], BF16)
    wsmall = sb("wsmall", [128, 2, 640], BF16)
    rstd = sb("rstd", [128, 512])
    KTp = sb("KTp", [128, 4, 256], BF16)
    Vp = sb("Vp", [128, 2, 512], BF16)
    halp = sb("halp", [128, 16, 3])
    hbh = sb("hbh", [128, KC, 4], BF16)
    hT = sb("hT", [128, 1024])
    hTb = sb("hTb", [128, 1024], BF16)
    UBW = max(15 + MB, NS * (15 + LS))
    ub = sb("ub", [128, 4, UBW])
    ARENA_B = 76 * 1024
    arena = sb("arena", [128, ARENA_B // 4])
    ps = [es.enter_context(nc.psum_tensor(f"ps{i}", [128, 512], F32)) for i in range(8)]
    psb = [p_[:].bitcast(BF16) for p_ in ps]

    def carve(off, shape, dt=F32):
        n = 1
        for s_ in shape[1:]:
            n *= s_
        nb = n * (4 if dt == F32 else 2)
        assert off % 4 == 0 and off + nb <= ARENA_B, (off, nb)
        v = arena[:, off // 4: (off + nb + 3) // 4]
        if dt != F32:
            v = v.bitcast(dt)[:, 0:n]
        if len(shape) == 3:
            v = v.rearrange("p (a b) -> p a b", b=shape[2])
        elif len(shape) == 4:
            v = v.rearrange("p (a b c) -> p a b c", b=shape[2], c=shape[3])
        return v, off + ((nb + 3) // 4) * 4

    o = 0
    xn, o = carve(o, [128, KC, T], BF16)
    sqF, o = carve(o, [128, KC, 512], BF16)
    hTf, o = carve(o, [128, 2, 4, 512], BF16)
    sg, o = carve(o, [128, 2, 512])
    youtA, o = carve(o, [128, 2, 512])
    o = 0
    hb, o = carve(o, [128, KC, MB], BF16)
    ypool, o = carve(o, [128, 4, MB], BF16)
    ymem, o = carve(o, [128, 4, MB], BF16)
    yssm, o = carve(o, [128, KC, MB], BF16)
    o_stage = o
    sqM, o = carve(o, [128, KC, MB], BF16)
    pa, o = carve(o, [128, UBW])
    pb_, o = carve(o, [128, UBW])
    dpl, o = carve(o, [128, 4, MB], BF16)
    qT, o = carve(o, [128, 4, MB], BF16)
    KTs, o = carve(o, [128, 2, 4, 256], BF16)
    Vs, o = carve(o, [128, 2, 2, 512], BF16)
    att_mx, o = carve(o, [128, 2, 4])
    att_pe, o = carve(o, [128, 2, 256])
    att_pb, o = carve(o, [128, 2, 256], BF16)
    att_pT, o = carve(o, [128, 2, 2, 128], BF16)
    Qexp, o = carve(o, [128, 4, 1088], BF16)
    att_pT4, o = carve(o, [128, 4, 2, 64], BF16)
    gsb, o = carve(o, [128, 2, 3, MB])
    mt, o = carve(o, [128, 2, 3, MB])
    mrgb, o = carve(o, [128, KC, MB], BF16)
    o_mem = o
    o = o_stage
    memx, o = carve(o, [128, KC, 256])
    memn, o = carve(o, [128, KC, 256], BF16)
    sqX, o = carve(o, [128, KC, 256], BF16)
    kst, o = carve(o, [128, 4, 256])
    vst, o = carve(o, [128, 2, 512])
    o = o_stage
    siluz, o = carve(o, [128, KC, MB])
    cbuf, o = carve(o, [128, 2, NS * (3 + LS) if NS * (3 + LS) > 3 + MB else 3 + MB])
    cacc, o = carve(o, [128, 2, MB])
    xsT, o = carve(o, [128, KC, MB])
    BT, o = carve(o, [128, 4, MB], BF16)
    CT, o = carve(o, [128, 4, MB], BF16)
    dtb, o = carve(o, [128, 2, 16])
    lab, o = carve(o, [128, 2, 16])
    acum, o = carve(o, [128, 2, 16])
    nacum, o = carve(o, [128, 2, 16])
    te, o = carve(o, [128, 2, 16])
    cdr, o = carve(o, [128, 2, 16])
    xdt, o = carve(o, [128, 2, 1024], BF16)
    Btok, o = carve(o, [128, 2, 512], BF16)
    seg, o = carve(o, [128, 2, 4, 128])
    Dm, o = carve(o, [128, 2, 16, 128], BF16)
    cbm, o = carve(o, [128, 2, 4, 128], BF16)
    fsr, o = carve(o, [128, 2, 16, 128], BF16)
    y2, o = carve(o, [128, 2, 128])
    ysq, o = carve(o, [128, 2, 128], BF16)
    yrs, o = carve(o, [128, 128])
    htmp, o = carve(o, [128, 1024])
    cdrS, o = carve(o, [128, 16, 16])
    Bmk, o = carve(o, [128, 2, 512], BF16)

    TRI = cst[:, 0:128]
    IDF = cst[:, 128:256]
    FIX = cst[:, 256:320]
    TRIBD = cst[:, 320:384]
    SAMEBD = cst[:, 384:448]
    ONESF = cst[:, 448:576]
    SEQM = cst[:, 576:592]

    def vcol(l, off, n=1):
        return vecs[:, l * NVL + off: l * NVL + off + n]
    V_F1, V_MIX, V_F2, V_GB, V_PS, V_CW, V_CB, V_DS, V_SN, V_MN = 0, 8, 16, 24, 48, 52, 116, 132, 140, 148
    V_FIN = NVL * DEPTH

    wctr = [0]

    def wload(src_ap, a, b, rk=()):
        n = a * b
        npg = (n + PAGE - 1) // PAGE
        p0 = wctr[0]
        if p0 + npg > NPAGE:
            p0 = 0
        wctr[0] = (p0 + npg) % NPAGE
        keys = [("wsl", p) for p in range(p0, p0 + npg)]
        view = wsl[:, p0 * PAGE:p0 * PAGE + n].rearrange("p (a b) -> p a b", b=b)
        P.dma("pool", view, src_ap, reads=list(rk), writes=keys)
        return view, keys

    def kcp(src):
        return src.rearrange("(c p) n -> p c n", p=128)

    def MM(out, lhsT, rhs, start=True, stop=True):
        return ("matmul", dict(out=out, lhsT=lhsT, rhs=rhs, start=start, stop=stop))

    def TR(out, in_, identity):
        return ("transpose", dict(out=out, in_=in_, identity=identity))

    def rmsnorm(dst, dkeyf, src, skeyf, tb, gcol, sq, inv_n=1.0 / D, nchunk=KC):
        for c in range(nchunk):
            P.op("act", "activation", reads=[skeyf(c)], writes=[("sq", c)], out=sq[:, c, 0:tb], in_=src[c], func=AF.Square)
        P.mmgroup([MM(ps[7][:, 0:tb], onesb[:], sq[:, c, 0:tb], c == 0, c == nchunk - 1) for c in range(nchunk)],
                  reads=["sq", "onesb"], writes=[("ps", 7)])
        P.op("act", "activation", reads=[("ps", 7), "epsc"], writes=["rstd"], out=rstd[:, 0:tb], in_=ps[7][:, 0:tb], func=AF.Sqrt, bias=EPSC[:], scale=inv_n)
        P.op("dve", "reciprocal", reads=["rstd"], writes=["rstd"], out=rstd[:, 0:tb], in_=rstd[:, 0:tb])
        for c in range(nchunk):
            P.op("dve", "scalar_tensor_tensor", reads=[skeyf(c), "rstd", "vecs"], writes=[dkeyf(c)],
                 out=dst[c], in0=src[c], scalar=gcol[:, c:c + 1], in1=rstd[:, 0:tb], op0=ALU.mult, op1=ALU.mult)

    P.dma("sp", vecs[:], vecs_d[:, :], writes=["vecs"])
    P.dma("sp", rows[:], rows_d[:, :], writes=["rows"])
    P.dma("sp", cst[:], consts_d[:, :], writes=["cst"])
    P.dma("sp", x[:], xT_d.rearrange("(c p) t -> p c t", p=128), writes=["x"])
    P.op("dve", "tensor_copy", reads=["cst"], writes=["identb"], out=identb[:], in_=IDF)
    P.op("dve", "memset", writes=["onesb"], ap=onesb[:], constant=1.0)
    P.op("dve", "memset", writes=["epsc"], ap=EPSC[:], constant=EPS)

    fblocks = [(t, min(512, LP - t)) for t in range(0, LP, 512)] + [(LP, NS * LS)]
    groups = [(0, 4), (4, 4), (8, 4), (12, 4), (16, 4), (20, 2)]

    def precast_pieces(l):
        pcs = []
        for r in range(8):
            pcs.append((WB["w_in"][l, r * 128:(r + 1) * 128, :].rearrange("p (a b) -> p a b", a=4), W["w_in"][l, r * 128:(r + 1) * 128, :].rearrange("p (a b) -> p a b", a=4)))
        for nm, nrow in (("w_br_pool", 512), ("w_br_ssm", 1024), ("w_br_mem", 512), ("w_o", 1024)):
            for r in range(0, nrow, 512):
                pcs.append((WB[nm][l, r:r + 512, :], W[nm][l, r:r + 512, :]))
        pcs.append((WB["pool_w"][l].rearrange("g c d -> (g c) d"), W["pool_w"][l].rearrange("g c d -> (g c) d")))
        return pcs

    def ffn(l, wg_d, wu_d, wd_d, gcol, pcs=()):
        pcs = list(pcs)
        def norm_blk(bi):
            t0, tb = fblocks[bi]
            rmsnorm([xn[:, c, t0:t0 + tb] for c in range(KC)], lambda c, t0=t0: ("xn", t0),
                    [x[:, c, t0:t0 + tb] for c in range(KC)], lambda c, t0=t0: ("x", t0, c), tb, gcol, sqF)
        norm_blk(0)
        normed = 1
        pi = 0
        for (f0, G) in groups:
            wgs, wus, wds = [], [], []
            for ci in range(G):
                f = f0 + ci
                wgs.append(wload(kcp(wg_d[l, :, f * 128:(f + 1) * 128]), KC, 128))
                wus.append(wload(kcp(wu_d[l, :, f * 128:(f + 1) * 128]), KC, 128))
                wds.append(wload(wd_d[l, f * 128:(f + 1) * 128, :].rearrange("p (o n) -> p o n", o=1), 1, D))
            for _ in range(3):
                if pcs:
                    o_, i_ = pcs.pop(0)
                    P.dma("pool", o_, i_, writes=[("wb%d" % l, len(pcs))])
            for bi_, (t0, tb) in enumerate(fblocks):
                par = pi % 2
                pi += 1
                if normed < len(fblocks) and bi_ + 1 == normed:
                    norm_blk(normed)
                    normed += 1
                for ci in range(G):
                    bg, bu = (0, 1) if (ci % 2 == 0) else (2, 3)
                    wg, kg = wgs[ci]
                    wu, ku = wus[ci]
                    P.mmgroup([MM(ps[bg][:, 0:tb], wg[:, c, :], xn[:, c, t0:t0 + tb], c == 0, c == KC - 1) for c in range(KC)],
                              reads=kg + [("xn", t0)], writes=[("ps", bg)])
                    P.mmgroup([MM(ps[bu][:, 0:tb], wu[:, c, :], xn[:, c, t0:t0 + tb], c == 0, c == KC - 1) for c in range(KC)],
                              reads=ku + [("xn", t0)], writes=[("ps", bu)])
                    P.op("act", "activation", reads=[("ps", bg)], writes=[("sg", ci % 2)], out=sg[:, ci % 2, 0:tb], in_=ps[bg][:, 0:tb], func=AF.Silu)
                    P.op("dve", "tensor_tensor", reads=[("ps", bu), ("sg", ci % 2)], writes=[("hTf", par, ci)],
                         out=hTf[:, par, ci, 0:tb], in0=ps[bu][:, 0:tb], in1=sg[:, ci % 2, 0:tb], op=ALU.mult)
                for dc in range(KC):
                    by = 4 + dc % 3
                    P.mmgroup([MM(ps[by][:, 0:tb], wds[ci][0][:, 0, dc * 128:(dc + 1) * 128], hTf[:, par, ci, 0:tb], ci == 0, ci == G - 1) for ci in range(G)],
                              reads=sum([wds[ci][1] for ci in range(G)], []) + [("hTf", par, ci) for ci in range(G)], writes=[("ps", by)])
                    P.op("dve", "scalar_tensor_tensor", reads=[("ps", by), ("x", t0, dc)], writes=[("x", t0, dc)],
                         out=x[:, dc, t0:t0 + tb], in0=ps[by][:, 0:tb], scalar=0.5, in1=x[:, dc, t0:t0 + tb], op0=ALU.mult, op1=ALU.add)

    WSRC = WB if (USE_PRECAST and 'ffn1' in PHASES) else W
    WRK = (lambda l: ["wb%d" % l]) if (USE_PRECAST and 'ffn1' in PHASES) else (lambda l: [])

    def memory_stage(l):
        P.dma("sp", memx[:], memT_d.rearrange("(c p) t -> p c t", p=128), writes=["memx"])
        rmsnorm([memn[:, c, :] for c in range(KC)], lambda c: ("memn", c), [memx[:, c, :] for c in range(KC)], lambda c: "memx",
                256, vcol(l, V_MN, 8), sqX)
        wk, kk = wload(kcp(W["w_mem_k"][l]), KC, 512)
        wv, kv = wload(kcp(W["w_mem_v"][l]), KC, 512)
        for hd in range(4):
            P.mmgroup([MM(ps[hd][:, 0:256], wk[:, c, hd * 128:(hd + 1) * 128], memn[:, c, :], c == 0, c == KC - 1) for c in range(KC)],
                      reads=kk + ["memn"], writes=[("ps", hd)])
            P.op("act", "activation", reads=[("ps", hd)], writes=[("kst", hd)], out=kst[:, hd, :], in_=ps[hd][:, 0:256], func=AF.Copy)
            P.op("dve", "tensor_copy", reads=[("ps", hd)], writes=[("KTp", hd)], out=KTp[:, hd, :], in_=ps[hd][:, 0:256])
        P.dma("sp", mkT_o[l].rearrange("(h p) m -> p h m", p=128), kst[:], reads=["kst"], writes=[("mkT_o", l)])
        for mc in range(2):
            P.mmgroup([MM(ps[4 + mc][:, 0:512], memn[:, c, mc * 128:(mc + 1) * 128], wv[:, c, :], c == 0, c == KC - 1) for c in range(KC)],
                      reads=kv + ["memn"], writes=[("ps", 4 + mc)])
            P.op("act", "activation", reads=[("ps", 4 + mc)], writes=[("vst", mc)], out=vst[:, mc, :], in_=ps[4 + mc][:, 0:512], func=AF.Copy)
            P.op("dve", "tensor_copy", reads=[("ps", 4 + mc)], writes=[("Vp", mc)], out=Vp[:, mc, :], in_=ps[4 + mc][:, 0:512])
        P.dma("sp", mv_o[l].rearrange("(c p) n -> p c n", p=128), vst[:], reads=["vst"], writes=[("mv_o", l)])
        P.op("act", "activation", reads=["rows"], writes=["arow"], out=arow[:], in_=rows[:, l * 32 + 16:l * 32 + 32], func=AF.Exp)
        P.op("dve", "tensor_scalar", reads=["arow"], writes=["arow"], out=arow[:], in0=arow[:], scalar1=-1.0, scalar2=None, op0=ALU.mult)

    def v3(ap2, nseq, L):
        return ap2.rearrange("p (s j) -> p s j", j=L)

    def s1_pool(l, kind, first, last, nseq, L, TB):
        wp, kp = wload(kcp(WSRC["w_in"][l, :, C_POOL:C_POOL + 512]), KC, 512, WRK(l))
        P.dma("pool", wsmall[:, 0, 0:512].rearrange("p (g d) -> p g d", d=128), WSRC["pool_w"][l].rearrange("g c d -> c g d"), reads=WRK(l), writes=[("wsmall", 0)])
        pw = wsmall[:, 0, 0:512].rearrange("p (g d) -> p g d", d=128)
        HL = 15 + L

        def U(g):
            return v3(ub[:, g, 0:nseq * HL], nseq, HL)
        A = v3(pa[:, 0:nseq * HL], nseq, HL)
        Bv = v3(pb_[:, 0:nseq * HL], nseq, HL)
        if kind == "p":
            if first:
                P.op("dve", "memset", writes=["ub"], ap=ub[:, :, 0:15], constant=0.0)
            else:
                P.op("dve", "tensor_copy", reads=["ub"], writes=["ub"], out=ub[:, :, 0:15], in_=ub[:, :, MB:MB + 15])
        else:
            for g in range(4):
                P.dma("sp", U(g)[:, :, 0:15], pool_in[l, g * 128:(g + 1) * 128, :, :], reads=[], writes=["ub"])
        for g in range(4):
            P.mmgroup([MM(ps[g][:, 0:TB], wp[:, c, g * 128:(g + 1) * 128], hb[:, c, 0:TB], c == 0, c == KC - 1) for c in range(KC)],
                      reads=kp + ["hb"], writes=[("ps", g)])
            P.op("act", "activation", reads=[("ps", g), "ub"], writes=[("ub", g)], out=U(g)[:, :, 15:15 + L], in_=v3(ps[g][:, 0:TB], nseq, L), func=AF.Copy)
        for g in range(4):
            w = 2 ** (g + 1)
            cur, ckey = U(g), ("ub", g)
            bufs = [(A, "pa"), (Bv, "pb")]
            for k in range(1, g + 2):
                lo = 2 ** k - 1
                sh = 2 ** (k - 1)
                nb, nkey = bufs[k % 2]
                P.op("dve", "tensor_tensor", reads=[ckey], writes=[nkey], out=nb[:, :, lo:HL], in0=cur[:, :, lo:HL], in1=cur[:, :, lo - sh:HL - sh], op=ALU.add)
                cur, ckey = nb, nkey
            if kind == "p" and first:
                P.op("dve", "tensor_tensor", reads=[ckey, "cst"], writes=[ckey], out=cur[:, 0, 15:30], in0=cur[:, 0, 15:30], in1=FIX[:, g * 16:g * 16 + 15], op=ALU.mult)
            P.op("dve", "scalar_tensor_tensor", reads=[ckey, ("ub", g)], writes=[("dpl", g)], out=v3(dpl[:, g, 0:TB], nseq, L),
                 in0=cur[:, :, 15:HL], scalar=1.0 / w, in1=U(g)[:, :, 15:HL], op0=ALU.mult, op1=ALU.subtract)
        for g in range(4):
            b_ = 4 + g % 2
            P.mmgroup([MM(ps[b_][:, 0:TB], pw[:, g, :], dpl[:, g, 0:TB])], reads=[("wsmall", 0), ("dpl", g)], writes=[("ps", b_)])
            P.op("act", "activation", reads=[("ps", b_), "vecs"], writes=[("ypool", g)], out=ypool[:, g, 0:TB], in_=ps[b_][:, 0:TB], func=AF.Identity,
                 scale=vcol(l, V_PS + g))
        if kind == "p" and last:
            for g in range(4):
                P.dma("sp", pool_o[l, g * 128:(g + 1) * 128, 0, :], ub[:, g, MB:MB + 15], reads=[("ub", g)], writes=[("pool_o", l, g, 0)])
        if kind == "s":
            for g in range(4):
                P.dma("sp", pool_o[l, g * 128:(g + 1) * 128, 1:NS + 1, :], U(g)[:, :, LS:LS + 15], reads=[("ub", g)], writes=[("pool_o", l, g, 1)])

    def s2_sample_batched(l):
        Lt = NS * LS
        for hd in range(4):
            P.op("dve", "memset", writes=[("Qexp", hd)], ap=Qexp[:, hd, :], constant=0.0)
            P.op("dve", "tensor_copy", reads=[("qT", hd)], writes=[("Qexp", hd)], out=Qexp[:, hd, :].rearrange("p (b s) -> p b s", s=68)[:, :, 0:LS],
                 in_=qT[:, hd, 0:Lt].rearrange("p (b j) -> p b j", j=LS))
        for b in range(NS):
            sl = b % 2
            P.dma("pool", KTs[:, sl], kT_in[l, b].rearrange("h d m -> d h m"), writes=[("KTs", sl)])
            for hd in range(4):
                P.mmgroup([("matmul", dict(out=ps[hd][0:Lt, 0:256], lhsT=Qexp[:, hd, b * 64:(b + 1) * 64], rhs=KTs[:, sl, hd, :], start=(b == 0), stop=(b == NS - 1)))],
                          reads=[("Qexp", hd), ("KTs", sl)], writes=[("ps", hd)])
        for hd in range(4):
            par = hd % 2
            sc = ps[hd][0:Lt, 0:256]
            mxk = ("att_mx", par)
            P.op("dve", "reduce_max", reads=[("ps", hd)], writes=[mxk], out=att_mx[0:Lt, par, 1:2], in_=sc, axis=AX.X, negate=True)
            P.op("act", "activation", reads=[("ps", hd), mxk], writes=[("att_pe", par), mxk], out=att_pe[0:Lt, par, :], in_=sc, func=AF.Exp,
                 bias=att_mx[0:Lt, par, 1:2], scale=1.0, accum_out=att_mx[0:Lt, par, 2:3])
            P.op("dve", "reciprocal", reads=[mxk], writes=[mxk], out=att_mx[0:Lt, par, 3:4], in_=att_mx[0:Lt, par, 2:3])
            P.op("dve", "tensor_scalar", reads=[("att_pe", par), mxk], writes=[("att_pb", par)], out=att_pb[0:Lt, par, :], in0=att_pe[0:Lt, par, :],
                 scalar1=att_mx[0:Lt, par, 3:4], scalar2=None, op0=ALU.mult)
            P.mmgroup([TR(psb[6][:, mc * 128:mc * 128 + Lt], att_pb[0:Lt, par, mc * 128:(mc + 1) * 128], identb[0:Lt, 0:Lt]) for mc in range(2)],
                      reads=[("att_pb", par), "identb"], writes=[("ps", 6)])
            P.op("act", "activation", reads=[("ps", 6)], writes=[("att_pT4", hd)], out=att_pT4[:, hd, :, 0:Lt],
                 in_=psb[6][:, 0:256].rearrange("p (m t) -> p m t", t=128)[:, :, 0:Lt], func=AF.Copy)
        for b in range(NS):
            sl = b % 2
            P.dma("pool", Vs[:, sl], kcp(v_in[l, b]), writes=[("Vs", sl)])
            kws = []
            for hd in range(4):
                for mc in range(2):
                    kws.append(("matmul", dict(out=ps[7][:, hd * 64 + b * LS:hd * 64 + (b + 1) * LS], lhsT=Vs[:, sl, mc, hd * 128:(hd + 1) * 128],
                                               rhs=att_pT4[:, hd, mc, b * LS:(b + 1) * LS], start=(mc == 0), stop=(mc == 1), skip_group_check=True)))
            P.mmgroup(kws, reads=[("Vs", sl), "att_pT4"], writes=[("ps", 7)])
        P.op("dve", "tensor_copy", reads=[("ps", 7)], writes=["ymem"], out=ymem[:, :, 0:Lt], in_=ps[7][:, 0:4 * 64].rearrange("p (h t) -> p h t", t=64))

    def s2_attn(l, kind, nseq, L, TB):
        wq, kq = wload(kcp(WSRC["w_in"][l, :, C_Q:C_Q + 512]), KC, 512, WRK(l))
        for hd in range(4):
            P.mmgroup([MM(ps[hd][:, 0:TB], wq[:, c, hd * 128:(hd + 1) * 128], hb[:, c, 0:TB], c == 0, c == KC - 1) for c in range(KC)],
                      reads=kq + ["hb"], writes=[("ps", hd)])
            P.op("dve", "tensor_scalar", reads=[("ps", hd)], writes=[("qT", hd)], out=qT[:, hd, 0:TB], in0=ps[hd][:, 0:TB], scalar1=SCALE, scalar2=None, op0=ALU.mult)
        it = 0
        if kind == "s":
            s2_sample_batched(l)
            return
        tiles = [(None, c0, 128) for c0 in range(0, TB, 128)]
        for (b, c0, Lt) in tiles:
            if b is None:
                KT, Vv, kkey, vkey = KTp, Vp, "KTp", "Vp"
            else:
                sl = b % 2
                P.dma("pool", KTs[:, sl], kT_in[l, b].rearrange("h d m -> d h m"), writes=[("KTs", sl)])
                P.dma("pool", Vs[:, sl], kcp(v_in[l, b]), writes=[("Vs", sl)])
                KT, Vv, kkey, vkey = KTs[:, sl], Vs[:, sl], ("KTs", sl), ("Vs", sl)
            for hd in range(4):
                par = it % 2
                it += 1
                sc = ps[4 + par][0:Lt, 0:256]
                P.mmgroup([MM(sc, qT[:, hd, c0:c0 + Lt], KT[:, hd, :])], reads=[("qT", hd), kkey], writes=[("ps", 4 + par)])
                mxk = ("att_mx", par)
                P.op("dve", "reduce_max", reads=[("ps", 4 + par)], writes=[mxk], out=att_mx[0:Lt, par, 1:2], in_=sc, axis=AX.X, negate=True)
                P.op("act", "activation", reads=[("ps", 4 + par), mxk], writes=[("att_pe", par), mxk], out=att_pe[0:Lt, par, :], in_=sc, func=AF.Exp,
                     bias=att_mx[0:Lt, par, 1:2], scale=1.0, accum_out=att_mx[0:Lt, par, 2:3])
                P.op("dve", "reciprocal", reads=[mxk], writes=[mxk], out=att_mx[0:Lt, par, 3:4], in_=att_mx[0:Lt, par, 2:3])
                P.op("dve", "tensor_scalar", reads=[("att_pe", par), mxk], writes=[("att_pb", par)], out=att_pb[0:Lt, par, :], in0=att_pe[0:Lt, par, :],
                     scalar1=att_mx[0:Lt, par, 3:4], scalar2=None, op0=ALU.mult)
                P.mmgroup([TR(psb[6][:, mc * 128:mc * 128 + Lt], att_pb[0:Lt, par, mc * 128:(mc + 1) * 128], identb[0:Lt, 0:Lt]) for mc in range(2)],
                          reads=[("att_pb", par), "identb"], writes=[("ps", 6)])
                P.op("act", "activation", reads=[("ps", 6)], writes=[("att_pT", par)], out=att_pT[:, par, :, 0:Lt],
                     in_=psb[6][:, 0:256].rearrange("p (m t) -> p m t", t=128)[:, :, 0:Lt], func=AF.Copy)
                P.mmgroup([MM(ps[7][:, 0:Lt], Vv[:, mc, hd * 128:(hd + 1) * 128], att_pT[:, par, mc, 0:Lt], mc == 0, mc == 1) for mc in range(2)],
                          reads=[vkey, ("att_pT", par)], writes=[("ps", 7)])
                P.op("dve", "tensor_copy", reads=[("ps", 7)], writes=[("ymem", hd)], out=ymem[:, hd, c0:c0 + Lt], in_=ps[7][:, 0:Lt])

    def ycol(j, Lc):
        return (0, j * 64) if Lc <= 64 else (j // 4, (j % 4) * 128)

    def chunk_A(l, c0, Lc, par, TRIm, SAMEm, sample):
        pk = lambda n: (n, par)
        wdt = wsmall[:, 1, 0:128].rearrange("p (c n) -> p c n", n=16)
        P.mmgroup([MM(ps[3][0:Lc, 16:32], hb[:, c, c0:c0 + Lc], wdt[:, c, :], c == 0, c == KC - 1) for c in range(KC)],
                  reads=["hb", ("wsmall", 1)], writes=[("ps", 3)])
        P.op("dve", "tensor_tensor", reads=[("ps", 3), "rows"], writes=[pk("dtb")], out=dtb[0:Lc, par, :], in0=ps[3][0:Lc, 16:32], in1=rows[0:Lc, l * 32:l * 32 + 16], op=ALU.add)
        P.op("act", "activation", reads=[pk("dtb")], writes=[pk("dtb")], out=dtb[0:Lc, par, :], in_=dtb[0:Lc, par, :], func=AF.Exp)
        P.op("act", "activation", reads=[pk("dtb")], writes=[pk("dtb")], out=dtb[0:Lc, par, :], in_=dtb[0:Lc, par, :], func=AF.Ln, bias=1.0)
        P.op("dve", "tensor_tensor", reads=[pk("dtb"), "arow"], writes=[pk("lab")], out=lab[0:Lc, par, :], in0=dtb[0:Lc, par, :], in1=arow[0:Lc, :], op=ALU.mult)
        P.mmgroup([MM(ps[3][0:Lc, 0:16], TRIm[0:Lc, 0:Lc], lab[0:Lc, par, :]), MM(ps[3][0:Lc, 32:48], SAMEm[0:Lc, 0:Lc], lab[0:Lc, par, :])],
                  reads=["cst", pk("lab")], writes=[("ps", 3)])
        P.op("dve", "tensor_scalar", reads=[("ps", 3)], writes=[pk("nacum")], out=nacum[0:Lc, par, :], in0=ps[3][0:Lc, 0:16], scalar1=-1.0, scalar2=None, op0=ALU.mult)
        P.op("dve", "tensor_tensor", reads=[("ps", 3), pk("nacum")], writes=[pk("te")], out=te[0:Lc, par, :], in0=ps[3][0:Lc, 32:48], in1=nacum[0:Lc, par, :], op=ALU.add)
        P.op("act", "activation", reads=[pk("te")], writes=[pk("te")], out=te[0:Lc, par, :], in_=te[0:Lc, par, :], func=AF.Exp)
        for hq in range(4):
            P.mmgroup([MM(ps[4 + hq][:, hh * 128:hh * 128 + Lc], lab[0:Lc, par, hq * 4 + hh:hq * 4 + hh + 1].broadcast_to([Lc, 128]), TRIm[0:Lc, 0:Lc]) for hh in range(4)],
                      reads=[pk("lab"), "cst"], writes=[("ps", 4 + hq)])
        nb = 1 if Lc <= 64 else 2
        for bk in range(2):
            P.mmgroup([TR(ps[bk][0:Lc, jj * 128:(jj + 1) * 128], xsT[:, bk * 4 + jj, c0:c0 + Lc], IDF) for jj in range(4)],
                      reads=["xsT", "cst"], writes=[("ps", bk)])
            P.op("dve", "tensor_tensor", reads=[("ps", bk), pk("dtb")], writes=[pk("xdt")], out=xdt[0:Lc, par, bk * 512:(bk + 1) * 512].rearrange("p (h q) -> p h q", q=64),
                 in0=ps[bk][0:Lc, :].rearrange("p (h q) -> p h q", q=64), in1=dtb[0:Lc, par, bk * 8:(bk + 1) * 8].unsqueeze(2).broadcast_to([Lc, 8, 64]), op=ALU.mult)
        P.mmgroup([TR(psb[2][0:Lc, g * 128:(g + 1) * 128], BT[:, g, c0:c0 + Lc], identb[:]) for g in range(4)], reads=["BT", "identb"], writes=[("ps", 2)])
        P.op("act", "activation", reads=[("ps", 2)], writes=[pk("Btok")], out=Btok[0:Lc, par, :], in_=psb[2][0:Lc, 0:512], func=AF.Copy)
        P.mmgroup([MM(ps[2][0:Lc, g * 128:g * 128 + Lc], BT[:, g, c0:c0 + Lc], CT[:, g, c0:c0 + Lc]) for g in range(4)], reads=["BT", "CT"], writes=[("ps", 2)])
        for g in range(4):
            P.op("dve", "tensor_tensor", reads=[("ps", 2), "cst"], writes=[("cbm", par, g)], out=cbm[0:Lc, par, g, 0:Lc], in0=ps[2][0:Lc, g * 128:g * 128 + Lc], in1=TRIm[0:Lc, 0:Lc], op=ALU.mult)
        for hq in range(4):
            sp_ = hq % 2
            for hh in range(4):
                h = hq * 4 + hh
                P.op("dve", "tensor_scalar", reads=[("ps", 4 + hq), pk("nacum")], writes=[("seg", sp_)], out=seg[0:Lc, sp_, hh, 0:Lc], in0=ps[4 + hq][0:Lc, hh * 128:hh * 128 + Lc],
                     scalar1=nacum[0:Lc, par, h:h + 1], scalar2=0.0, op0=ALU.add, op1=ALU.min)
            P.op("act", "activation", reads=[("seg", sp_)], writes=[("Dm", par, hq)], out=Dm[0:Lc, par, hq * 4:(hq + 1) * 4, 0:Lc], in_=seg[0:Lc, sp_, :, 0:Lc], func=AF.Exp)
            pv = ps[4 + hq][:, :].rearrange("p (h t) -> p h t", t=128)
            P.op("act", "activation", reads=[("ps", 4 + hq)], writes=[("fsr", par, hq)], out=fsr[:, par, hq * 4:(hq + 1) * 4, 0:Lc], in_=pv[:, :, 0:Lc], func=AF.Exp)
            if not sample:
                P.op("act", "activation", reads=[("ps", 4 + hq)], writes=[pk("cdr")], out=cdr[:, par, hq * 4:(hq + 1) * 4], in_=pv[:, :, Lc - 1], func=AF.Exp)
            else:
                P.op("act", "activation", reads=[("ps", 4 + hq)], writes=["cdrS"], out=cdrS[:, :, hq * 4:(hq + 1) * 4].rearrange("p b h -> p h b"),
                     in_=pv[:, :, LS - 1:Lc:LS], func=AF.Exp)
        for g in range(4):
            P.op("dve", "tensor_tensor", reads=[("Dm", par, g), ("cbm", par, g)], writes=[("Dm", par, g)], out=Dm[0:Lc, par, g * 4:(g + 1) * 4, 0:Lc], in0=Dm[0:Lc, par, g * 4:(g + 1) * 4, 0:Lc],
                 in1=cbm[0:Lc, par, g, 0:Lc].unsqueeze(1).broadcast_to([Lc, 4, Lc]), op=ALU.mult)
            P.op("dve", "tensor_tensor", reads=[("fsr", par, g), "CT"], writes=[("fsr", par, g)], out=fsr[:, par, g * 4:(g + 1) * 4, 0:Lc], in0=fsr[:, par, g * 4:(g + 1) * 4, 0:Lc],
                 in1=CT[:, g, c0:c0 + Lc].unsqueeze(1).broadcast_to([128, 4, Lc]), op=ALU.mult)

    def gating(l, c0, Lc):
        for g in range(4):
            for jj in range(2):
                j = 2 * g + jj
                bk, cl = ycol(j, Lc)
                yps = ps[bk][:, cl:cl + Lc]
                P.op("dve", "scalar_tensor_tensor", reads=["xsT", ("ps", bk), "vecs"], writes=[("y2", jj)], out=y2[:, jj, 0:Lc], in0=xsT[:, j, c0:c0 + Lc],
                     scalar=vcol(l, V_DS + j), in1=yps, op0=ALU.mult, op1=ALU.add)
                P.op("dve", "tensor_tensor", reads=[("y2", jj), "siluz"], writes=[("y2", jj)], out=y2[:, jj, 0:Lc], in0=y2[:, jj, 0:Lc], in1=siluz[:, j, c0:c0 + Lc], op=ALU.mult)
                P.op("act", "activation", reads=[("y2", jj)], writes=[("ysq", jj)], out=ysq[:, jj, 0:Lc], in_=y2[:, jj, 0:Lc], func=AF.Square)
            P.mmgroup([MM(ps[3][:, 128:128 + Lc], onesb[:], ysq[:, jj, 0:Lc], jj == 0, jj == 1) for jj in range(2)], reads=["ysq", "onesb"], writes=[("ps", 3)])
            P.op("act", "activation", reads=[("ps", 3), "epsc"], writes=["yrs"], out=yrs[:, 0:Lc], in_=ps[3][:, 128:128 + Lc], func=AF.Sqrt, bias=EPSC[:], scale=1.0 / 256)
            P.op("dve", "reciprocal", reads=["yrs"], writes=["yrs"], out=yrs[:, 0:Lc], in_=yrs[:, 0:Lc])
            for jj in range(2):
                j = 2 * g + jj
                P.op("dve", "scalar_tensor_tensor", reads=[("y2", jj), "yrs", "vecs"], writes=[("yssm", j)], out=yssm[:, j, c0:c0 + Lc], in0=y2[:, jj, 0:Lc],
                     scalar=vcol(l, V_SN + j), in1=yrs[:, 0:Lc], op0=ALU.mult, op1=ALU.mult)

    def chunk_B_prompt(l, c0, Lc, par, init, seq_out):
        if init == "zero":
            P.op("dve", "memset", writes=["hT"], ap=hT[:], constant=0.0)
            P.op("dve", "memset", writes=["hTb"], ap=hTb[:], constant=0.0)
        for j in range(8):
            kws = []
            bk, cl = ycol(j, Lc)
            for hh in range(2):
                h = 2 * j + hh
                outp = ps[bk][64 * hh:64 * hh + 64, cl:cl + Lc]
                kws.append(MM(outp, hTb[:, h * 64:(h + 1) * 64], fsr[:, par, h, 0:Lc], True, False))
                kws.append(MM(outp, xdt[0:Lc, par, h * 64:(h + 1) * 64], Dm[0:Lc, par, h, 0:Lc], False, True))
            P.mmgroup(kws, reads=["hTb", ("fsr", par, j // 2), ("xdt", par), ("Dm", par, j // 2)], writes=[("ps", bk)])
        gating(l, c0, Lc)
        P.op("dve", "tensor_tensor", reads=[("xdt", par), ("te", par)], writes=[("xdt", par)], out=xdt[0:Lc, par, :].rearrange("p (h q) -> p h q", q=64),
             in0=xdt[0:Lc, par, :].rearrange("p (h q) -> p h q", q=64), in1=te[0:Lc, par, :].unsqueeze(2).broadcast_to([Lc, 16, 64]), op=ALU.mult)
        for g in range(4):
            P.mmgroup([MM(ps[4 + g // 2][:, (g % 2) * 256:(g % 2 + 1) * 256], Btok[0:Lc, par, g * 128:(g + 1) * 128], xdt[0:Lc, par, g * 256:(g + 1) * 256])],
                      reads=[("Btok", par), ("xdt", par)], writes=[("ps", 4 + g // 2)])
        P.op("dve", "tensor_tensor", reads=["hT", ("cdr", par)], writes=["htmp"], out=htmp[:, :].rearrange("p (h q) -> p h q", q=64), in0=hT[:, :].rearrange("p (h q) -> p h q", q=64),
             in1=cdr[:, par, :].unsqueeze(2).broadcast_to([128, 16, 64]), op=ALU.mult)
        for bk in range(2):
            P.op("dve", "tensor_tensor", reads=["htmp", ("ps", 4 + bk)], writes=["hT"], out=hT[:, bk * 512:(bk + 1) * 512], in0=ps[4 + bk][:, :], in1=htmp[:, bk * 512:(bk + 1) * 512], op=ALU.add)
        P.op("act", "activation", reads=["hT"], writes=["hTb"], out=hTb[:], in_=hT[:], func=AF.Copy)
        if seq_out is not None:
            P.dma("sp", ssmT_o[l, seq_out], hT[:], reads=["hT"], writes=[("ssmT_o", l, seq_out)])

    def chunk_B_sample(l):
        Lc = NS * LS
        par = 0
        P.op("dve", "tensor_tensor", reads=[("xdt", 0), ("te", 0)], writes=[("xdt", 1)], out=xdt[0:Lc, 1, :].rearrange("p (h q) -> p h q", q=64),
             in0=xdt[0:Lc, 0, :].rearrange("p (h q) -> p h q", q=64), in1=te[0:Lc, 0, :].unsqueeze(2).broadcast_to([Lc, 16, 64]), op=ALU.mult)
        hbuf = [hT[:, :], seg[:, :, :, :].rearrange("p a b c -> p (a b c)")]
        hkey = [["hT"], ["hT2", ("seg", 0), ("seg", 1)]]
        P.dma("act", hbuf[0], ssmT_in[l, 0], writes=hkey[0])
        for b in range(NS):
            hcur, hk = hbuf[b % 2], hkey[b % 2]
            if b + 1 < NS:
                P.dma("act", hbuf[(b + 1) % 2], ssmT_in[l, b + 1], writes=hkey[(b + 1) % 2])
            P.op("act", "activation", reads=[hk[0]], writes=["hTb"], out=hTb[:], in_=hcur, func=AF.Copy)
            kws = []
            for j in range(8):
                for hh in range(2):
                    h = 2 * j + hh
                    outp = ps[0][64 * hh:64 * hh + 64, j * 64 + b * LS:j * 64 + (b + 1) * LS]
                    kws.append(("matmul", dict(out=outp, lhsT=hTb[:, h * 64:(h + 1) * 64], rhs=fsr[:, 0, h, b * LS:(b + 1) * LS], start=(b == 0 and j == 0), stop=False, skip_group_check=True)))
            P.mmgroup(kws, reads=["hTb", "fsr"], writes=[("ps", 0)])
            mb = b % 2
            P.op("dve", "tensor_scalar", reads=[("Btok", 0), "cst"], writes=[("Bmk", mb)], out=Bmk[0:Lc, mb, :], in0=Btok[0:Lc, 0, :], scalar1=SEQM[0:Lc, b:b + 1], scalar2=None, op0=ALU.mult)
            pb0 = 4 + 2 * mb
            for g in range(4):
                P.mmgroup([MM(ps[pb0 + g // 2][:, (g % 2) * 256:(g % 2 + 1) * 256], Bmk[0:Lc, mb, g * 128:(g + 1) * 128], xdt[0:Lc, 1, g * 256:(g + 1) * 256])],
                          reads=[("Bmk", mb), ("xdt", 1)], writes=[("ps", pb0 + g // 2)])
            P.op("dve", "tensor_tensor", reads=[hk[0], "cdrS"], writes=["htmp"], out=htmp[:, :].rearrange("p (h q) -> p h q", q=64), in0=hcur.rearrange("p (h q) -> p h q", q=64),
                 in1=cdrS[:, b, :].unsqueeze(2).broadcast_to([128, 16, 64]), op=ALU.mult)
            for bk in range(2):
                P.op("dve", "tensor_tensor", reads=["htmp", ("ps", pb0 + bk)], writes=["htmp"], out=htmp[:, bk * 512:(bk + 1) * 512], in0=ps[pb0 + bk][:, :], in1=htmp[:, bk * 512:(bk + 1) * 512], op=ALU.add)
            P.dma("sp", ssmT_o[l, 1 + b], htmp[:], reads=["htmp"], writes=[("ssmT_o", l, 1 + b)])
        kws = []
        for j in range(8):
            for hh in range(2):
                h = 2 * j + hh
                outp = ps[0][64 * hh:64 * hh + 64, j * 64:j * 64 + Lc]
                kws.append(("matmul", dict(out=outp, lhsT=xdt[0:Lc, 0, h * 64:(h + 1) * 64], rhs=Dm[0:Lc, 0, h, 0:Lc], start=False, stop=True, skip_group_check=True)))
        P.mmgroup(kws, reads=[("xdt", 0), "Dm"], writes=[("ps", 0)])
        gating(l, 0, Lc)

    def s3_ssd(l, kind, first, last, nseq, L, TB):
        P.dma("pool", wsmall[:, 1, 0:128].rearrange("p (c n) -> p c n", n=16), kcp(WSRC["w_in"][l, :, C_DT:C_DT + 16]), reads=WRK(l), writes=[("wsmall", 1)])
        rot = 0
        for half in range(2):
            for q in range(4):
                zc = half * 4 + q
                if zc % 2 == 0:
                    wz, kz = wload(kcp(WSRC["w_in"][l, :, C_Z + zc * 128:C_Z + (zc + 2) * 128]), KC, 256, WRK(l))
                b_ = rot % 4
                rot += 1
                P.mmgroup([MM(ps[b_][:, 0:TB], wz[:, c, (zc % 2) * 128:(zc % 2 + 1) * 128], hb[:, c, 0:TB], c == 0, c == KC - 1) for c in range(KC)], reads=kz + ["hb"], writes=[("ps", b_)])
                P.op("act", "activation", reads=[("ps", b_)], writes=["siluz"], out=siluz[:, zc, 0:TB], in_=ps[b_][:, 0:TB], func=AF.Silu)
        HL = 3 + L
        if kind == "p":
            if first:
                P.op("dve", "memset", writes=["hbh"], ap=hbh[:], constant=0.0)
            for c2 in range(0, 16, 2):
                wx, kx = wload(kcp(WSRC["w_in"][l, :, C_XBC + c2 * 128:C_XBC + (c2 + 2) * 128]), KC, 256, WRK(l))
                pvs, accs, aks, bks = [], [], [], []
                for q in range(2):
                    cc = c2 + q
                    b_ = rot % 4
                    rot += 1
                    P.mmgroup([MM(ps[b_][:, 0:3], wx[:, c, q * 128:(q + 1) * 128], hbh[:, c, 0:3], c == 0, c == KC - 1) for c in range(KC)]
                              + [MM(ps[b_][:, 3:3 + L], wx[:, c, q * 128:(q + 1) * 128], hb[:, c, 0:L], c == 0, c == KC - 1) for c in range(KC)],
                              reads=kx + ["hb", "hbh"], writes=[("ps", b_)])
                    pvs.append(ps[b_])
                    bks.append(b_)
                    accs.append(cacc[:, q, 0:L])
                    aks.append(("cacc", q))
                for q in range(2):
                    cc = c2 + q
                    P.op("dve", "tensor_scalar", reads=[("ps", bks[q]), "vecs"], writes=[aks[q]], out=accs[q], in0=pvs[q][:, 3:3 + L], scalar1=vcol(l, V_CW + 3 * 16 + cc),
                         scalar2=vcol(l, V_CB + cc), op0=ALU.mult, op1=ALU.add)
                for j in range(3):
                    for q in range(2):
                        cc = c2 + q
                        P.op("dve", "scalar_tensor_tensor", reads=[("ps", bks[q]), aks[q], "vecs"], writes=[aks[q]], out=accs[q], in0=pvs[q][:, j:j + L], scalar=vcol(l, V_CW + j * 16 + cc),
                             in1=accs[q], op0=ALU.mult, op1=ALU.add)
                for q in range(2):
                    cc = c2 + q
                    if last:
                        nhal = cbuf[:, q, 0:3]
                        P.op("act", "activation", reads=[("ps", bks[q])], writes=[("nhalo", q)], out=nhal, in_=pvs[q][:, L:L + 3], func=AF.Copy)
                        P.dma("sp", conv_o[l, cc * 128:(cc + 1) * 128, 0, :], nhal, reads=[("nhalo", q)], writes=[("conv_o", l, cc, 0)])
                    if cc < 8:
                        dst, dk = xsT[:, cc, 0:TB], "xsT"
                    elif cc < 12:
                        dst, dk = BT[:, cc - 8, 0:TB], "BT"
                    else:
                        dst, dk = CT[:, cc - 12, 0:TB], "CT"
                    P.op("act", "activation", reads=[aks[q]], writes=[dk], out=dst, in_=accs[q], func=AF.Silu)
            P.op("dve", "tensor_copy", reads=["hb"], writes=["hbh"], out=hbh[:, :, 0:3], in_=hb[:, :, L - 3:L])
        else:
            for cc in range(16):
                if cc % 2 == 0:
                    wx, kx = wload(kcp(WSRC["w_in"][l, :, C_XBC + cc * 128:C_XBC + (cc + 2) * 128]), KC, 256, WRK(l))
                b_ = rot % 4
                rot += 1
                par = cc % 2
                hk = ("halo", par)
                P.mmgroup([MM(ps[b_][:, 0:TB], wx[:, c, (cc % 2) * 128:(cc % 2 + 1) * 128], hb[:, c, 0:TB], c == 0, c == KC - 1) for c in range(KC)], reads=kx + ["hb"], writes=[("ps", b_)])
                pv3 = v3(ps[b_][:, 0:TB], nseq, L)
                hal = v3(cbuf[:, par, 0:nseq * 3], nseq, 3)
                if kind == "p":
                    if first:
                        P.op("dve", "memset", writes=[hk], ap=hal, constant=0.0)
                    else:
                        P.op("dve", "tensor_copy", reads=[("halp", cc)], writes=[hk], out=hal[:, 0, :], in_=halp[:, cc, :])
                else:
                    P.dma("sp", hal, conv_in[l, cc * 128:(cc + 1) * 128, :, :], writes=[hk])
                acc = v3(cacc[:, par, 0:TB], nseq, L)
                ak = ("cacc", par)
                P.op("dve", "tensor_scalar", reads=[("ps", b_), "vecs"], writes=[ak], out=acc, in0=pv3, scalar1=vcol(l, V_CW + 3 * 16 + cc), scalar2=vcol(l, V_CB + cc), op0=ALU.mult, op1=ALU.add)
                for j in range(3):
                    sh = 3 - j
                    if L > sh:
                        P.op("dve", "scalar_tensor_tensor", reads=[("ps", b_), ak, "vecs"], writes=[ak], out=acc[:, :, sh:L], in0=pv3[:, :, 0:L - sh], scalar=vcol(l, V_CW + j * 16 + cc),
                             in1=acc[:, :, sh:L], op0=ALU.mult, op1=ALU.add)
                    n_h = min(sh, L)
                    P.op("dve", "scalar_tensor_tensor", reads=[hk, ak, "vecs"], writes=[ak], out=acc[:, :, 0:n_h], in0=hal[:, :, j:j + n_h], scalar=vcol(l, V_CW + j * 16 + cc),
                         in1=acc[:, :, 0:n_h], op0=ALU.mult, op1=ALU.add)
                nhal = v3(cbuf[:, par, nseq * 3:nseq * 6], nseq, 3)
                nk_ = ("nhalo", par)
                if L >= 3:
                    P.op("act", "activation", reads=[("ps", b_)], writes=[nk_], out=nhal, in_=pv3[:, :, L - 3:L], func=AF.Copy)
                else:
                    P.op("act", "activation", reads=[hk], writes=[nk_], out=nhal[:, :, 0:3 - L], in_=hal[:, :, L:3], func=AF.Copy)
                    P.op("act", "activation", reads=[("ps", b_), nk_], writes=[nk_], out=nhal[:, :, 3 - L:3], in_=pv3, func=AF.Copy)
                if kind == "p":
                    P.op("dve", "tensor_copy", reads=[nk_], writes=[("halp", cc)], out=halp[:, cc, :], in_=nhal[:, 0, :])
                    if last:
                        P.dma("sp", conv_o[l, cc * 128:(cc + 1) * 128, 0, :], nhal[:, 0, :], reads=[nk_], writes=[("conv_o", l, cc, 0)])
                else:
                    P.dma("sp", conv_o[l, cc * 128:(cc + 1) * 128, 1:NS + 1, :], nhal, reads=[nk_], writes=[("conv_o", l, cc, 1)])
                if cc < 8:
                    dst, dk = xsT[:, cc, 0:TB], "xsT"
                elif cc < 12:
                    dst, dk = BT[:, cc - 8, 0:TB], "BT"
                else:
                    dst, dk = CT[:, cc - 12, 0:TB], "CT"
                P.op("act", "activation", reads=[ak], writes=[dk], out=dst, in_=cacc[:, par, 0:TB], func=AF.Silu)

        if kind == "p":
            nch = TB // 128
            for ci in range(nch):
                chunk_A(l, ci * 128, 128, ci % 2, TRI, ONESF, False)
            for ci in range(nch):
                chunk_B_prompt(l, ci * 128, 128, ci % 2, "zero" if (first and ci == 0) else None, 0 if (last and ci == nch - 1) else None)
        else:
            chunk_A(l, 0, NS * LS, 0, TRIBD, SAMEBD, True)
            chunk_B_sample(l)

    def s4_merge(l, t0, TB):
        brs = [(ypool, "ypool", 4, "w_br_pool"), (yssm, "yssm", 8, "w_br_ssm"), (ymem, "ymem", 4, "w_br_mem")]
        wcache = {}
        for dc in range(KC):
            par = dc % 2
            for br, (yb, ykey, nk, wname) in enumerate(brs):
                if dc % 2 == 0:
                    wcache[("g", br)] = wload(kcp(WSRC["w_in"][l, :, C_G + br * 1024 + dc * 128:C_G + br * 1024 + (dc + 2) * 128]), KC, 256, WRK(l))
                    wcache[("b", br)] = wload(kcp(WSRC[wname][l, :, dc * 128:(dc + 2) * 128]), nk, 256, WRK(l))
                wg, kg = wcache[("g", br)]
                wb, kb = wcache[("b", br)]
                cs = slice((dc % 2) * 128, (dc % 2 + 1) * 128)
                P.mmgroup([MM(ps[br][:, 0:TB], wg[:, c, cs], hb[:, c, 0:TB], c == 0, c == KC - 1) for c in range(KC)], reads=kg + ["hb"], writes=[("ps", br)])
                P.op("act", "activation", reads=[("ps", br), "vecs"], writes=[("gsb", par, br)], out=gsb[:, par, br, 0:TB], in_=ps[br][:, 0:TB], func=AF.Sigmoid,
                     bias=vcol(l, V_GB + br * 8 + dc))
                P.mmgroup([MM(ps[3 + br][:, 0:TB], wb[:, k, cs], yb[:, k, 0:TB], k == 0, k == nk - 1) for k in range(nk)], reads=kb + [ykey], writes=[("ps", 3 + br)])
            for br in range(3):
                P.op("dve", "tensor_tensor", reads=[("ps", 3 + br), ("gsb", par, br)], writes=[("mt", par, br)], out=mt[:, par, br, 0:TB], in0=ps[3 + br][:, 0:TB],
                     in1=gsb[:, par, br, 0:TB], op=ALU.mult)
            P.op("dve", "tensor_tensor", reads=[("mt", par, 0), ("mt", par, 1)], writes=[("mt", par, 0)], out=mt[:, par, 0, 0:TB], in0=mt[:, par, 0, 0:TB], in1=mt[:, par, 1, 0:TB], op=ALU.add)
            P.op("dve", "tensor_tensor", reads=[("mt", par, 0), ("mt", par, 2)], writes=[("mrgb", dc)], out=mrgb[:, dc, 0:TB], in0=mt[:, par, 0, 0:TB], in1=mt[:, par, 2, 0:TB], op=ALU.add)
        for dc in range(KC):
            if dc % 2 == 0:
                wo, ko = wload(kcp(WSRC["w_o"][l, :, dc * 128:(dc + 2) * 128]), KC, 256, WRK(l))
            po = 6 + dc % 2
            cs = slice((dc % 2) * 128, (dc % 2 + 1) * 128)
            P.mmgroup([MM(ps[po][:, 0:TB], wo[:, k, cs], mrgb[:, k, 0:TB], k == 0, k == KC - 1) for k in range(KC)], reads=ko + ["mrgb"], writes=[("ps", po)])
            P.op("dve", "tensor_tensor", reads=[("ps", po), ("x", "m", dc)], writes=[("x", "m", dc)], out=x[:, dc, t0:t0 + TB], in0=ps[po][:, 0:TB], in1=x[:, dc, t0:t0 + TB], op=ALU.add)

    def mixer(l):
        ON = lambda n: ("mix" in PHASES) or (n in PHASES)
        P.mark(f"L{l} mem")
        if ON("mem"):
            memory_stage(l)
        blocks = [("p", t, MB) for t in range(0, LP, MB)] + [("s", LP, NS * LS)]
        for bi, (kind, t0, TB) in enumerate(blocks):
            nseq, L = (1, TB) if kind == "p" else (NS, LS)
            first = (bi == 0)
            last = (kind == "p" and t0 + TB == LP)
            if kind == "s" or bi == 0:
                P.barrier(["pe", "act", "dve", "sp"] + (["pool"] if kind == "s" else []))
            rmsnorm([hb[:, c, 0:TB] for c in range(KC)], lambda c: "hb", [x[:, c, t0:t0 + TB] for c in range(KC)], lambda c: ("x", "m", c), TB, vcol(l, V_MIX, 8), sqM)
            P.mark(f"L{l} b{bi} s1")
            if ON("s1"):
                s1_pool(l, kind, first, last, nseq, L, TB)
            P.mark(f"L{l} b{bi} s2")
            if ON("s2"):
                s2_attn(l, kind, nseq, L, TB)
            P.barrier(["pe", "act", "dve", "sp", "pool"] if kind == "s" else ["pe", "act", "dve", "sp"])
            P.mark(f"L{l} b{bi} s3")
            if ON("s3"):
                s3_ssd(l, kind, first, last, nseq, L, TB)
            P.barrier(["pe", "act", "dve", "sp"])
            P.mark(f"L{l} b{bi} s4")
            if ON("s4"):
                s4_merge(l, t0, TB)

    for l in range(DEPTH):
        P.mark(f"L{l} ffn1")
        if "ffn1" in PHASES:
            ffn(l, W["ffn1_w_gate"], W["ffn1_w_up"], W["ffn1_w_down"], vcol(l, V_F1, 8), precast_pieces(l) if USE_PRECAST else ())
        P.barrier(["pe", "act", "dve", "sp"])
        if any(p_ in PHASES for p_ in ("mix", "mem", "s1", "s2", "s3", "s4")):
            mixer(l)
        P.barrier(["pe", "act", "dve", "sp"])
        P.mark(f"L{l} ffn2")
        if "ffn2" in PHASES:
            ffn(l, W["ffn2_w_gate"], W["ffn2_w_up"], W["ffn2_w_down"], vcol(l, V_F2, 8))
        P.barrier(["pe", "act", "dve", "sp"])

    P.force = True
    P.mark('final')
    for (t0, tb) in fblocks:
        for c in range(KC):
            P.op("act", "activation", reads=[("x", t0, c)], writes=[("sq", c)], out=sqF[:, c, 0:tb], in_=x[:, c, t0:t0 + tb], func=AF.Square)
        P.mmgroup([MM(ps[7][:, 0:tb], onesb[:], sqF[:, c, 0:tb], c == 0, c == KC - 1) for c in range(KC)], reads=["sq", "onesb"], writes=[("ps", 7)])
        P.op("act", "activation", reads=[("ps", 7), "epsc"], writes=["rstd"], out=rstd[:, 0:tb], in_=ps[7][:, 0:tb], func=AF.Sqrt, bias=EPSC[:], scale=1.0 / D)
        P.op("dve", "reciprocal", reads=["rstd"], writes=["rstd"], out=rstd[:, 0:tb], in_=rstd[:, 0:tb])
        for c in range(KC):
            k = c % 2
            P.op("dve", "scalar_tensor_tensor", reads=[("x", t0, c), "rstd", "vecs"], writes=[("yout", k)], out=youtA[:, k, 0:tb], in0=x[:, c, t0:t0 + tb],
                 scalar=vecs[:, V_FIN + c:V_FIN + c + 1], in1=rstd[:, 0:tb], op0=ALU.mult, op1=ALU.mult)
            P.dma("sp", yT_o[c * 128:(c + 1) * 128, t0:t0 + tb], youtA[:, k, 0:tb], reads=[("yout", k)], writes=[("yT_o", t0, c)])
    P.barrier(["sp"])
    with nc.Block() as block:
        P.emit(block)
    es.close()
    return nc, P


def _consts():
    c = np.zeros((128, 720), np.float32)
    c[:, 0:128] = np.triu(np.ones((128, 128), np.float32))
    c[:, 128:256] = np.eye(128, dtype=np.float32)
    for g in range(4):
        w = 2 ** (g + 1)
        for t in range(16):
            c[:, 256 + g * 16 + t] = w / min(t + 1, w)
    same = np.kron(np.eye(16, dtype=np.float32), np.ones((4, 4), np.float32))
    c[0:64, 320:384] = same * np.triu(np.ones((64, 64), np.float32))
    c[0:64, 384:448] = same
    c[:, 448:576] = 1.0
    c[0:64, 576:592] = np.kron(np.eye(16, dtype=np.float32), np.ones((4, 1), np.float32))
    return c


def _cols(v):
    return np.ascontiguousarray(np.asarray(v, np.float32).reshape(-1, 128).T)


def _pack_vecs(inp, depth):
    vecs = np.zeros((128, NVL * depth + 8), np.float32)
    rows = np.zeros((128, depth * 32), np.float32)
    for l in range(depth):
        o = l * NVL
        vecs[:, o + 0:o + 8] = _cols(inp["ffn1_norm"][l])
        vecs[:, o + 8:o + 16] = _cols(inp["mix_norm"][l])
        vecs[:, o + 16:o + 24] = _cols(inp["ffn2_norm"][l])
        vecs[:, o + 24:o + 48] = _cols(inp["gate_bias"][l])
        vecs[:, o + 48:o + 52] = _cols(inp["pool_scale"][l])
        for j in range(4):
            vecs[:, o + 52 + j * 16:o + 52 + (j + 1) * 16] = _cols(inp["conv_w"][l, j])
        vecs[:, o + 116:o + 132] = _cols(inp["conv_b"][l])
        vecs[:, o + 132:o + 140] = _cols(np.repeat(np.asarray(inp["d_skip"][l]), 64))
        vecs[:, o + 140:o + 148] = _cols(inp["ssm_norm"][l])
        vecs[:, o + 148:o + 156] = _cols(inp["mem_norm"][l])
        rows[:, l * 32:l * 32 + 16] = np.asarray(inp["dt_bias"][l])[None, :]
        rows[:, l * 32 + 16:l * 32 + 32] = np.asarray(inp["a_log"][l])[None, :]
    vecs[:, NVL * depth:NVL * depth + 8] = _cols(inp["final_norm"])
    return vecs, rows


_WNAMES = ("ffn1_w_gate", "ffn1_w_up", "ffn1_w_down", "ffn2_w_gate", "ffn2_w_up", "ffn2_w_down", "w_in", "pool_w",
           "w_mem_k", "w_mem_v", "w_br_pool", "w_br_ssm", "w_br_mem", "w_o")


def run_cores(inp, n_cores, LP, depth, phases=("ffn1", "mix", "ffn2"), runner=None):
    f32 = lambda a: np.ascontiguousarray(np.asarray(a, np.float32))
    nc, _ = build_program(LP, depth, PHASES=phases)
    vecs, rows = _pack_vecs(inp, depth)
    cst = _consts()
    shared = {nm: f32(inp[nm]) for nm in _WNAMES}
    in_maps = []
    for i in range(n_cores):
        sl = slice(i * NS, (i + 1) * NS)
        xs = f32(inp["x_sample"][sl]).reshape(NS * LS, D)
        m = dict(shared)
        m["xT"] = f32(np.concatenate([f32(inp["x_prompt"][i]), xs], axis=0).T)
        m["memT"] = f32(f32(inp["mem_prompt"][i]).T)
        m["pool_in"] = f32(np.transpose(f32(inp["state_pool"][:, sl]), (0, 3, 1, 2)))
        m["conv_in"] = f32(np.transpose(f32(inp["state_conv"][:, sl]), (0, 3, 1, 2)))
        m["ssmT_in"] = f32(np.transpose(f32(inp["state_ssm"][:, sl]).reshape(depth, NS, 1024, 128), (0, 1, 3, 2)))
        m["kT_in"] = f32(np.transpose(f32(inp["cache_mem_k"][:, sl]), (0, 1, 3, 4, 2)))
        m["v_in"] = f32(f32(inp["cache_mem_v"][:, sl]).reshape(depth, NS, 256, 512))
        m["vecs"] = vecs
        m["rows"] = rows
        m["consts"] = cst
        in_maps.append(m)
    if runner is None:
        res = bass_utils.run_bass_kernel_spmd(nc, in_maps, core_ids=list(range(n_cores))).results
    else:
        res = runner(nc, in_maps)
    B = n_cores
    y_p = np.stack([res[i]["yT"][:, :LP].T for i in range(B)])
    y_s = np.concatenate([res[i]["yT"][:, LP:].T.reshape(NS, LS, D) for i in range(B)])
    po = np.stack([res[i]["pool_o"] for i in range(B)])
    co = np.stack([res[i]["conv_o"] for i in range(B)])
    so = np.stack([res[i]["ssmT_o"] for i in range(B)])
    pool_p = np.transpose(po[:, :, :, 0, :], (1, 0, 3, 2))
    conv_p = np.transpose(co[:, :, :, 0, :], (1, 0, 3, 2))
    ssm_p = np.transpose(so[:, :, 0], (1, 0, 3, 2)).reshape(depth, B, 16, 64, 128)
    mk = np.stack([np.transpose(res[i]["mkT_o"], (0, 2, 1)).reshape(depth, 256, 4, 128) for i in range(B)], axis=1)
    mv = np.stack([res[i]["mv_o"].reshape(depth, 256, 4, 128) for i in range(B)], axis=1)
    pool_s = np.transpose(po[:, :, :, 1:, :], (1, 0, 3, 4, 2)).reshape(depth, B * NS, 15, 512)
    conv_s = np.transpose(co[:, :, :, 1:, :], (1, 0, 3, 4, 2)).reshape(depth, B * NS, 3, 2048)
    ssm_s = np.transpose(so[:, :, 1:], (1, 0, 2, 4, 3)).reshape(depth, B * NS, 16, 64, 128)
    outs = (y_p, y_s, pool_p, conv_p, ssm_p, mk, mv, pool_s, conv_s, ssm_s)
    return tuple(np.ascontiguousarray(o, dtype=np.float32) for o in outs)


def kernel(**inputs):
    return run_cores(inputs, 8, 2048, 4)
```
